# Optimizing a Trainium2 kernel written in Bass

```python
import jax, jax.numpy as jnp
from jax import lax
import numpy as np

D_MODEL = 2048
BATCH = 8
SEQ = 4096
DEPTH = 2
DEC_BATCH = 8
DEC_SEQ = 16
PAST_LEN = 4096

CHUNK = 64
Q_BLOCK = 128
N_MIXERS = 2
N_SB_LAYERS = (DEPTH + 1) // 2
N_GLA_LAYERS = DEPTH // 2
SB_HEADS = 16
SB_HEAD_DIM = D_MODEL // SB_HEADS
SB_WIDTH = SB_HEADS * SB_HEAD_DIM
GLA_HEADS = 4
GLA_DK = (D_MODEL // 2) // GLA_HEADS
GLA_DV = D_MODEL // GLA_HEADS
GLA_GATE_RANK = 16
GLA_TAU = 16.0
GLA_IN = 2 * GLA_HEADS * GLA_DK + 2 * GLA_HEADS * GLA_DV + GLA_GATE_RANK
EPS = 1e-6

kernel_name = "sb_gla_hybrid_stream_step"


def rmsnorm(x, g):
    xf = x.astype(jnp.float32)
    y = xf * lax.rsqrt(jnp.mean(xf * xf, axis=-1, keepdims=True) + EPS)
    return (y * g.astype(jnp.float32)).astype(x.dtype)


def sb_block(q, qpos, k, v, kpos):
    z = jnp.einsum('bqhd,bkhd->bhqk', q, k).astype(jnp.float32) * (SB_HEAD_DIM ** -0.5)
    mask = kpos[None, :] < qpos[:, None]
    neg_log_keep = jnp.where(mask, jax.nn.softplus(z), 0.0)
    between = lax.cumsum(neg_log_keep, axis=3, reverse=True) - neg_log_keep
    a = jnp.where(mask, jnp.exp(jax.nn.log_sigmoid(z) - between), 0.0)
    return jnp.einsum('bhqk,bkhd->bqhd', a, v.astype(jnp.float32))


def sb_mixer(h, w_in, w_out, past_k=None, past_v=None):
    B, T, _ = h.shape
    q, k, v, zg = jnp.split(h @ w_in, 4, axis=-1)
    shp = (B, T, SB_HEADS, SB_HEAD_DIM)
    q, k, v = q.reshape(shp), k.reshape(shp), v.reshape(shp)
    if past_k is None:
        nb = T // Q_BLOCK
        pos = jnp.arange(T, dtype=jnp.int32)
        qb = q.reshape(B, nb, Q_BLOCK, SB_HEADS, SB_HEAD_DIM).transpose(1, 0, 2, 3, 4)
        o = lax.map(lambda a: sb_block(a[0], a[1], k, v, pos), (qb, pos.reshape(nb, Q_BLOCK)))
        o = o.transpose(1, 0, 2, 3, 4).reshape(B, T, SB_WIDTH)
    else:
        P = past_k.shape[1]
        keys = jnp.concatenate([past_k, k], axis=1)
        vals = jnp.concatenate([past_v, v], axis=1)
        kpos = jnp.arange(P + T, dtype=jnp.int32)
        qpos = P + jnp.arange(T, dtype=jnp.int32)
        o = sb_block(q, qpos, keys, vals, kpos).reshape(B, T, SB_WIDTH)
    y = (o * jax.nn.silu(zg.astype(jnp.float32))).astype(h.dtype) @ w_out
    return y, k, v


def gla_chunk(S, inp):
    q, k, v, g = inp
    C = q.shape[1]
    b = jnp.cumsum(g, axis=1)
    b_last = b[:, -1]
    q_in = q * jnp.exp(b)
    a = jnp.einsum('bchd,bshd->bhcs', q_in, k * jnp.exp(-b))
    a = jnp.where(jnp.tril(jnp.ones((C, C), dtype=bool)), a, 0.0)
    o = jnp.einsum('bhcs,bshe->bche', a, v) + jnp.einsum('bchd,bhde->bche', q_in, S)
    S = jnp.exp(b_last)[..., None] * S + jnp.einsum('bshd,bshe->bhde', k * jnp.exp(b_last[:, None] - b), v)
    return S, o


def gla_mixer(h, w_in, w_a2, b_a, g_out, w_out, S0=None):
    B, T, _ = h.shape
    f32 = jnp.float32
    nk = GLA_HEADS * GLA_DK
    nv = GLA_HEADS * GLA_DV
    q, k, v, r, a_lr = jnp.split(h @ w_in, [nk, 2 * nk, 2 * nk + nv, 2 * nk + 2 * nv], axis=-1)
    q = q.astype(f32).reshape(B, T, GLA_HEADS, GLA_DK) * (GLA_DK ** -0.5)
    k = k.astype(f32).reshape(B, T, GLA_HEADS, GLA_DK)
    v = v.astype(f32).reshape(B, T, GLA_HEADS, GLA_DV)
    g = (jax.nn.log_sigmoid((a_lr @ w_a2 + b_a).astype(f32)) / GLA_TAU).reshape(B, T, GLA_HEADS, GLA_DK)
    S0 = jnp.zeros((B, GLA_HEADS, GLA_DK, GLA_DV), f32) if S0 is None else S0.astype(f32)
    C = min(CHUNK, T)
    nc = T // C
    to_chunks = lambda t: t.reshape(B, nc, C, *t.shape[2:]).swapaxes(0, 1)
    S, o = lax.scan(gla_chunk, S0, (to_chunks(q), to_chunks(k), to_chunks(v), to_chunks(g)))
    o = o.swapaxes(0, 1).reshape(B, T, GLA_HEADS, GLA_DV)
    o = o * lax.rsqrt(jnp.mean(o * o, axis=-1, keepdims=True) + EPS) * g_out.astype(f32).reshape(GLA_HEADS, GLA_DV)
    y = (o.reshape(B, T, nv) * jax.nn.silu(r.astype(f32))).astype(h.dtype) @ w_out
    return y, S.astype(h.dtype)


def trunk(x, c, w_ada, b_ada, norm_g, sb_w_in, sb_w_out, gla_w_in, gla_w_a2, gla_b_a,
          gla_norm_g, gla_w_out, final_norm_g, cache_sb_k=None, cache_sb_v=None, state_gla=None):
    new_k, new_v, new_s = [], [], []
    for i in range(DEPTH):
        j = i // N_MIXERS
        shift, scale, gate = jnp.split(c @ w_ada[i] + b_ada[i], 3, axis=-1)
        h = rmsnorm(x, norm_g[i]) * (1 + scale[:, None]) + shift[:, None]
        if i % N_MIXERS == 0:
            if cache_sb_k is None:
                y, k, v = sb_mixer(h, sb_w_in[j], sb_w_out[j])
            else:
                y, k, v = sb_mixer(h, sb_w_in[j], sb_w_out[j], cache_sb_k[j], cache_sb_v[j])
            new_k.append(k)
            new_v.append(v)
        else:
            S0 = None if state_gla is None else state_gla[j]
            y, S = gla_mixer(h, gla_w_in[j], gla_w_a2[j], gla_b_a[j], gla_norm_g[j], gla_w_out[j], S0)
            new_s.append(S)
        x = x + gate[:, None] * y
    return rmsnorm(x, final_norm_g), jnp.stack(new_k), jnp.stack(new_v), jnp.stack(new_s)


def setup_inputs(seed: int = 0) -> dict:
    key = jax.random.key(seed)
    ks = jax.random.split(key, 18)
    nrm = lambda k, shape, s: jax.random.normal(k, shape, jnp.float32) * s
    D = D_MODEL
    return {
        "x_prompt": nrm(ks[0], (BATCH, SEQ, D), 1.0),
        "x_sample": nrm(ks[1], (DEC_BATCH, DEC_SEQ, D), 1.0),
        "cache_sb_k": nrm(ks[2], (N_SB_LAYERS, DEC_BATCH, PAST_LEN, SB_HEADS, SB_HEAD_DIM), 1.0),
        "cache_sb_v": nrm(ks[3], (N_SB_LAYERS, DEC_BATCH, PAST_LEN, SB_HEADS, SB_HEAD_DIM), 1.0),
        "state_gla": nrm(ks[4], (N_GLA_LAYERS, DEC_BATCH, GLA_HEADS, GLA_DK, GLA_DV), 1.0),
        "c_prompt": nrm(ks[5], (BATCH, D), 1.0),
        "c_sample": nrm(ks[6], (DEC_BATCH, D), 1.0),
        "w_ada": nrm(ks[7], (DEPTH, D, 3 * D), 0.5 * D ** -0.5),
        "b_ada": nrm(ks[8], (DEPTH, 3 * D), 0.01),
        "norm_g": 1.0 + nrm(ks[9], (DEPTH, D), 0.01),
        "sb_w_in": nrm(ks[10], (N_SB_LAYERS, D, 4 * SB_WIDTH), D ** -0.5),
        "sb_w_out": nrm(ks[11], (N_SB_LAYERS, SB_WIDTH, D), SB_WIDTH ** -0.5),
        "gla_w_in": nrm(ks[12], (N_GLA_LAYERS, D, GLA_IN), D ** -0.5),
        "gla_w_a2": nrm(ks[13], (N_GLA_LAYERS, GLA_GATE_RANK, GLA_HEADS * GLA_DK), GLA_GATE_RANK ** -0.5),
        "gla_b_a": nrm(ks[14], (N_GLA_LAYERS, GLA_HEADS * GLA_DK), 0.1),
        "gla_norm_g": 1.0 + nrm(ks[15], (N_GLA_LAYERS, GLA_HEADS * GLA_DV), 0.01),
        "gla_w_out": nrm(ks[16], (N_GLA_LAYERS, GLA_HEADS * GLA_DV, D), (GLA_HEADS * GLA_DV) ** -0.5),
        "final_norm_g": 1.0 + nrm(ks[17], (D,), 0.01),
    }


def reference(x_prompt, x_sample, cache_sb_k, cache_sb_v, state_gla, c_prompt, c_sample,
              w_ada, b_ada, norm_g, sb_w_in, sb_w_out, gla_w_in, gla_w_a2, gla_b_a,
              gla_norm_g, gla_w_out, final_norm_g):
    weights = (w_ada, b_ada, norm_g, sb_w_in, sb_w_out, gla_w_in, gla_w_a2, gla_b_a,
               gla_norm_g, gla_w_out, final_norm_g)
    y_prompt, k_prompt, v_prompt, s_prompt = trunk(x_prompt, c_prompt, *weights)
    y_sample, k_sample, v_sample, s_sample = trunk(x_sample, c_sample, *weights,
                                                   cache_sb_k, cache_sb_v, state_gla)
    return (y_prompt, y_sample, k_prompt, v_prompt, k_sample, v_sample, s_prompt, s_sample)
```

```python
import numpy as np
import ml_dtypes
from contextlib import ExitStack
import concourse.bass as bass
import concourse.mybir as mybir
from concourse.bass_utils import run_bass_kernel_spmd

F32 = mybir.dt.float32
BF16 = mybir.dt.bfloat16
AF = mybir.ActivationFunctionType
ALU = mybir.AluOpType

D = 2048
T = 4096
TS = 16
NT = T + TS
H = 16
DH = 128
EPS = 1e-6
NCF = 640
NCB = 384
STAGE = 9
NBLK = 8
ENG = ["pe", "act", "dve", "pool", "sp"]
SEM_CAP = 30000


class Res:
    __slots__ = ("name", "w", "rc", "rd", "psum")

    def __init__(self, name=""):
        self.name = name
        self.psum = False
        self.w = None
        self.rc = {}
        self.rd = []


class Op:
    __slots__ = ("eng", "fn", "deps", "dma", "key", "sem", "val", "sig")


class Prog:
    def __init__(self):
        self.ops = []
        self.bar = {e: set() for e in ENG}
        self.last = {}
        self.dmas = []

    def op(self, eng, fn, reads=(), writes=(), key=None, nobar=False):
        o = Op()
        o.eng = eng
        o.fn = fn
        o.dma = key is not None
        o.key = key
        o.sig = False
        o.sem = None
        o.val = 0
        deps = set()
        for r in reads:
            if r.w is not None:
                deps.add(r.w)
            if r.psum:
                deps.update(v for k, v in r.rc.items() if k != eng)
        for w in writes:
            if w.w is not None:
                deps.add(w.w)
            deps.update(w.rc.values())
            deps.update(w.rd)
        deps |= self.bar[eng]
        self.bar[eng] = set()
        o.deps = [d for d in deps if d is not o and (d.dma or o.dma or d.eng != eng or eng != "pe")]
        for d in o.deps:
            d.sig = True
        for r in reads:
            if o.dma:
                r.rd.append(o)
            else:
                r.rc[eng] = o
        for w in writes:
            w.w = o
            w.rc = {}
            w.rd = []
        self.ops.append(o)
        self.last[eng] = o
        if o.dma and not nobar:
            self.dmas.append(o)
        return o

    def barrier(self):
        s = set(self.last.values()) | set(self.dmas)
        for e in ENG:
            self.bar[e] |= s
        self.dmas = []

    def emit(self, nc, es):
        keys = []
        for o in self.ops:
            if o.dma and o.key not in keys:
                keys.append(o.key)
        dsem = {k: es.enter_context(nc.semaphore("d_" + k)) for k in keys}
        cnt = {e: 0 for e in ENG}
        csem = {e: es.enter_context(nc.semaphore("c_" + e + "0")) for e in ENG}
        gen = {e: 0 for e in ENG}
        dcnt = {}
        for o in self.ops:
            if o.dma:
                v = dcnt.get(o.key, 0) + 16
                dcnt[o.key] = v
                o.sem = dsem[o.key]
                o.val = v
            elif o.sig:
                if cnt[o.eng] >= SEM_CAP:
                    gen[o.eng] += 1
                    csem[o.eng] = es.enter_context(nc.semaphore("c_%s%d" % (o.eng, gen[o.eng])))
                    cnt[o.eng] = 0
                cnt[o.eng] += 1
                o.sem = csem[o.eng]
                o.val = cnt[o.eng]
        ops = self.ops

        def run(engname, eng):
            waited = {}
            for o in ops:
                if o.eng != engname:
                    continue
                need = {}
                for d in o.deps:
                    k = id(d.sem)
                    if k not in need or need[k][1] < d.val:
                        need[k] = (d.sem, d.val)
                for k, (s, v) in need.items():
                    if waited.get(k, 0) < v:
                        eng.wait_ge(s, v)
                        waited[k] = v
                ins = o.fn(eng)
                if o.dma:
                    ins.then_inc(o.sem, 16)
                elif o.sig:
                    ins.then_inc(o.sem, 1)

        with nc.Block() as block:
            @block.tensor
            def _(e):
                run("pe", e)

            @block.scalar
            def _(e):
                run("act", e)

            @block.vector
            def _(e):
                run("dve", e)

            @block.gpsimd
            def _(e):
                run("pool", e)

            @block.sync
            def _(e):
                run("sp", e)


class Buf:
    __slots__ = ("t", "r")

    def __init__(self, t, name):
        self.t = t
        self.r = Res(name)


class KB:
    def __init__(self, nc):
        self.nc = nc
        self.P = Prog()
        self.uid = 0
        self.pool_dmas = []

    def sb(self, es, shape, dt, name=None):
        self.uid += 1
        nm = "%s_%d" % (name or "sb", self.uid)
        return Buf(es.enter_context(self.nc.sbuf_tensor(nm, list(shape), dt)), nm)

    def ps(self, es, shape, dt, name=None):
        self.uid += 1
        nm = "%s_%d" % (name or "ps", self.uid)
        b = Buf(es.enter_context(self.nc.psum_tensor(nm, list(shape), dt)), nm)
        b.r.psum = True
        return b

    def ring(self, es, n, shape, dt, name=None, psum=False):
        return [(self.ps if psum else self.sb)(es, shape, dt, name) for _ in range(n)]

    def mm(self, out, lhsT, rhs, start, stop, R, W):
        self.P.op("pe", lambda e: e.matmul(out, lhsT, rhs, start=start, stop=stop), R, W)

    def tr(self, out, in_, ident, R, W):
        self.P.op("pe", lambda e: e.transpose(out, in_, ident), R, W)

    def act(self, out, in_, func, R, W, bias=None, scale=None):
        kw = {}
        if bias is not None:
            kw["bias"] = bias
        if scale is not None:
            kw["scale"] = scale
        self.P.op("act", lambda e: e.activation(out=out, in_=in_, func=func, **kw), R, W)

    def tt(self, eng, out, in0, in1, op, R, W):
        self.P.op(eng, lambda e: e.tensor_tensor(out=out, in0=in0, in1=in1, op=op), R, W)

    def ts(self, eng, out, in0, s1, s2, op0, op1, R, W):
        if op1 is None:
            self.P.op(eng, lambda e: e.tensor_scalar(out=out, in0=in0, scalar1=s1, scalar2=None, op0=op0), R, W)
        else:
            self.P.op(eng, lambda e: e.tensor_scalar(out=out, in0=in0, scalar1=s1, scalar2=s2, op0=op0, op1=op1), R, W)

    def stt(self, out, in0, scalar, in1, op0, op1, R, W):
        self.P.op("dve", lambda e: e.scalar_tensor_tensor(out=out, in0=in0, scalar=scalar, in1=in1, op0=op0, op1=op1), R, W)

    def ttr(self, out, in0, in1, accum, R, W):
        self.P.op("act", lambda e: e.activation(out=out, in_=in0, func=AF.Square, accum_out=accum), R, W)

    def cp(self, eng, out, in_, R, W):
        if eng == "act":
            self.P.op("act", lambda e: e.activation(out=out, in_=in_, func=AF.Copy), R, W)
        else:
            self.P.op(eng, lambda e: e.tensor_copy(out=out, in_=in_), R, W)

    def memset(self, eng, ap, val, W):
        self.P.op(eng, lambda e: e.memset(ap, val), (), W)

    def recip(self, out, in_, R, W):
        self.P.op("dve", lambda e: e.reciprocal(out=out, in_=in_), R, W)

    def dma(self, eng, out, in_, R, W, key, nobar=False, **kw):
        if eng == "pool":
            key = "pl%d" % (len(self.pool_dmas) % 5)
        o = self.P.op(eng, lambda e: e.dma_start(out=out, in_=in_, **kw), R, W, key=key, nobar=nobar)
        if eng == "pool":
            self.pool_dmas.append(o)
            if len(self.pool_dmas) > 4:
                d = self.pool_dmas[-5]
                if d not in o.deps:
                    o.deps.append(d)
                    d.sig = True


def build():
    nc = bass.Bass("TRN2", target_bir_lowering=False)
    K = KB(nc)
    P = K.P

    def din(name, shape, dt=F32):
        return nc.dram_tensor(name, list(shape), dt, kind="ExternalInput").ap()

    def dout(name, shape):
        return nc.dram_tensor(name, list(shape), F32, kind="ExternalOutput").ap()

    def dscr(name, shape, dt):
        return nc.dram_tensor(name, list(shape), dt, kind="Internal").ap()

    xp = din("xp", [T, D]); xs = din("xs", [TS, D])
    ck = din("ck", [T, H, DH]); cv = din("cv", [T, H, DH]); sg_in = din("sg", [4, 256, 512])
    c32 = din("c32", [32, 128]); wada = din("wada", [2, D, 3 * D]); bada = din("bada", [96, 128])
    brow = din("brow", [2, 3 * D]); ng = din("ng", [32, 128])
    wq = din("wq", [D, 4 * D]); wo0 = din("wo0", [D, D]); wg = din("wg", [D, 6160])
    wa2 = din("wa2", [16, 1024]); ba = din("ba", [1, 1024]); glag = din("glag", [1, D])
    wo1 = din("wo1", [D, D]); fing = din("fing", [1, D])
    cf_d = din("cf", [128, NCF]); cb_d = din("cb", [128, NCB], BF16)

    yp = dout("yp", [T, D]); ys = dout("ys", [TS, D]); kp = dout("kp", [T, D]); vp = dout("vp", [T, D])
    ks = dout("ks", [TS, D]); vs = dout("vs", [TS, D]); sp_o = dout("spo", [4, 256, 512]); ss_o = dout("sso", [4, 256, 512])

    wq_s = dscr("wq_s", [D, 4 * D], BF16); wo0_s = dscr("wo0_s", [D, D], BF16)
    wg_s = dscr("wg_s", [D, 6160], BF16); wo1_s = dscr("wo1_s", [D, D], BF16)
    qT_s = dscr("qT_s", [H, DH, NT], BF16); kT_s = dscr("kT_s", [H, DH, NT], BF16)
    sgT_s = dscr("sgT_s", [H, DH, NT], BF16); v_s = dscr("v_s", [NT, D], BF16)
    ogT_s = dscr("ogT_s", [H, DH, NT], BF16)
    x1_s = dscr("x1_s", [NT, D], F32); h1T_s = dscr("h1T_s", [16, 128, NT], BF16)
    qinT_s = dscr("qinT_s", [8, 128, NT], BF16); kdT_s = dscr("kdT_s", [8, 128, NT], BF16)
    kd_s = dscr("kd_s", [NT, 1024], BF16); vg_s = dscr("vg_s", [NT, D], BF16); sr_s = dscr("sr_s", [NT, D], BF16)

    R_wq = [Res("wq%d" % i) for i in range(16)]
    R_wo0 = [Res("wo0") for i in range(16)]; R_wg = [Res("wg") for i in range(16)]; R_wo1 = [Res("wo1") for i in range(16)]
    R_scr = {n: Res(n) for n in ["qT", "kT", "sgT", "v", "ogT", "x1", "h1T", "qinT", "kdT", "kd", "vg", "sr"]}

    top = ExitStack()
    with top:
        cf = K.sb(top, [128, NCF], F32, "cf"); cb = K.sb(top, [128, NCB], BF16, "cb")
        identf = cf.t[:, 0:128]; maskST = cf.t[:, 128:256]; uincl = cf.t[:, 256:384]; maskLE = cf.t[:, 384:512]; ones_f = cf.t[:, 512:640]
        identb = cb.t[:, 0:128]; lmat = cb.t[:, 128:256]; ones_b = cb.t[:, 256:384]
        esL0 = ExitStack()
        gate_b = [None, [K.sb(top, [128, D], F32, "gate") for g in range(2)]]
        glag_b = K.sb(top, [128, D], F32, "glag"); fing_b = K.sb(top, [128, D], F32, "fing")
        shiftT = [K.sb(top, [128, 16, 2], F32, "shT") for l in range(2)]
        gsT = [K.sb(top, [128, 16, 2], F32, "gsT") for l in range(2)]
        dec_all = K.sb(top, [128, 33, 8], F32, "dec")
        junk = K.sb(top, [128, D], BF16, "junk")

        K.dma("sp", cf.t[:], cf_d[:, :], [], [cf.r], "cf")
        K.dma("sp", cb.t[:], cb_d[:, :], [], [cb.r], "cb")
        K.dma("sp", glag_b.t[:], glag[0, :].partition_broadcast(128), [], [glag_b.r], "glag")
        K.dma("sp", fing_b.t[:], fing[0, :].partition_broadcast(128), [], [fing_b.r], "fing")

        gate_b[0] = [K.sb(esL0, [128, D], F32, "gate0") for g in range(2)]
        with ExitStack() as es:
            c32_t = K.sb(es, [32, 128], F32); bada_t = K.sb(es, [96, 128], F32); ng_t = K.sb(es, [32, 128], F32)
            brow_b = K.sb(es, [1, 2 * 3 * D], BF16)
            cT = K.sb(es, [128, 32], F32); cTb2 = K.sb(es, [128, 16, 2], BF16)
            cB = [K.sb(es, [128, 16, 128], BF16) for g in range(2)]
            badaT = K.sb(es, [128, 96], F32); ngT = K.sb(es, [128, 32], F32)
            tmpA = K.sb(es, [128, 16, 2], F32)
            wa_ring = K.ring(es, 2, [128, 16, 512], BF16, "wa")
            waf_ring = K.ring(es, 2, [128, 16, 512], F32, "waf")
            tps = K.ps(es, [128, 512], F32, "tps")
            adaps = K.ps(es, [128, 256, 2], F32, "adaps")
            gps = K.ring(es, 2, [128, 512], F32, "gps", psum=True)

            K.dma("sp", c32_t.t[:], c32[:, :], [], [c32_t.r], "c32")
            K.dma("sp", bada_t.t[:], bada[:, :], [], [bada_t.r], "bada")
            K.dma("sp", ng_t.t[:], ng[:, :], [], [ng_t.r], "ng")
            K.dma("pool", brow_b.t[:], brow.rearrange("l n -> (l n)").rearrange("(o n) -> o n", o=1), [], [brow_b.r], "browb",
                  max_dma_last_dim=4096)
            for wc in range(16):
                K.dma("pool", wq_s[:, wc * 512:(wc + 1) * 512], wq[:, wc * 512:(wc + 1) * 512], [], [R_wq[wc]], "pcq", nobar=True)
            for (src, dst, rl) in [(wo0, wo0_s, R_wo0), (wg, wg_s, R_wg), (wo1, wo1_s, R_wo1)]:
                for rb in range(16):
                    K.dma("pool", dst[rb * 128:(rb + 1) * 128, :], src[rb * 128:(rb + 1) * 128, :], [], [rl[rb]], "pc", nobar=True,
                          max_dma_last_dim=8192)
            K.tr(tps.t[:, 0:32], c32_t.t[:, :], identf[0:32, 0:32], [c32_t.r, cf.r], [tps.r])
            K.cp("dve", cT.t[:], tps.t[:, 0:32], [tps.r], [cT.r])
            K.tr(tps.t[:, 0:96], bada_t.t[:, :], identf[0:96, 0:96], [bada_t.r, cf.r], [tps.r])
            K.cp("dve", badaT.t[:], tps.t[:, 0:96], [tps.r], [badaT.r])
            K.tr(tps.t[:, 0:32], ng_t.t[:, :], identf[0:32, 0:32], [ng_t.r, cf.r], [tps.r])
            K.cp("dve", ngT.t[:], tps.t[:, 0:32], [tps.r], [ngT.r])
            for g in range(2):
                K.cp("dve", cTb2.t[:, :, g], cT.t[:, g * 16:(g + 1) * 16], [cT.r], [cTb2.r])
                for kc in range(16):
                    K.ts("dve", cB[g].t[:, kc, :], ones_f, cT.t[:, g * 16 + kc:g * 16 + kc + 1], None, ALU.mult, None,
                         [cT.r, cf.r], [cB[g].r])
            wav = wada.rearrange("l (kc p) n -> l p kc n", p=128)
            wi = 0
            wa_r2 = [Res("wa2a"), Res("wa2b")]

            def ada_load(k):
                if k < 24:
                    K.dma("sp", waf_ring[k % 2].t[:], wav[k // 12, :, :, (k % 12) * 512:(k % 12 + 1) * 512], [], [waf_ring[k % 2].r], "waf%d" % (k % 2))

            ada_load(0)
            for l in range(2):
                for j in range(12):
                    wa = wa_ring[wi % 2]; waf = waf_ring[wi % 2]; wi += 1
                    ada_load(wi)
                    wa2r = wa_r2[(wi - 1) % 2]
                    K.cp("dve", wa.t[:, 0:8, :], waf.t[:, 0:8, :], [waf.r], [wa.r])
                    K.cp("act", wa.t[:, 8:16, :], waf.t[:, 8:16, :], [waf.r], [wa2r])
                    if j < 8:
                        for fi in range(4):
                            fc = j * 4 + fi
                            for kc in range(16):
                                K.mm(adaps.t[:, fc, :], wa.t[:, kc, fi * 128:(fi + 1) * 128], cTb2.t[:, kc, :],
                                     kc == 0, kc == 15, [wa.r, wa2r, cTb2.r], [adaps.r])
                    else:
                        for g in range(2):
                            gp = gps[g]
                            for kc in range(16):
                                K.mm(gp.t[:, :], cB[g].t[:, kc, :], wa.t[:, kc, :], kc == 0, False, [wa.r, wa2r, cB[g].r], [gp.r])
                            K.mm(gp.t[:, :], ones_b[0:1, :], brow_b.t[0:1, l * 6144 + j * 512: l * 6144 + (j + 1) * 512],
                                 False, True, [cb.r, brow_b.r], [gp.r])
                            K.cp("act", gate_b[l][g].t[:, (j - 8) * 512:(j - 7) * 512], gp.t[:, :], [gp.r], [gate_b[l][g].r])
                    if j == 7:
                        for g in range(2):
                            K.tt("dve", shiftT[l].t[:, :, g], adaps.t[:, 0:16, g], badaT.t[:, l * 48:l * 48 + 16], ALU.add,
                                 [adaps.r, badaT.r], [shiftT[l].r])
                            K.tt("dve", tmpA.t[:, :, g], adaps.t[:, 16:32, g], badaT.t[:, l * 48 + 16:l * 48 + 32], ALU.add,
                                 [adaps.r, badaT.r], [tmpA.r])
                            K.stt(gsT[l].t[:, :, g], tmpA.t[:, :, g], 1.0, ngT.t[:, l * 16:(l + 1) * 16], ALU.add, ALU.mult,
                                  [tmpA.r, ngT.r], [gsT[l].r])
            P.barrier()
        def norm_p1(xt_ap, xt_res, ntok, hb, ssb):
            K.ttr(junk.t[0:ntok, :], xt_ap, xt_ap, ssb.t[0:ntok, 0:1], [xt_res], [ssb.r])
            K.act(ssb.t[0:ntok, 1:2], ssb.t[0:ntok, 0:1], AF.Ln, [ssb.r], [ssb.r], bias=EPS, scale=1.0 / D)
            K.act(ssb.t[0:ntok, 2:3], ssb.t[0:ntok, 1:2], AF.Exp, [ssb.r], [ssb.r], scale=-0.5)
            K.ts("dve", hb.t[0:ntok, :], xt_ap, ssb.t[0:ntok, 2:3], None, ALU.mult, None, [xt_res, ssb.r], [hb.r])

        def norm_tile(xt_ap, xt_res, ntok, l, g, hb, ssb, tpr, tpi, hT_ap_fn, hT_res, evac_eng):
            norm_p1(xt_ap, xt_res, ntok, hb, ssb)
            norm_p2(ntok, l, g, hb, tpr, tpi, hT_ap_fn, hT_res, evac_eng)

        def norm_p2(ntok, l, g, hb, tpr, tpi, hT_ap_fn, hT_res, evac_eng):
            for grp in range(4):
                tp = tpr[tpi[0] % len(tpr)]; tpi[0] += 1
                for i in range(4):
                    kc = grp * 4 + i
                    K.tr(tp.t[:, i, 0:ntok], hb.t[0:ntok, kc * 128:(kc + 1) * 128], identb[0:ntok, 0:ntok], [hb.r, cb.r], [tp.r])
                for i in range(4):
                    kc = grp * 4 + i
                    if evac_eng == "dve":
                        K.ts("dve", hT_ap_fn(kc), tp.t[:, i, 0:ntok], gsT[l].t[:, kc, g:g + 1], shiftT[l].t[:, kc, g:g + 1],
                             ALU.mult, ALU.add, [tp.r, gsT[l].r, shiftT[l].r], [hT_res])
                    else:
                        K.act(hT_ap_fn(kc), tp.t[:, i, 0:ntok], AF.Identity, [tp.r, gsT[l].r, shiftT[l].r], [hT_res],
                              bias=shiftT[l].t[:, kc, g:g + 1], scale=gsT[l].t[:, kc, g:g + 1])

        def silu_from_psum(ps_ap, ps_res, n_p, n_f, out_ap, out_res, tmp):
            K.act(tmp.t[0:n_p, 0:n_f], ps_ap, AF.Exp, [ps_res], [tmp.r], scale=-1.0)
            K.ts("dve", tmp.t[0:n_p, 0:n_f], tmp.t[0:n_p, 0:n_f], 1.0, None, ALU.add, None, [tmp.r], [tmp.r])
            K.recip(tmp.t[0:n_p, 0:n_f], tmp.t[0:n_p, 0:n_f], [tmp.r], [tmp.r])
            K.tt("dve", out_ap, ps_ap, tmp.t[0:n_p, 0:n_f], ALU.mult, [ps_res, tmp.r], [out_res])

        blocks = [(bi * 512, 512, 0) for bi in range(NBLK)] + [(T, TS, 1)]

        def xrows(tok0, n):
            return xp[tok0:tok0 + n, :] if tok0 < T else xs[tok0 - T:tok0 - T + n, :]

        with ExitStack() as es:
          if STAGE >= 1:
                xt_r = K.ring(es, 4, [128, D], F32, "xt"); hb_r = K.ring(es, 4, [128, D], BF16, "hb")
                ss_r = K.ring(es, 4, [128, 4], F32, "ss")
                hT_r = K.ring(es, 2, [128, 16, 512], BF16, "hT")
                hT_res = [[Res("hTr") for t in range(4)] for s in range(2)]
                wt_r = K.ring(es, 3, [128, 16, 512], BF16, "wt")
                qst_r = K.ring(es, 2, [128, 512], BF16, "qst"); stmp_r = K.ring(es, 1, [128, 512], F32, "stmp")
                kf_r = K.ring(es, 2, [128, 512], F32, "kf"); kb_r = K.ring(es, 2, [128, 512], BF16, "kb")
                kTst_r = K.ring(es, 1, [128, 4, 512], BF16, "kTst")
                tp_r = K.ring(es, 2, [128, 8, 128], BF16, "tp", psum=True)
                ps_r = K.ring(es, 4, [128, 512], F32, "psA", psum=True)
                tp2_r = K.ring(es, 2, [128, 8, 128], BF16, "tp2", psum=True)
                tpi = [0]; ci = {"xt": 0, "wt": 0, "ps": 0, "st": 0, "kf": 0, "kT": 0, "tp2": 0}
                wqv = wq_s.rearrange("(kc p) n -> p kc n", p=128)
                def a_xload(bi):
                    tok0, ntokb, g = blocks[bi]
                    for t in range((ntokb + 127) // 128):
                        ntok = min(128, ntokb - t * 128)
                        K.dma("sp", xt_r[t].t[0:ntok, :], xrows(tok0 + t * 128, ntok), [], [xt_r[t].r], "xt%d" % t)

                def a_p1(bi):
                    tok0, ntokb, g = blocks[bi]
                    for t in range((ntokb + 127) // 128):
                        ntok = min(128, ntokb - t * 128)
                        norm_p1(xt_r[t].t[0:ntok, :], xt_r[t].r, ntok, hb_r[t], ss_r[t])

                def a_p2(bi):
                    tok0, ntokb, g = blocks[bi]
                    hTb = hT_r[bi % 2]; hTres = hT_res[bi % 2]
                    for t in range((ntokb + 127) // 128):
                        ntok = min(128, ntokb - t * 128)
                        norm_p2(ntok, 0, g, hb_r[t], tp_r, tpi,
                                (lambda kc, hTb=hTb, ntok=ntok, t=t: hTb.t[:, kc, t * 128:t * 128 + ntok]), hTres[t], "dve" if t % 2 == 0 else "act")

                steps = [(bi, wc) for bi in range(len(blocks)) for wc in range(16)]

                def a_wload(si):
                    if si < len(steps):
                        wc = steps[si][1]
                        K.dma("sp", wt_r[si % 3].t[:], wqv[:, :, wc * 512:(wc + 1) * 512], [R_wq[wc]], [wt_r[si % 3].r], "wt%d" % (si % 3))

                a_xload(0); a_p1(0); a_p2(0)
                a_wload(0); a_wload(1)
                for bi, (tok0, ntokb, g) in enumerate(blocks):
                    ntl = (ntokb + 127) // 128
                    hTb = hT_r[bi % 2]; hTres = hT_res[bi % 2]
                    hres = [hTres[t] for t in range(ntl)]
                    for wc in range(16):
                        si = bi * 16 + wc
                        wt = wt_r[si % 3]
                        a_wload(si + 2)
                        if bi + 1 < len(blocks):
                            if wc == 2:
                                a_xload(bi + 1)
                            if wc == 5:
                                a_p1(bi + 1)
                            if wc == 9:
                                a_p2(bi + 1)
                        kind = wc // 4; hbase = (wc % 4) * 4
                        if kind in (0, 3):
                            for hh in range(4):
                                ps = ps_r[ci["ps"] % 4]; ci["ps"] += 1
                                for kc in range(16):
                                    K.mm(ps.t[:, 0:ntokb], wt.t[:, kc, hh * 128:(hh + 1) * 128], hTb.t[:, kc, 0:ntokb],
                                         kc == 0, kc == 15, [wt.r] + hres, [ps.r])
                                qst = qst_r[ci["st"] % 2]; stmp = stmp_r[0]; ci["st"] += 1
                                if kind == 0:
                                    K.act(qst.t[:, 0:ntokb], ps.t[:, 0:ntokb], AF.Copy, [ps.r], [qst.r], scale=DH ** -0.5)
                                    K.dma("sp", qT_s[hbase + hh, :, tok0:tok0 + ntokb], qst.t[:, 0:ntokb], [qst.r], [],
                                          "qst%d" % (ci["st"] % 2))
                                else:
                                    silu_from_psum(ps.t[:, 0:ntokb], ps.r, 128, ntokb, qst.t[:, 0:ntokb], qst.r, stmp)
                                    K.dma("sp", sgT_s[hbase + hh, :, tok0:tok0 + ntokb], qst.t[:, 0:ntokb], [qst.r], [],
                                          "qst%d" % (ci["st"] % 2))
                        else:
                            kTst = kTst_r[0]; ci["kT"] += 1
                            for t in range(ntl):
                                ntok = min(128, ntokb - t * 128)
                                ps = ps_r[ci["ps"] % 4]; ci["ps"] += 1
                                for kc in range(16):
                                    K.mm(ps.t[0:ntok, :], hTb.t[:, kc, t * 128:t * 128 + ntok], wt.t[:, kc, :], kc == 0, kc == 15, [wt.r, hTres[t]], [ps.r])
                                kf = kf_r[ci["kf"] % 2]; kb = kb_r[ci["kf"] % 2]; ci["kf"] += 1
                                K.cp("act", kf.t[0:ntok, :], ps.t[0:ntok, :], [ps.r], [kf.r])
                                if tok0 < T:
                                    dst = (kp if kind == 1 else vp)[tok0 + t * 128:tok0 + t * 128 + ntok, hbase * 128:hbase * 128 + 512]
                                else:
                                    dst = (ks if kind == 1 else vs)[0:ntok, hbase * 128:hbase * 128 + 512]
                                K.dma("sp", dst, kf.t[0:ntok, :], [kf.r], [Res()], "kf%d" % (ci["kf"] % 2))
                                K.cp("dve", kb.t[0:ntok, :], ps.t[0:ntok, :], [ps.r], [kb.r])
                                if kind == 1:
                                    tp2 = tp2_r[ci["tp2"] % 2]; ci["tp2"] += 1
                                    for hh in range(4):
                                        K.tr(tp2.t[:, hh, 0:ntok], kb.t[0:ntok, hh * 128:(hh + 1) * 128], identb[0:ntok, 0:ntok],
                                             [kb.r, cb.r], [tp2.r])
                                    K.cp("act" if t % 2 else "dve", kTst.t[:, :, t * 128:t * 128 + ntok], tp2.t[:, 0:4, 0:ntok], [tp2.r], [kTst.r])
                                else:
                                    K.dma("sp", v_s[tok0 + t * 128:tok0 + t * 128 + ntok, hbase * 128:hbase * 128 + 512], kb.t[0:ntok, :],
                                          [kb.r], [], "kb%d" % (ci["kf"] % 2))
                            if kind == 1:
                                K.dma("sp", kT_s[hbase:hbase + 4, :, tok0:tok0 + ntokb].rearrange("h d t -> d h t"), kTst.t[:, :, 0:ntokb],
                                      [kTst.r], [], "kTst0")
                P.barrier()

        if STAGE >= 2:
            with ExitStack() as es:
                qh_r = K.ring(es, 2, [128, NT], BF16, "qh"); kh_r = K.ring(es, 2, [128, NT], BF16, "kh")
                vh_r = K.ring(es, 2, [128, 33, 128], BF16, "vh"); sgh_r = K.ring(es, 2, [128, NT], BF16, "sgh")
                e_r = K.ring(es, 6, [128, 512], F32, "e"); spb_r = K.ring(es, 3, [128, 512], BF16, "spb")
                R_r = K.ring(es, 4, [128, 512], BF16, "Rr"); C_r = K.ring(es, 3, [128, 512], BF16, "C")
                a_r = K.ring(es, 3, [128, 512], BF16, "a"); og_r = K.ring(es, 2, [128, 512], BF16, "og")
                kcf = K.sb(es, [128, 32, 128], F32, "kcf"); vcf = K.sb(es, [128, 32, 128], F32, "vcf")
                kcb = K.sb(es, [128, 32, 128], BF16, "kcb"); vcb = K.sb(es, [128, 32, 128], BF16, "vcb")
                kcT = K.sb(es, [128, T], BF16, "kcT")
                z_r = K.ring(es, 3, [128, 512], F32, "z", psum=True); cum_r = K.ring(es, 2, [128, 512], F32, "cum", psum=True)
                o_r = K.ring(es, 2, [128, 512], F32, "o", psum=True); tpB_r = K.ring(es, 1, [128, 8, 128], BF16, "tpB", psum=True)
                ctr = {"blk": 0, "qb": 0, "tp": 0}

                def head_loads(h):
                    qh = qh_r[h % 2]; kh = kh_r[h % 2]; vh = vh_r[h % 2]; sgh = sgh_r[h % 2]
                    K.dma("sp", qh.t[:], qT_s[h], [], [qh.r], "qh%d" % (h % 2))
                    K.dma("sp", kh.t[:], kT_s[h], [], [kh.r], "kh%d" % (h % 2))
                    K.dma("sp", vh.t[:, 0:32, :], v_s[0:T, h * 128:(h + 1) * 128].rearrange("(t p) d -> p t d", p=128),
                          [], [vh.r], "vh%d" % (h % 2))
                    K.dma("sp", vh.t[0:TS, 32, :], v_s[T:NT, h * 128:(h + 1) * 128], [], [vh.r], "vh%d" % (h % 2))
                    K.dma("sp", sgh.t[:], sgT_s[h], [], [sgh.r], "sgh%d" % (h % 2))

                def cache_prep(h):
                    K.dma("sp", kcf.t[:], ck[:, h, :].rearrange("(t p) d -> p t d", p=128), [], [kcf.r], "kcf")
                    K.dma("sp", vcf.t[:], cv[:, h, :].rearrange("(t p) d -> p t d", p=128), [], [vcf.r], "vcf")
                    K.cp("pool", kcb.t[:], kcf.t[:], [kcf.r], [kcb.r])
                    K.cp("pool", vcb.t[:], vcf.t[:], [vcf.r], [vcb.r])
                    for gq in range(8):
                        tp = tpB_r[0]
                        for i4 in range(4):
                            K.tr(tp.t[:, i4, :], kcb.t[:, gq * 4 + i4, :], identb, [kcb.r, cb.r], [tp.r])
                        K.cp("dve", kcT.t[:, gq * 512:(gq + 1) * 512], tp.t[:, 0:4, :], [tp.r], [kcT.r])

                items = []
                for h in range(H):
                    qh = qh_r[h % 2]; kh = kh_r[h % 2]; vh = vh_r[h % 2]; sgh = sgh_r[h % 2]
                    qbs = []
                    for i in range(NBLK):
                        q0 = i * 512
                        kl = []
                        for kb in range(4 * i + 3, -1, -1):
                            m = kb - 4 * i
                            off = 128 * m if m > 0 else 0
                            kl.append((kh.t[:, kb * 128:(kb + 1) * 128], vh.t[:, kb, :], [kh.r, vh.r], 128, off, m >= 0))
                        qbs.append((q0, 512, kl, ogT_s[h, :, q0:q0 + 512]))
                    kl = [(kh.t[:, T:NT], vh.t[0:TS, 32, :], [kh.r, vh.r], TS, 0, True)]
                    for kb in range(31, -1, -1):
                        kl.append((kcT.t[:, kb * 128:(kb + 1) * 128], vcb.t[:, kb, :], [kcT.r, vcb.r], 128, 0, False))
                    qbs.append((T, TS, kl, ogT_s[h, :, T:NT]))
                    for qi, (q0, nq, kl, out_dram) in enumerate(qbs):
                        qb = {"q0": q0, "nq": nq, "out": out_dram, "qh": qh, "sgh": sgh, "n": len(kl), "slot": None}
                        for idx, (kT_ap, v_ap, rds, nk, off, diag) in enumerate(kl):
                            items.append({"qb": qb, "idx": idx, "kT": kT_ap, "v": v_ap, "rds": rds, "nk": nk, "off": off, "diag": diag,
                                          "hstart": h if (qi == 0 and idx == 0) else None})

                def s0(it):
                    qb = it["qb"]; nq = qb["nq"]; q0 = qb["q0"]; off = it["off"]; nk = it["nk"]; n = nq - off
                    if it["idx"] == 0:
                        s = ctr["qb"] % 2; ctr["qb"] += 1
                        qb["R"] = [R_r[2 * s], R_r[2 * s + 1]]; qb["o"] = o_r[s]; qb["og"] = og_r[s]; qb["s"] = s
                        K.memset("pool", qb["R"][0].t[:, 0:nq], 0.0, [qb["R"][0].r])
                        K.memset("pool", qb["R"][1].t[:, 0:nq], 0.0, [qb["R"][1].r])
                    j = ctr["blk"]; ctr["blk"] += 1
                    it["z"] = z_r[j % 3]; it["e"] = e_r[j % 6]; it["sp"] = spb_r[j % 3]; it["cum"] = cum_r[j % 2]
                    it["C"] = C_r[j % 3]; it["a"] = a_r[j % 3]
                    zb = it["z"]
                    K.mm(zb.t[0:nk, 0:n], it["kT"], qb["qh"].t[:, q0 + off:q0 + nq], True, True, it["rds"] + [qb["qh"].r], [zb.r])

                def s1(it):
                    qb = it["qb"]; nq = qb["nq"]; off = it["off"]; nk = it["nk"]; n = nq - off
                    zb = it["z"]; eb = it["e"]
                    K.act(eb.t[0:nk, 0:n], zb.t[0:nk, 0:n], AF.Exp, [zb.r], [eb.r])
                    if it["diag"]:
                        w = min(128, n)
                        K.tt("dve", eb.t[0:nk, 0:w], eb.t[0:nk, 0:w], maskST[0:nk, 0:w], ALU.mult, [eb.r, cf.r], [eb.r])

                def s2(it):
                    qb = it["qb"]; nq = qb["nq"]; off = it["off"]; nk = it["nk"]; n = nq - off
                    K.act(it["sp"].t[0:nk, 0:n], it["e"].t[0:nk, 0:n], AF.Ln, [it["e"].r], [it["sp"].r], bias=1.0)

                def s3(it):
                    qb = it["qb"]; nq = qb["nq"]; off = it["off"]; nk = it["nk"]; n = nq - off
                    sb_ = it["sp"]; cb_ = it["cum"]; Rb = qb["R"][it["idx"] % 2]; Rn = qb["R"][(it["idx"] + 1) % 2]
                    K.mm(cb_.t[0:nk, 0:n], lmat[0:nk, 0:nk], sb_.t[0:nk, 0:n], True, False, [cb.r, sb_.r], [cb_.r])
                    K.mm(cb_.t[0:nk, 0:n], ones_b[:, 0:nk], Rb.t[:, off:nq], False, True, [cb.r, Rb.r], [cb_.r])
                    if it["idx"] < qb["n"] - 1:
                        if nk < 128:
                            K.tt("pool", Rn.t[0:nk, off:nq], Rb.t[0:nk, off:nq], sb_.t[0:nk, 0:n], ALU.add, [Rb.r, sb_.r], [Rn.r])
                        else:
                            K.tt("pool", Rn.t[:, off:nq], Rb.t[:, off:nq], sb_.t[:, 0:n], ALU.add, [Rb.r, sb_.r], [Rn.r])

                def s4(it):
                    qb = it["qb"]; nq = qb["nq"]; off = it["off"]; nk = it["nk"]; n = nq - off
                    K.act(it["C"].t[0:nk, 0:n], it["cum"].t[0:nk, 0:n], AF.Exp, [it["cum"].r], [it["C"].r], scale=-1.0)

                def s5(it):
                    qb = it["qb"]; nq = qb["nq"]; off = it["off"]; nk = it["nk"]; n = nq - off
                    eb = it["e"]; Cb = it["C"]; ab = it["a"]
                    if it["idx"] == 0 and off > 0:
                        K.memset("pool", ab.t[0:nk, 0:off], 0.0, [ab.r])
                        K.tt("dve", ab.t[0:nk, off:nq], eb.t[0:nk, 0:n], Cb.t[0:nk, 0:n], ALU.mult, [eb.r, Cb.r], [ab.r])
                    else:
                        K.tt("dve", ab.t[0:nk, 0:n], eb.t[0:nk, 0:n], Cb.t[0:nk, 0:n], ALU.mult, [eb.r, Cb.r], [ab.r])

                def s6(it):
                    qb = it["qb"]; nq = qb["nq"]; off = it["off"]; nk = it["nk"]; n = nq - off
                    ab = it["a"]; ob = qb["o"]; idx = it["idx"]; nblk = qb["n"]
                    if idx == 0 and off > 0:
                        K.mm(ob.t[:, 0:nq], it["v"], ab.t[0:nk, 0:nq], True, nblk == 1, it["rds"] + [ab.r], [ob.r])
                    else:
                        K.mm(ob.t[:, off:nq], it["v"], ab.t[0:nk, 0:n], idx == 0, idx == nblk - 1, it["rds"] + [ab.r], [ob.r])

                def s7(it):
                    qb = it["qb"]; nq = qb["nq"]; q0 = qb["q0"]
                    if it["idx"] == qb["n"] - 1:
                        ogb = qb["og"]; ob = qb["o"]
                        K.tt("dve", ogb.t[:, 0:nq], ob.t[:, 0:nq], qb["sgh"].t[:, q0:q0 + nq], ALU.mult, [ob.r, qb["sgh"].r], [ogb.r])
                        K.dma("sp", qb["out"], ogb.t[:, 0:nq], [ogb.r], [], "og%d" % qb["s"])

                stages = [s0, s1, s2, s3, s4, s5, s6, s7]
                NS = len(stages)
                head_loads(0)
                nit = len(items)
                for i in range(nit + NS):
                    k = i - NS
                    if 0 <= k < nit and items[k]["hstart"] is not None:
                        hh_ = items[k]["hstart"]
                        if hh_ + 1 < H:
                            head_loads(hh_ + 1)
                        cache_prep(hh_)
                    for si, fn in enumerate(stages):
                        if 0 <= i - si < nit:
                            fn(items[i - si])
                P.barrier()

        if STAGE >= 3:
            with ExitStack() as es:
                wo = K.sb(es, [128, 16, D], BF16, "wo")
                ogb_r = K.ring(es, 2, [128, 16, 512], BF16, "ogblk")
                xt_r = K.ring(es, 2, [128, D], F32, "xtC"); hb_r = K.ring(es, 2, [128, D], BF16, "hbC")
                ss_r = K.ring(es, 2, [128, 4], F32, "ssC"); tmp_r = K.ring(es, 2, [128, 512], F32, "tmpC")
                h1st_r = K.ring(es, 2, [128, 16, 128], BF16, "h1st")
                y_r = K.ring(es, 4, [128, 512], F32, "yC", psum=True)
                tp_r = K.ring(es, 2, [128, 8, 128], BF16, "tpC", psum=True)
                tpi = [0]; ci = {"x": 0, "y": 0, "t": 0}
                K.dma("sp", wo.t[:], wo0_s.rearrange("(kc p) n -> p kc n", p=128), R_wo0, [wo.r], "woC")
                ctiles = [(tok0 + t * 128, min(128, ntokb - t * 128)) for (tok0, ntokb, g) in blocks for t in range((ntokb + 127) // 128)]

                def c_xload(k):
                    if k < len(ctiles):
                        K.dma("sp", xt_r[k % 2].t[0:ctiles[k][1], :], xrows(ctiles[k][0], ctiles[k][1]), [], [xt_r[k % 2].r], "xtC%d" % (k % 2))

                def c_ogload(bi):
                    if bi < len(blocks):
                        tok0_, ntokb_, g_ = blocks[bi]
                        K.dma("sp", ogb_r[bi % 2].t[:, :, 0:ntokb_], ogT_s[:, :, tok0_:tok0_ + ntokb_].rearrange("h d t -> d h t"), [],
                              [ogb_r[bi % 2].r], "ogblk%d" % (bi % 2))

                c_ogload(0); c_xload(0)
                pend = [None]
                for bi, (tok0, ntokb, g) in enumerate(blocks):
                    ogb = ogb_r[bi % 2]
                    c_ogload(bi + 1)
                    ntl = (ntokb + 127) // 128
                    for t in range(ntl):
                        ntok = min(128, ntokb - t * 128)
                        s = ci["x"] % 2; ci["x"] += 1
                        xt = xt_r[s]; hb = hb_r[s]; ssb = ss_r[s]; h1st = h1st_r[s]
                        c_xload(ci["x"])
                        for n in range(4):
                            y = y_r[ci["y"] % 4]; ci["y"] += 1
                            tmp = tmp_r[ci["t"] % 2]; ci["t"] += 1
                            for kc in range(16):
                                K.mm(y.t[0:ntok, :], ogb.t[:, kc, t * 128:t * 128 + ntok], wo.t[:, kc, n * 512:(n + 1) * 512], kc == 0, kc == 15,
                                     [ogb.r, wo.r], [y.r])
                            K.tt("dve", tmp.t[0:ntok, :], y.t[0:ntok, :], gate_b[0][g].t[0:ntok, n * 512:(n + 1) * 512], ALU.mult,
                                 [y.r, gate_b[0][g].r], [tmp.r])
                            K.tt("pool", xt.t[0:ntok, n * 512:(n + 1) * 512], tmp.t[0:ntok, :], xt.t[0:ntok, n * 512:(n + 1) * 512], ALU.add,
                                 [tmp.r, xt.r], [xt.r])
                        K.dma("sp", x1_s[tok0 + t * 128:tok0 + t * 128 + ntok, :], xt.t[0:ntok, :], [xt.r], [], "x1o%d" % s)
                        norm_p1(xt.t[0:ntok, :], xt.r, ntok, hb, ssb)
                        if pend[0] is not None:
                            pend[0]()
                        def _p2(ntok=ntok, g=g, hb=hb, h1st=h1st, t=t, tok0=tok0, s=s):
                            norm_p2(ntok, 1, g, hb, tp_r, tpi, (lambda kc: h1st.t[:, kc, 0:ntok]), h1st.r, "dve" if t % 2 == 0 else "act")
                            K.dma("sp", h1T_s[:, :, tok0 + t * 128:tok0 + t * 128 + ntok].rearrange("c p t -> p c t"), h1st.t[:, :, 0:ntok],
                                  [h1st.r], [], "h1o%d" % s)
                        pend[0] = _p2
                pend[0]()
                P.barrier()

        esL0.close()
        if STAGE >= 4:
            with ExitStack() as es:
                h1b_r = K.ring(es, 2, [128, 16, 512], BF16, "h1b")
                wa2_b = K.sb(es, [16, 1024], BF16, "wa2b"); ba_b = K.sb(es, [1, 1024], BF16, "bab")
                K.dma("pool", wa2_b.t[:], wa2[:, :], [], [wa2_b.r], "wa2")
                K.dma("pool", ba_b.t[:], ba[:, :], [], [ba_b.r], "bab")
                wt_r = K.ring(es, 3, [128, 16, 512], BF16, "wtD")
                wta = K.sb(es, [128, 16, 16], BF16, "wta")
                alrT = K.sb(es, [16, 512], BF16, "alrT")
                eg_r = K.ring(es, 2, [128, 1024], F32, "eg")
                EbT = K.sb(es, [128, 8, 512], F32, "EbT"); EnbT = K.sb(es, [128, 8, 512], F32, "EnbT")
                st_r = K.ring(es, 2, [128, 512], BF16, "stD"); stmp_r = K.ring(es, 2, [128, 512], F32, "stmpD")
                kdst_r = K.ring(es, 2, [128, 4, 1024], BF16, "kdst")
                ps_r = K.ring(es, 4, [128, 512], F32, "psD", psum=True)
                bT_r = K.ring(es, 2, [128, 4, 128], F32, "bT", psum=True)
                tp_r = K.ring(es, 2, [128, 8, 128], BF16, "tpD", psum=True)
                ci = {"wt": 0, "ps": 0, "st": 0, "tp": 0, "eg": 0, "bT": 0}
                wgv = wg_s.rearrange("(kc p) n -> p kc n", p=128)
                K.dma("sp", wta.t[:], wgv[:, :, 6144:6160], R_wg, [wta.r], "wta")
                tile_idx = 0
                def d_hload(bi):
                    if bi < len(blocks):
                        tok0_, ntokb_, g_ = blocks[bi]
                        K.dma("sp", h1b_r[bi % 2].t[:, :, 0:ntokb_], h1T_s[:, :, tok0_:tok0_ + ntokb_].rearrange("c p t -> p c t"), [],
                              [h1b_r[bi % 2].r], "h1b%d" % (bi % 2))

                def d_wload(si):
                    if si < 12 * len(blocks):
                        wc_ = si % 12
                        K.dma("sp", wt_r[si % 3].t[:], wgv[:, :, wc_ * 512:(wc_ + 1) * 512], R_wg, [wt_r[si % 3].r], "wtD%d" % (si % 3))

                d_hload(0); d_wload(0); d_wload(1)
                for bi, (tok0, ntokb, g) in enumerate(blocks):
                    h1b = h1b_r[bi % 2]
                    d_hload(bi + 1)
                    ntl = (ntokb + 127) // 128
                    ps = ps_r[ci["ps"] % 4]; ci["ps"] += 1
                    for kc in range(16):
                        K.mm(ps.t[0:16, 0:ntokb], wta.t[:, kc, :], h1b.t[:, kc, 0:ntokb], kc == 0, kc == 15, [wta.r, h1b.r], [ps.r])
                    K.cp("act", alrT.t[:, 0:ntokb], ps.t[0:16, 0:ntokb], [ps.r], [alrT.r])
                    for t in range(ntl):
                        ntok = min(128, ntokb - t * 128)
                        eg = eg_r[ci["eg"] % 2]; ci["eg"] += 1
                        for n in range(2):
                            ps = ps_r[ci["ps"] % 4]; ci["ps"] += 1
                            K.mm(ps.t[0:ntok, :], alrT.t[0:16, t * 128:t * 128 + ntok], wa2_b.t[0:16, n * 512:(n + 1) * 512], True, False,
                                 [alrT.r, wa2_b.r], [ps.r])
                            K.mm(ps.t[0:ntok, :], ones_b[0:1, 0:ntok], ba_b.t[0:1, n * 512:(n + 1) * 512], False, True, [cb.r, ba_b.r], [ps.r])
                            K.act(eg.t[0:ntok, n * 512:(n + 1) * 512], ps.t[0:ntok, :], AF.Exp, [ps.r], [eg.r], scale=-1.0)
                        K.act(eg.t[0:ntok, :], eg.t[0:ntok, :], AF.Ln, [eg.r], [eg.r], bias=1.0)
                        for half in range(2):
                            bT = bT_r[ci["bT"] % 2]; ci["bT"] += 1
                            for i4 in range(4):
                                dc = half * 4 + i4
                                K.mm(bT.t[:, i4, 0:ntok], eg.t[0:ntok, dc * 128:(dc + 1) * 128], uincl[0:ntok, 0:ntok], True, True,
                                     [eg.r, cf.r], [bT.r])
                            K.act(EbT.t[:, half * 4:half * 4 + 4, t * 128:t * 128 + ntok], bT.t[:, :, 0:ntok], AF.Exp, [bT.r], [EbT.r])
                            K.act(EnbT.t[:, half * 4:half * 4 + 4, t * 128:t * 128 + ntok], bT.t[:, :, 0:ntok], AF.Exp, [bT.r], [EnbT.r], scale=-1.0)
                        K.cp("dve", dec_all.t[:, tile_idx, :], EbT.t[:, :, t * 128 + ntok - 1], [EbT.r], [dec_all.r])
                        tile_idx += 1
                    kdst = kdst_r[bi % 2]
                    for wc in range(12):
                        si = bi * 12 + wc
                        wt = wt_r[si % 3]
                        d_wload(si + 2)
                        if wc < 4:
                            for i4 in range(4):
                                dc = (wc % 2) * 4 + i4
                                ps = ps_r[ci["ps"] % 4]; ci["ps"] += 1
                                for kc in range(16):
                                    K.mm(ps.t[:, 0:ntokb], wt.t[:, kc, i4 * 128:(i4 + 1) * 128], h1b.t[:, kc, 0:ntokb],
                                         kc == 0, kc == 15, [wt.r, h1b.r], [ps.r])
                                st = st_r[ci["st"] % 2]; ci["st"] += 1
                                if wc < 2:
                                    K.stt(st.t[:, 0:ntokb], ps.t[:, 0:ntokb], 256 ** -0.5, EbT.t[:, dc, 0:ntokb], ALU.mult, ALU.mult,
                                          [ps.r, EbT.r], [st.r])
                                    K.dma("sp", qinT_s[dc, :, tok0:tok0 + ntokb], st.t[:, 0:ntokb], [st.r], [], "stD%d" % (ci["st"] % 2))
                                else:
                                    K.tt("dve", st.t[:, 0:ntokb], ps.t[:, 0:ntokb], EnbT.t[:, dc, 0:ntokb], ALU.mult, [ps.r, EnbT.r], [st.r])
                                    K.dma("sp", kdT_s[dc, :, tok0:tok0 + ntokb], st.t[:, 0:ntokb], [st.r], [], "stD%d" % (ci["st"] % 2))
                                    for t in range(ntl):
                                        ntok = min(128, ntokb - t * 128)
                                        tp = tp_r[ci["tp"] % 2]; ci["tp"] += 1
                                        K.tr(tp.t[0:ntok, 0, :], st.t[:, t * 128:t * 128 + ntok], identb, [st.r, cb.r], [tp.r])
                                        K.cp("act", kdst.t[0:ntok, t, dc * 128:(dc + 1) * 128], tp.t[0:ntok, 0, :], [tp.r], [kdst.r])
                            if wc == 3:
                                for t in range(ntl):
                                    ntok = min(128, ntokb - t * 128)
                                    K.dma("sp", kd_s[tok0 + t * 128:tok0 + t * 128 + ntok, :], kdst.t[0:ntok, t, :], [kdst.r], [],
                                          "kdst%d" % (bi % 2))
                        else:
                            isv = wc < 8
                            col0 = ((wc - 4) % 4) * 512
                            for t in range(ntl):
                                ntok = min(128, ntokb - t * 128)
                                ps = ps_r[ci["ps"] % 4]; ci["ps"] += 1
                                for kc in range(16):
                                    K.mm(ps.t[0:ntok, :], h1b.t[:, kc, t * 128:t * 128 + ntok], wt.t[:, kc, :], kc == 0, kc == 15, [wt.r, h1b.r], [ps.r])
                                st = st_r[ci["st"] % 2]; stmp = stmp_r[ci["st"] % 2]; ci["st"] += 1
                                if isv:
                                    K.cp("act", st.t[0:ntok, :], ps.t[0:ntok, :], [ps.r], [st.r])
                                    K.dma("sp", vg_s[tok0 + t * 128:tok0 + t * 128 + ntok, col0:col0 + 512], st.t[0:ntok, :], [st.r], [],
                                          "stD%d" % (ci["st"] % 2))
                                else:
                                    silu_from_psum(ps.t[0:ntok, :], ps.r, ntok, 512, st.t[0:ntok, :], st.r, stmp)
                                    K.dma("sp", sr_s[tok0 + t * 128:tok0 + t * 128 + ntok, col0:col0 + 512], st.t[0:ntok, :], [st.r], [],
                                          "stD%d" % (ci["st"] % 2))
                P.barrier()

            with ExitStack() as es:
                wo = K.sb(es, [128, 16, D], BF16, "wo1")
                S = K.sb(es, [128, 8, 512], F32, "S"); Sb = K.sb(es, [128, 8, 512], BF16, "Sb")
                qin_r = K.ring(es, 2, [128, 8, 128], BF16, "qin"); kdT_r = K.ring(es, 2, [128, 8, 128], BF16, "kdT")
                kd_r = K.ring(es, 2, [128, 1024], BF16, "kd"); vg_r = K.ring(es, 2, [128, D], BF16, "vg")
                sr_r = K.ring(es, 1, [128, D], BF16, "sr"); x1_r = K.ring(es, 3, [128, D], F32, "x1")
                gsr_r = K.ring(es, 1, [128, D], F32, "gsr"); aT_r = K.ring(es, 4, [128, 128], BF16, "aT")
                osb_r = K.ring(es, 2, [128, 512], F32, "osb"); ss_r = K.ring(es, 8, [128, 4], F32, "ssD")
                og1_r = K.ring(es, 2, [128, D], BF16, "og1"); og1T_r = K.ring(es, 1, [128, 16, 128], BF16, "og1T")
                tmp_r = K.ring(es, 2, [128, 512], F32, "tmpE")
                aps_r = K.ring(es, 1, [128, 512], F32, "aps", psum=True)
                ops_r = K.ring(es, 2, [128, 512], F32, "ops", psum=True)
                sps_r = K.ring(es, 2, [128, 512], F32, "sps", psum=True)
                tp_r = K.ring(es, 1, [128, 8, 128], BF16, "tpE", psum=True)
                y_r = K.ring(es, 2, [128, 512], F32, "yE", psum=True)
                ci = {"ops": 0, "sps": 0, "ss": 0, "y": 0, "t": 0, "osb": 0, "aT": 0}
                S_res = [Res("S%d" % i) for i in range(8)]; Sb_res = [Res("Sb%d" % i) for i in range(8)]
                K.dma("sp", wo.t[:], wo1_s.rearrange("(kc p) n -> p kc n", p=128), R_wo1, [wo.r], "woE")
                K.memset("pool", S.t[:], 0.0, S_res)
                K.memset("pool", Sb.t[:], 0.0, Sb_res)
                tiles = [(t * 128, 128, 0) for t in range(NBLK * 4)] + [(T, TS, 1)]

                def e_loads(k):
                    if k >= len(tiles):
                        return
                    tok0_, ntok_, g_ = tiles[k]
                    s_ = k % 2
                    K.dma("sp", qin_r[s_].t[:, :, 0:ntok_], qinT_s[:, :, tok0_:tok0_ + ntok_].rearrange("c p t -> p c t"), [], [qin_r[s_].r], "qin%d" % s_)
                    K.dma("sp", kdT_r[s_].t[:, :, 0:ntok_], kdT_s[:, :, tok0_:tok0_ + ntok_].rearrange("c p t -> p c t"), [], [kdT_r[s_].r], "kdT%d" % s_)
                    K.dma("sp", kd_r[s_].t[0:ntok_, :], kd_s[tok0_:tok0_ + ntok_, :], [], [kd_r[s_].r], "kd%d" % s_)
                    K.dma("sp", vg_r[s_].t[0:ntok_, :], vg_s[tok0_:tok0_ + ntok_, :], [], [vg_r[s_].r], "vg%d" % s_)
                    K.dma("sp", x1_r[k % 3].t[0:ntok_, :], x1_s[tok0_:tok0_ + ntok_, :], [], [x1_r[k % 3].r], "x1i%d" % (k % 3))

                def e_srload(k):
                    if k < len(tiles):
                        tok0_, ntok_, g_ = tiles[k]
                        K.dma("sp", sr_r[0].t[0:ntok_, :], sr_s[tok0_:tok0_ + ntok_, :], [], [sr_r[0].r], "sr0")

                def e_xyz(ti):
                    tok0, ntok, g = tiles[ti]
                    s = ti % 2
                    qin = qin_r[s]; kdT = kdT_r[s]; kd = kd_r[s]; vg = vg_r[s]; sr = sr_r[0]; gsr = gsr_r[0]; og1 = og1_r[s]
                    K.tt("pool", gsr.t[0:ntok, :], sr.t[0:ntok, :], glag_b.t[0:ntok, :], ALU.mult, [sr.r, glag_b.r], [gsr.r])
                    e_srload(ti + 1)
                    aTs = []
                    for h in range(4):
                        aps = aps_r[0]; aT = aT_r[h]
                        for dc in range(2):
                            K.mm(aps.t[0:ntok, 0:ntok], kdT.t[:, h * 2 + dc, 0:ntok], qin.t[:, h * 2 + dc, 0:ntok], dc == 0, dc == 1,
                                 [kdT.r, qin.r], [aps.r])
                        K.tt("dve", aT.t[0:ntok, 0:ntok], aps.t[0:ntok, 0:ntok], maskLE[0:ntok, 0:ntok], ALU.mult, [aps.r, cf.r], [aT.r])
                        aTs.append(aT)
                    for h in range(4):
                        aT = aTs[h]
                        ops = ops_r[ci["ops"] % 2]; ci["ops"] += 1
                        K.mm(ops.t[0:ntok, :], aT.t[0:ntok, 0:ntok], vg.t[0:ntok, h * 512:(h + 1) * 512], True, False, [aT.r, vg.r], [ops.r])
                        for dc in range(2):
                            K.mm(ops.t[0:ntok, :], qin.t[:, h * 2 + dc, 0:ntok], Sb.t[:, h * 2 + dc, :], False, dc == 1,
                                 [qin.r, Sb_res[h * 2 + dc]], [ops.r])
                        osb = osb_r[ci["osb"] % 2]; ci["osb"] += 1
                        ssb = ss_r[ci["ss"] % 8]; ci["ss"] += 1
                        K.cp("act", osb.t[0:ntok, :], ops.t[0:ntok, :], [ops.r], [osb.r])
                        K.ttr(junk.t[0:ntok, 0:512], osb.t[0:ntok, :], osb.t[0:ntok, :], ssb.t[0:ntok, 0:1], [osb.r], [ssb.r])
                        K.act(ssb.t[0:ntok, 1:2], ssb.t[0:ntok, 0:1], AF.Ln, [ssb.r], [ssb.r], bias=EPS, scale=1.0 / 512)
                        K.act(ssb.t[0:ntok, 2:3], ssb.t[0:ntok, 1:2], AF.Exp, [ssb.r], [ssb.r], scale=-0.5)
                        K.stt(og1.t[0:ntok, h * 512:(h + 1) * 512], osb.t[0:ntok, :], ssb.t[0:ntok, 2:3], gsr.t[0:ntok, h * 512:(h + 1) * 512],
                              ALU.mult, ALU.mult, [osb.r, ssb.r, gsr.r], [og1.r])
                    for h in range(4):
                        for dc in range(2):
                            hd = h * 2 + dc
                            sps = sps_r[ci["sps"] % 2]; ci["sps"] += 1
                            K.mm(sps.t[:, :], kd.t[0:ntok, hd * 128:(hd + 1) * 128], vg.t[0:ntok, h * 512:(h + 1) * 512], True, True,
                                 [kd.r, vg.r], [sps.r])
                            K.tt("dve", S.t[:, hd, :], sps.t[:, :], S.t[:, hd, :], ALU.add, [sps.r, S_res[hd]], [S_res[hd]])
                            K.act(S.t[:, hd, :], S.t[:, hd, :], AF.Copy, [S_res[hd], dec_all.r], [S_res[hd]], scale=dec_all.t[:, ti, hd:hd + 1])
                            K.cp("pool", Sb.t[:, hd, :], S.t[:, hd, :], [S_res[hd]], [Sb_res[hd]])

                def e_w(ti):
                    tok0, ntok, g = tiles[ti]
                    og1 = og1_r[ti % 2]; og1T = og1T_r[0]; x1 = x1_r[ti % 3]
                    for grp in range(4):
                        tp = tp_r[0]
                        for i4 in range(4):
                            kc = grp * 4 + i4
                            K.tr(tp.t[:, i4, 0:ntok], og1.t[0:ntok, kc * 128:(kc + 1) * 128], identb[0:ntok, 0:ntok], [og1.r, cb.r], [tp.r])
                        K.cp("act" if grp % 2 else "dve", og1T.t[:, grp * 4:grp * 4 + 4, 0:ntok], tp.t[:, 0:4, 0:ntok], [tp.r], [og1T.r])
                    for n in range(4):
                        y = y_r[ci["y"] % 2]; ci["y"] += 1
                        tmp = tmp_r[ci["t"] % 2]; ci["t"] += 1
                        for kc in range(16):
                            K.mm(y.t[0:ntok, :], og1T.t[:, kc, 0:ntok], wo.t[:, kc, n * 512:(n + 1) * 512], kc == 0, kc == 15, [og1T.r, wo.r], [y.r])
                        K.tt("dve", tmp.t[0:ntok, :], y.t[0:ntok, :], gate_b[1][g].t[0:ntok, n * 512:(n + 1) * 512], ALU.mult,
                             [y.r, gate_b[1][g].r], [tmp.r])
                        K.tt("pool", x1.t[0:ntok, n * 512:(n + 1) * 512], tmp.t[0:ntok, :], x1.t[0:ntok, n * 512:(n + 1) * 512], ALU.add,
                             [tmp.r, x1.r], [x1.r])
                    ssb = ss_r[ci["ss"] % 8]; ci["ss"] += 1
                    K.ttr(junk.t[0:ntok, :], x1.t[0:ntok, :], x1.t[0:ntok, :], ssb.t[0:ntok, 0:1], [x1.r], [ssb.r])
                    K.act(ssb.t[0:ntok, 1:2], ssb.t[0:ntok, 0:1], AF.Ln, [ssb.r], [ssb.r], bias=EPS, scale=1.0 / D)
                    K.act(ssb.t[0:ntok, 2:3], ssb.t[0:ntok, 1:2], AF.Exp, [ssb.r], [ssb.r], scale=-0.5)
                    K.stt(x1.t[0:ntok, :], x1.t[0:ntok, :], ssb.t[0:ntok, 2:3], fing_b.t[0:ntok, :], ALU.mult, ALU.mult,
                          [x1.r, ssb.r, fing_b.r], [x1.r])
                    dst = yp[tok0:tok0 + ntok, :] if g == 0 else ys[0:ntok, :]
                    K.dma("sp", dst, x1.t[0:ntok, :], [x1.r], [Res()], "yo%d" % (ti % 3))

                e_loads(0); e_srload(0)
                for ti, (tok0, ntok, g) in enumerate(tiles):
                    if g == 1:
                        K.dma("sp", sp_o.rearrange("h (c p) e -> p (h c) e", p=128), S.t[:], S_res, [Res()], "Sout")
                        K.dma("sp", S.t[:], sg_in.rearrange("h (c p) e -> p (h c) e", p=128), [], S_res, "Sin")
                        K.cp("pool", Sb.t[:], S.t[:], S_res, Sb_res)
                    e_loads(ti + 1)
                    e_xyz(ti)
                    if ti >= 1:
                        e_w(ti - 1)
                e_w(len(tiles) - 1)
                K.dma("sp", ss_o.rearrange("h (c p) e -> p (h c) e", p=128), S.t[:], S_res, [Res()], "Sout2")
                P.barrier()

        P.barrier()
        for e in ENG:
            if e == "pe":
                continue
            P.op(e, lambda eng: eng.nop(), (), ())
        with ExitStack() as es2:
            P.emit(nc, es2)
    return nc


_NC = None


def _consts():
    i = np.arange(128)
    cfv = np.zeros((128, NCF), np.float32)
    cfv[:, 0:128] = np.eye(128, dtype=np.float32)
    cfv[:, 128:256] = (i[:, None] < i[None, :]).astype(np.float32)
    cfv[:, 256:384] = (i[:, None] <= i[None, :]).astype(np.float32) * (-1.0 / 16.0)
    cfv[:, 384:512] = (i[:, None] <= i[None, :]).astype(np.float32)
    cfv[:, 512:640] = 1.0
    cbv = np.zeros((128, NCB), np.float32)
    cbv[:, 0:128] = np.eye(128, dtype=np.float32)
    cbv[:, 128:256] = (i[:, None] >= i[None, :]).astype(np.float32)
    cbv[:, 256:384] = 1.0
    return cfv, cbv.astype(ml_dtypes.bfloat16)


def kernel(x_prompt, x_sample, cache_sb_k, cache_sb_v, state_gla, c_prompt, c_sample,
           w_ada, b_ada, norm_g, sb_w_in, sb_w_out, gla_w_in, gla_w_a2, gla_b_a,
           gla_norm_g, gla_w_out, final_norm_g):
    global _NC
    if _NC is None:
        _NC = build()
    f = lambda a: np.ascontiguousarray(np.asarray(a), dtype=np.float32)
    cfv, cbv = _consts()
    shared = {
        "wada": f(w_ada), "bada": f(b_ada).reshape(96, 128), "brow": f(b_ada), "ng": f(norm_g).reshape(32, 128),
        "wq": f(sb_w_in)[0], "wo0": f(sb_w_out)[0], "wg": f(gla_w_in)[0], "wa2": f(gla_w_a2)[0], "ba": f(gla_b_a),
        "glag": f(gla_norm_g), "wo1": f(gla_w_out)[0], "fing": f(final_norm_g).reshape(1, D), "cf": cfv, "cb": cbv,
    }
    x_prompt = f(x_prompt); x_sample = f(x_sample); cache_sb_k = f(cache_sb_k); cache_sb_v = f(cache_sb_v)
    state_gla = f(state_gla); c_prompt = f(c_prompt); c_sample = f(c_sample)
    in_maps = []
    for b in range(8):
        m = dict(shared)
        m["xp"] = x_prompt[b]; m["xs"] = x_sample[b]
        m["ck"] = cache_sb_k[0, b]; m["cv"] = cache_sb_v[0, b]; m["sg"] = state_gla[0, b]
        m["c32"] = np.ascontiguousarray(np.stack([c_prompt[b], c_sample[b]]).reshape(32, 128))
        in_maps.append(m)
    res = run_bass_kernel_spmd(_NC, in_maps, core_ids=list(range(8)))
    r = res.results
    y_p = np.stack([r[b]["yp"] for b in range(8)])
    y_s = np.stack([r[b]["ys"] for b in range(8)])
    k_p = np.stack([r[b]["kp"].reshape(T, H, DH) for b in range(8)])[None]
    v_p = np.stack([r[b]["vp"].reshape(T, H, DH) for b in range(8)])[None]
    k_s = np.stack([r[b]["ks"].reshape(TS, H, DH) for b in range(8)])[None]
    v_s = np.stack([r[b]["vs"].reshape(TS, H, DH) for b in range(8)])[None]
    s_p = np.stack([r[b]["spo"] for b in range(8)])[None]
    s_s = np.stack([r[b]["sso"] for b in range(8)])[None]
    return (y_p, y_s, k_p, v_p, k_s, v_s, s_p, s_s)
```

```python
import numpy as np
import ml_dtypes
from contextlib import ExitStack
import concourse.bass as bass
import concourse.mybir as mybir
from concourse.bass_utils import run_bass_kernel_spmd

F32 = mybir.dt.float32
BF16 = mybir.dt.bfloat16
AF = mybir.ActivationFunctionType
ALU = mybir.AluOpType

D = 2048
T = 4096
TS = 16
NT = T + TS
H = 16
DH = 128
EPS = 1e-6
NCF = 640
NCB = 384
STAGE = 9
NBLK = 8
ENG = ["pe", "act", "dve", "pool", "sp"]
SEM_CAP = 30000


class Res:
    __slots__ = ("name", "w", "rc", "rd", "psum")

    def __init__(self, name=""):
        self.name = name
        self.psum = False
        self.w = None
        self.rc = {}
        self.rd = []


class Op:
    __slots__ = ("eng", "fn", "deps", "dma", "key", "sem", "val", "sig")


class Prog:
    def __init__(self):
        self.ops = []
        self.bar = {e: set() for e in ENG}
        self.last = {}
        self.dmas = []

    def op(self, eng, fn, reads=(), writes=(), key=None, nobar=False):
        o = Op()
        o.eng = eng
        o.fn = fn
        o.dma = key is not None
        o.key = key
        o.sig = False
        o.sem = None
        o.val = 0
        deps = set()
        for r in reads:
            if r.w is not None:
                deps.add(r.w)
            if r.psum:
                deps.update(v for k, v in r.rc.items() if k != eng)
        for w in writes:
            if w.w is not None:
                deps.add(w.w)
            deps.update(w.rc.values())
            deps.update(w.rd)
        deps |= self.bar[eng]
        self.bar[eng] = set()
        o.deps = [d for d in deps if d is not o and (d.dma or o.dma or d.eng != eng or eng != "pe")]
        for d in o.deps:
            d.sig = True
        for r in reads:
            if o.dma:
                r.rd.append(o)
            else:
                r.rc[eng] = o
        for w in writes:
            w.w = o
            w.rc = {}
            w.rd = []
        self.ops.append(o)
        self.last[eng] = o
        if o.dma and not nobar:
            self.dmas.append(o)
        return o

    def barrier(self):
        s = set(self.last.values()) | set(self.dmas)
        for e in ENG:
            self.bar[e] |= s
        self.dmas = []

    def emit(self, nc, es):
        keys = []
        for o in self.ops:
            if o.dma and o.key not in keys:
                keys.append(o.key)
        dsem = {k: es.enter_context(nc.semaphore("d_" + k)) for k in keys}
        cnt = {e: 0 for e in ENG}
        csem = {e: es.enter_context(nc.semaphore("c_" + e + "0")) for e in ENG}
        gen = {e: 0 for e in ENG}
        dcnt = {}
        for o in self.ops:
            if o.dma:
                v = dcnt.get(o.key, 0) + 16
                dcnt[o.key] = v
                o.sem = dsem[o.key]
                o.val = v
            elif o.sig:
                if cnt[o.eng] >= SEM_CAP:
                    gen[o.eng] += 1
                    csem[o.eng] = es.enter_context(nc.semaphore("c_%s%d" % (o.eng, gen[o.eng])))
                    cnt[o.eng] = 0
                cnt[o.eng] += 1
                o.sem = csem[o.eng]
                o.val = cnt[o.eng]
        ops = self.ops

        def run(engname, eng):
            waited = {}
            for o in ops:
                if o.eng != engname:
                    continue
                need = {}
                for d in o.deps:
                    k = id(d.sem)
                    if k not in need or need[k][1] < d.val:
                        need[k] = (d.sem, d.val)
                for k, (s, v) in need.items():
                    if waited.get(k, 0) < v:
                        eng.wait_ge(s, v)
                        waited[k] = v
                ins = o.fn(eng)
                if o.dma:
                    ins.then_inc(o.sem, 16)
                elif o.sig:
                    ins.then_inc(o.sem, 1)

        with nc.Block() as block:
            @block.tensor
            def _(e):
                run("pe", e)

            @block.scalar
            def _(e):
                run("act", e)

            @block.vector
            def _(e):
                run("dve", e)

            @block.gpsimd
            def _(e):
                run("pool", e)

            @block.sync
            def _(e):
                run("sp", e)


class Buf:
    __slots__ = ("t", "r")

    def __init__(self, t, name):
        self.t = t
        self.r = Res(name)


class KB:
    def __init__(self, nc):
        self.nc = nc
        self.P = Prog()
        self.uid = 0
        self.pool_dmas = []

    def sb(self, es, shape, dt, name=None):
        self.uid += 1
        nm = "%s_%d" % (name or "sb", self.uid)
        return Buf(es.enter_context(self.nc.sbuf_tensor(nm, list(shape), dt)), nm)

    def ps(self, es, shape, dt, name=None):
        self.uid += 1
        nm = "%s_%d" % (name or "ps", self.uid)
        b = Buf(es.enter_context(self.nc.psum_tensor(nm, list(shape), dt)), nm)
        b.r.psum = True
        return b

    def ring(self, es, n, shape, dt, name=None, psum=False):
        return [(self.ps if psum else self.sb)(es, shape, dt, name) for _ in range(n)]

    def mm(self, out, lhsT, rhs, start, stop, R, W):
        self.P.op("pe", lambda e: e.matmul(out, lhsT, rhs, start=start, stop=stop), R, W)

    def tr(self, out, in_, ident, R, W):
        self.P.op("pe", lambda e: e.transpose(out, in_, ident), R, W)

    def act(self, out, in_, func, R, W, bias=None, scale=None):
        kw = {}
        if bias is not None:
            kw["bias"] = bias
        if scale is not None:
            kw["scale"] = scale
        self.P.op("act", lambda e: e.activation(out=out, in_=in_, func=func, **kw), R, W)

    def tt(self, eng, out, in0, in1, op, R, W):
        self.P.op(eng, lambda e: e.tensor_tensor(out=out, in0=in0, in1=in1, op=op), R, W)

    def ts(self, eng, out, in0, s1, s2, op0, op1, R, W):
        if op1 is None:
            self.P.op(eng, lambda e: e.tensor_scalar(out=out, in0=in0, scalar1=s1, scalar2=None, op0=op0), R, W)
        else:
            self.P.op(eng, lambda e: e.tensor_scalar(out=out, in0=in0, scalar1=s1, scalar2=s2, op0=op0, op1=op1), R, W)

    def stt(self, out, in0, scalar, in1, op0, op1, R, W):
        self.P.op("dve", lambda e: e.scalar_tensor_tensor(out=out, in0=in0, scalar=scalar, in1=in1, op0=op0, op1=op1), R, W)

    def ttr(self, out, in0, in1, accum, R, W):
        self.P.op("act", lambda e: e.activation(out=out, in_=in0, func=AF.Square, accum_out=accum), R, W)

    def cp(self, eng, out, in_, R, W):
        if eng == "act":
            self.P.op("act", lambda e: e.activation(out=out, in_=in_, func=AF.Copy), R, W)
        else:
            self.P.op(eng, lambda e: e.tensor_copy(out=out, in_=in_), R, W)

    def memset(self, eng, ap, val, W):
        self.P.op(eng, lambda e: e.memset(ap, val), (), W)

    def recip(self, out, in_, R, W):
        self.P.op("dve", lambda e: e.reciprocal(out=out, in_=in_), R, W)

    def dma(self, eng, out, in_, R, W, key, nobar=False, **kw):
        if eng == "pool":
            key = "pl%d" % (len(self.pool_dmas) % 5)
        o = self.P.op(eng, lambda e: e.dma_start(out=out, in_=in_, **kw), R, W, key=key, nobar=nobar)
        if eng == "pool":
            self.pool_dmas.append(o)
            if len(self.pool_dmas) > 4:
                d = self.pool_dmas[-5]
                if d not in o.deps:
                    o.deps.append(d)
                    d.sig = True


def build():
    nc = bass.Bass("TRN2", target_bir_lowering=False)
    K = KB(nc)
    P = K.P

    def din(name, shape, dt=F32):
        return nc.dram_tensor(name, list(shape), dt, kind="ExternalInput").ap()

    def dout(name, shape):
        return nc.dram_tensor(name, list(shape), F32, kind="ExternalOutput").ap()

    def dscr(name, shape, dt):
        return nc.dram_tensor(name, list(shape), dt, kind="Internal").ap()

    xp = din("xp", [T, D]); xs = din("xs", [TS, D])
    ck = din("ck", [T, H, DH]); cv = din("cv", [T, H, DH]); sg_in = din("sg", [4, 256, 512])
    c32 = din("c32", [32, 128]); wada = din("wada", [2, D, 3 * D]); bada = din("bada", [96, 128])
    brow = din("brow", [2, 3 * D]); ng = din("ng", [32, 128])
    wq = din("wq", [D, 4 * D]); wo0 = din("wo0", [D, D]); wg = din("wg", [D, 6160])
    wa2 = din("wa2", [16, 1024]); ba = din("ba", [1, 1024]); glag = din("glag", [1, D])
    wo1 = din("wo1", [D, D]); fing = din("fing", [1, D])
    cf_d = din("cf", [128, NCF]); cb_d = din("cb", [128, NCB], BF16)

    yp = dout("yp", [T, D]); ys = dout("ys", [TS, D]); kp = dout("kp", [T, D]); vp = dout("vp", [T, D])
    ks = dout("ks", [TS, D]); vs = dout("vs", [TS, D]); sp_o = dout("spo", [4, 256, 512]); ss_o = dout("sso", [4, 256, 512])

    wq_s = dscr("wq_s", [16, 128, 16, 512], BF16); wo0_s = dscr("wo0_s", [D, D], BF16)
    wg_s = dscr("wg_s", [12, 128, 16, 512], BF16); wta_s = dscr("wta_s", [128, 16, 16], BF16); wo1_s = dscr("wo1_s", [D, D], BF16)
    qT_s = dscr("qT_s", [H, DH, NT], BF16); kT_s = dscr("kT_s", [H, DH, NT], BF16)
    sgT_s = dscr("sgT_s", [H, DH, NT], BF16); v_s = dscr("v_s", [NT, D], BF16)
    ogT_s = dscr("ogT_s", [H, DH, NT], BF16)
    x1_s = dscr("x1_s", [NT, D], F32); h1T_s = dscr("h1T_s", [16, 128, NT], BF16)
    qinT_s = dscr("qinT_s", [8, 128, NT], BF16); kdT_s = dscr("kdT_s", [8, 128, NT], BF16)
    kd_s = dscr("kd_s", [NT, 1024], BF16); vg_s = dscr("vg_s", [NT, D], BF16); sr_s = dscr("sr_s", [NT, D], BF16)

    R_wq = [Res("wq%d" % i) for i in range(16)]
    R_wo0 = [Res("wo0") for i in range(16)]; R_wg = [Res("wg") for i in range(16)]; R_wo1 = [Res("wo1") for i in range(16)]
    R_scr = {n: Res(n) for n in ["qT", "kT", "sgT", "v", "ogT", "x1", "h1T", "qinT", "kdT", "kd", "vg", "sr"]}

    top = ExitStack()
    with top:
        cf = K.sb(top, [128, NCF], F32, "cf"); cb = K.sb(top, [128, NCB], BF16, "cb")
        identf = cf.t[:, 0:128]; maskST = cf.t[:, 128:256]; uincl = cf.t[:, 256:384]; maskLE = cf.t[:, 384:512]; ones_f = cf.t[:, 512:640]
        identb = cb.t[:, 0:128]; lmat = cb.t[:, 128:256]; ones_b = cb.t[:, 256:384]
        esL0 = ExitStack()
        gate_b = [None, [K.sb(top, [128, D], F32, "gate") for g in range(2)]]
        glag_b = K.sb(top, [128, D], F32, "glag"); fing_b = K.sb(top, [128, D], F32, "fing")
        shiftT = [K.sb(top, [128, 16, 2], F32, "shT") for l in range(2)]
        gsT = [K.sb(top, [128, 16, 2], F32, "gsT") for l in range(2)]
        dec_all = K.sb(top, [128, 33, 8], F32, "dec")
        junk = K.sb(top, [128, D], BF16, "junk")

        K.dma("sp", cf.t[:], cf_d[:, :], [], [cf.r], "cf")
        K.dma("sp", cb.t[:], cb_d[:, :], [], [cb.r], "cb")
        K.dma("sp", glag_b.t[:], glag[0, :].partition_broadcast(128), [], [glag_b.r], "glag")
        K.dma("sp", fing_b.t[:], fing[0, :].partition_broadcast(128), [], [fing_b.r], "fing")

        gate_b[0] = [K.sb(esL0, [128, D], F32, "gate0") for g in range(2)]
        with ExitStack() as es:
            c32_t = K.sb(es, [32, 128], F32); bada_t = K.sb(es, [96, 128], F32); ng_t = K.sb(es, [32, 128], F32)
            brow_b = K.sb(es, [1, 2 * 3 * D], BF16)
            cT = K.sb(es, [128, 32], F32); cTb2 = K.sb(es, [128, 16, 2], BF16)
            cB = [K.sb(es, [128, 16, 128], BF16) for g in range(2)]
            badaT = K.sb(es, [128, 96], F32); ngT = K.sb(es, [128, 32], F32)
            tmpA = K.sb(es, [128, 16, 2], F32)
            wa_ring = K.ring(es, 2, [128, 16, 512], BF16, "wa")
            waf_ring = K.ring(es, 2, [128, 16, 512], F32, "waf")
            tps = K.ps(es, [128, 512], F32, "tps")
            adaps = K.ps(es, [128, 256, 2], F32, "adaps")
            gps = K.ring(es, 2, [128, 512], F32, "gps", psum=True)

            K.dma("sp", c32_t.t[:], c32[:, :], [], [c32_t.r], "c32")
            K.dma("sp", bada_t.t[:], bada[:, :], [], [bada_t.r], "bada")
            K.dma("sp", ng_t.t[:], ng[:, :], [], [ng_t.r], "ng")
            K.dma("pool", brow_b.t[:], brow.rearrange("l n -> (l n)").rearrange("(o n) -> o n", o=1), [], [brow_b.r], "browb",
                  max_dma_last_dim=4096)
            wq_v = wq.rearrange("(kc p) n -> p kc n", p=128); wg_v = wg.rearrange("(kc p) n -> p kc n", p=128)
            for wc in range(16):
                K.dma("pool", wq_s[wc], wq_v[:, :, wc * 512:(wc + 1) * 512], [], [R_wq[wc]], "pcq", nobar=True)
            for wc in range(12):
                K.dma("pool", wg_s[wc], wg_v[:, :, wc * 512:(wc + 1) * 512], [], [R_wg[wc]], "pcg", nobar=True)
            K.dma("pool", wta_s[:, :, :], wg_v[:, :, 6144:6160], [], [R_wg[12]], "pcg", nobar=True)
            for (src, dst, rl) in [(wo0, wo0_s, R_wo0), (wo1, wo1_s, R_wo1)]:
                for rb in range(16):
                    K.dma("pool", dst[rb * 128:(rb + 1) * 128, :], src[rb * 128:(rb + 1) * 128, :], [], [rl[rb]], "pc", nobar=True,
                          max_dma_last_dim=8192)
            K.tr(tps.t[:, 0:32], c32_t.t[:, :], identf[0:32, 0:32], [c32_t.r, cf.r], [tps.r])
            K.cp("dve", cT.t[:], tps.t[:, 0:32], [tps.r], [cT.r])
            K.tr(tps.t[:, 0:96], bada_t.t[:, :], identf[0:96, 0:96], [bada_t.r, cf.r], [tps.r])
            K.cp("dve", badaT.t[:], tps.t[:, 0:96], [tps.r], [badaT.r])
            K.tr(tps.t[:, 0:32], ng_t.t[:, :], identf[0:32, 0:32], [ng_t.r, cf.r], [tps.r])
            K.cp("dve", ngT.t[:], tps.t[:, 0:32], [tps.r], [ngT.r])
            for g in range(2):
                K.cp("dve", cTb2.t[:, :, g], cT.t[:, g * 16:(g + 1) * 16], [cT.r], [cTb2.r])
                for kc in range(16):
                    K.ts("dve", cB[g].t[:, kc, :], ones_f, cT.t[:, g * 16 + kc:g * 16 + kc + 1], None, ALU.mult, None,
                         [cT.r, cf.r], [cB[g].r])
            wav = wada.rearrange("l (kc p) n -> l p kc n", p=128)
            wi = 0
            wa_r2 = [Res("wa2a"), Res("wa2b")]

            def ada_load(k):
                if k < 24:
                    K.dma("sp", waf_ring[k % 2].t[:], wav[k // 12, :, :, (k % 12) * 512:(k % 12 + 1) * 512], [], [waf_ring[k % 2].r], "waf%d" % (k % 2))

            ada_load(0)
            for l in range(2):
                for j in range(12):
                    wa = wa_ring[wi % 2]; waf = waf_ring[wi % 2]; wi += 1
                    ada_load(wi)
                    wa2r = wa_r2[(wi - 1) % 2]
                    K.cp("dve", wa.t[:, 0:8, :], waf.t[:, 0:8, :], [waf.r], [wa.r])
                    K.cp("act", wa.t[:, 8:16, :], waf.t[:, 8:16, :], [waf.r], [wa2r])
                    if j < 8:
                        for fi in range(4):
                            fc = j * 4 + fi
                            for kc in range(16):
                                K.mm(adaps.t[:, fc, :], wa.t[:, kc, fi * 128:(fi + 1) * 128], cTb2.t[:, kc, :],
                                     kc == 0, kc == 15, [wa.r, wa2r, cTb2.r], [adaps.r])
                    else:
                        for g in range(2):
                            gp = gps[g]
                            for kc in range(16):
                                K.mm(gp.t[:, :], cB[g].t[:, kc, :], wa.t[:, kc, :], kc == 0, False, [wa.r, wa2r, cB[g].r], [gp.r])
                            K.mm(gp.t[:, :], ones_b[0:1, :], brow_b.t[0:1, l * 6144 + j * 512: l * 6144 + (j + 1) * 512],
                                 False, True, [cb.r, brow_b.r], [gp.r])
                            K.cp("act", gate_b[l][g].t[:, (j - 8) * 512:(j - 7) * 512], gp.t[:, :], [gp.r], [gate_b[l][g].r])
                    if j == 7:
                        for g in range(2):
                            K.tt("dve", shiftT[l].t[:, :, g], adaps.t[:, 0:16, g], badaT.t[:, l * 48:l * 48 + 16], ALU.add,
                                 [adaps.r, badaT.r], [shiftT[l].r])
                            K.tt("dve", tmpA.t[:, :, g], adaps.t[:, 16:32, g], badaT.t[:, l * 48 + 16:l * 48 + 32], ALU.add,
                                 [adaps.r, badaT.r], [tmpA.r])
                            K.stt(gsT[l].t[:, :, g], tmpA.t[:, :, g], 1.0, ngT.t[:, l * 16:(l + 1) * 16], ALU.add, ALU.mult,
                                  [tmpA.r, ngT.r], [gsT[l].r])
            P.barrier()
        def norm_p1(xt_ap, xt_res, ntok, hb, ssb):
            K.ttr(junk.t[0:ntok, :], xt_ap, xt_ap, ssb.t[0:ntok, 0:1], [xt_res], [ssb.r])
            K.act(ssb.t[0:ntok, 1:2], ssb.t[0:ntok, 0:1], AF.Ln, [ssb.r], [ssb.r], bias=EPS, scale=1.0 / D)
            K.act(ssb.t[0:ntok, 2:3], ssb.t[0:ntok, 1:2], AF.Exp, [ssb.r], [ssb.r], scale=-0.5)
            K.ts("dve", hb.t[0:ntok, :], xt_ap, ssb.t[0:ntok, 2:3], None, ALU.mult, None, [xt_res, ssb.r], [hb.r])

        def norm_tile(xt_ap, xt_res, ntok, l, g, hb, ssb, tpr, tpi, hT_ap_fn, hT_res, evac_eng):
            norm_p1(xt_ap, xt_res, ntok, hb, ssb)
            norm_p2(ntok, l, g, hb, tpr, tpi, hT_ap_fn, hT_res, evac_eng)

        def norm_p2(ntok, l, g, hb, tpr, tpi, hT_ap_fn, hT_res, evac_eng):
            for grp in range(4):
                tp = tpr[tpi[0] % len(tpr)]; tpi[0] += 1
                for i in range(4):
                    kc = grp * 4 + i
                    K.tr(tp.t[:, i, 0:ntok], hb.t[0:ntok, kc * 128:(kc + 1) * 128], identb[0:ntok, 0:ntok], [hb.r, cb.r], [tp.r])
                for i in range(4):
                    kc = grp * 4 + i
                    if evac_eng == "dve":
                        K.ts("dve", hT_ap_fn(kc), tp.t[:, i, 0:ntok], gsT[l].t[:, kc, g:g + 1], shiftT[l].t[:, kc, g:g + 1],
                             ALU.mult, ALU.add, [tp.r, gsT[l].r, shiftT[l].r], [hT_res])
                    else:
                        K.act(hT_ap_fn(kc), tp.t[:, i, 0:ntok], AF.Identity, [tp.r, gsT[l].r, shiftT[l].r], [hT_res],
                              bias=shiftT[l].t[:, kc, g:g + 1], scale=gsT[l].t[:, kc, g:g + 1])

        def silu_from_psum(ps_ap, ps_res, n_p, n_f, out_ap, out_res, tmp):
            K.act(tmp.t[0:n_p, 0:n_f], ps_ap, AF.Exp, [ps_res], [tmp.r], scale=-1.0)
            K.ts("dve", tmp.t[0:n_p, 0:n_f], tmp.t[0:n_p, 0:n_f], 1.0, None, ALU.add, None, [tmp.r], [tmp.r])
            K.recip(tmp.t[0:n_p, 0:n_f], tmp.t[0:n_p, 0:n_f], [tmp.r], [tmp.r])
            K.tt("dve", out_ap, ps_ap, tmp.t[0:n_p, 0:n_f], ALU.mult, [ps_res, tmp.r], [out_res])

        blocks = [(bi * 512, 512, 0) for bi in range(NBLK)] + [(T, TS, 1)]

        def xrows(tok0, n):
            return xp[tok0:tok0 + n, :] if tok0 < T else xs[tok0 - T:tok0 - T + n, :]

        with ExitStack() as es:
          if STAGE >= 1:
                xt_r = K.ring(es, 4, [128, D], F32, "xt"); hb_r = K.ring(es, 4, [128, D], BF16, "hb")
                ss_r = K.ring(es, 4, [128, 4], F32, "ss")
                hT_r = K.ring(es, 2, [128, 16, 512], BF16, "hT")
                hT_res = [[Res("hTr") for t in range(4)] for s in range(2)]
                wt_r = K.ring(es, 3, [128, 16, 512], BF16, "wt")
                qst_r = K.ring(es, 2, [128, 512], BF16, "qst"); stmp_r = K.ring(es, 1, [128, 512], F32, "stmp")
                kf_r = K.ring(es, 2, [128, 512], F32, "kf"); kb_r = K.ring(es, 2, [128, 512], BF16, "kb")
                kTst_r = K.ring(es, 1, [128, 4, 512], BF16, "kTst")
                tp_r = K.ring(es, 2, [128, 8, 128], BF16, "tp", psum=True)
                ps_r = K.ring(es, 4, [128, 512], F32, "psA", psum=True)
                tp2_r = K.ring(es, 2, [128, 8, 128], BF16, "tp2", psum=True)
                tpi = [0]; ci = {"xt": 0, "wt": 0, "ps": 0, "st": 0, "kf": 0, "kT": 0, "tp2": 0}
                def a_xload(bi):
                    tok0, ntokb, g = blocks[bi]
                    for t in range((ntokb + 127) // 128):
                        ntok = min(128, ntokb - t * 128)
                        K.dma("sp", xt_r[t].t[0:ntok, :], xrows(tok0 + t * 128, ntok), [], [xt_r[t].r], "xt%d" % t)

                def a_p1(bi):
                    tok0, ntokb, g = blocks[bi]
                    for t in range((ntokb + 127) // 128):
                        ntok = min(128, ntokb - t * 128)
                        norm_p1(xt_r[t].t[0:ntok, :], xt_r[t].r, ntok, hb_r[t], ss_r[t])

                def a_p2(bi):
                    tok0, ntokb, g = blocks[bi]
                    hTb = hT_r[bi % 2]; hTres = hT_res[bi % 2]
                    for t in range((ntokb + 127) // 128):
                        ntok = min(128, ntokb - t * 128)
                        norm_p2(ntok, 0, g, hb_r[t], tp_r, tpi,
                                (lambda kc, hTb=hTb, ntok=ntok, t=t: hTb.t[:, kc, t * 128:t * 128 + ntok]), hTres[t], "dve" if t % 2 == 0 else "act")

                steps = [(bi, wc) for bi in range(len(blocks)) for wc in range(16)]

                def a_wload(si):
                    if si < len(steps):
                        wc = steps[si][1]
                        K.dma("sp", wt_r[si % 3].t[:], wq_s[wc], [R_wq[wc]], [wt_r[si % 3].r], "wt%d" % (si % 3))

                a_xload(0); a_p1(0); a_p2(0)
                a_wload(0); a_wload(1)
                for bi, (tok0, ntokb, g) in enumerate(blocks):
                    ntl = (ntokb + 127) // 128
                    hTb = hT_r[bi % 2]; hTres = hT_res[bi % 2]
                    hres = [hTres[t] for t in range(ntl)]
                    for wc in range(16):
                        si = bi * 16 + wc
                        wt = wt_r[si % 3]
                        a_wload(si + 2)
                        if bi + 1 < len(blocks):
                            if wc == 2:
                                a_xload(bi + 1)
                            if wc == 5:
                                a_p1(bi + 1)
                            if wc == 9:
                                a_p2(bi + 1)
                        kind = wc // 4; hbase = (wc % 4) * 4
                        if kind in (0, 3):
                            for hh in range(4):
                                ps = ps_r[ci["ps"] % 4]; ci["ps"] += 1
                                for kc in range(16):
                                    K.mm(ps.t[:, 0:ntokb], wt.t[:, kc, hh * 128:(hh + 1) * 128], hTb.t[:, kc, 0:ntokb],
                                         kc == 0, kc == 15, [wt.r] + hres, [ps.r])
                                qst = qst_r[ci["st"] % 2]; stmp = stmp_r[0]; ci["st"] += 1
                                if kind == 0:
                                    K.act(qst.t[:, 0:ntokb], ps.t[:, 0:ntokb], AF.Copy, [ps.r], [qst.r], scale=DH ** -0.5)
                                    K.dma("sp", qT_s[hbase + hh, :, tok0:tok0 + ntokb], qst.t[:, 0:ntokb], [qst.r], [],
                                          "qst%d" % (ci["st"] % 2))
                                else:
                                    silu_from_psum(ps.t[:, 0:ntokb], ps.r, 128, ntokb, qst.t[:, 0:ntokb], qst.r, stmp)
                                    K.dma("sp", sgT_s[hbase + hh, :, tok0:tok0 + ntokb], qst.t[:, 0:ntokb], [qst.r], [],
                                          "qst%d" % (ci["st"] % 2))
                        else:
                            kTst = kTst_r[0]; ci["kT"] += 1
                            for t in range(ntl):
                                ntok = min(128, ntokb - t * 128)
                                ps = ps_r[ci["ps"] % 4]; ci["ps"] += 1
                                for kc in range(16):
                                    K.mm(ps.t[0:ntok, :], hTb.t[:, kc, t * 128:t * 128 + ntok], wt.t[:, kc, :], kc == 0, kc == 15, [wt.r, hTres[t]], [ps.r])
                                kf = kf_r[ci["kf"] % 2]; kb = kb_r[ci["kf"] % 2]; ci["kf"] += 1
                                K.cp("act", kf.t[0:ntok, :], ps.t[0:ntok, :], [ps.r], [kf.r])
                                if tok0 < T:
                                    dst = (kp if kind == 1 else vp)[tok0 + t * 128:tok0 + t * 128 + ntok, hbase * 128:hbase * 128 + 512]
                                else:
                                    dst = (ks if kind == 1 else vs)[0:ntok, hbase * 128:hbase * 128 + 512]
                                K.dma("sp", dst, kf.t[0:ntok, :], [kf.r], [Res()], "kf%d" % (ci["kf"] % 2))
                                K.cp("dve", kb.t[0:ntok, :], ps.t[0:ntok, :], [ps.r], [kb.r])
                                if kind == 1:
                                    tp2 = tp2_r[ci["tp2"] % 2]; ci["tp2"] += 1
                                    for hh in range(4):
                                        K.tr(tp2.t[:, hh, 0:ntok], kb.t[0:ntok, hh * 128:(hh + 1) * 128], identb[0:ntok, 0:ntok],
                                             [kb.r, cb.r], [tp2.r])
                                    K.cp("act" if t % 2 else "dve", kTst.t[:, :, t * 128:t * 128 + ntok], tp2.t[:, 0:4, 0:ntok], [tp2.r], [kTst.r])
                                else:
                                    K.dma("sp", v_s[tok0 + t * 128:tok0 + t * 128 + ntok, hbase * 128:hbase * 128 + 512], kb.t[0:ntok, :],
                                          [kb.r], [], "kb%d" % (ci["kf"] % 2))
                            if kind == 1:
                                K.dma("sp", kT_s[hbase:hbase + 4, :, tok0:tok0 + ntokb].rearrange("h d t -> d h t"), kTst.t[:, :, 0:ntokb],
                                      [kTst.r], [], "kTst0")
                P.barrier()

        if STAGE >= 2:
            with ExitStack() as es:
                qh_r = K.ring(es, 2, [128, NT], BF16, "qh"); kh_r = K.ring(es, 2, [128, NT], BF16, "kh")
                vh_r = K.ring(es, 2, [128, 33, 128], BF16, "vh"); sgh_r = K.ring(es, 2, [128, NT], BF16, "sgh")
                e_r = K.ring(es, 6, [128, 512], F32, "e"); spb_r = K.ring(es, 3, [128, 512], BF16, "spb")
                R_r = K.ring(es, 4, [128, 512], BF16, "Rr"); C_r = K.ring(es, 3, [128, 512], BF16, "C")
                a_r = K.ring(es, 3, [128, 512], BF16, "a"); og_r = K.ring(es, 2, [128, 512], BF16, "og")
                kcf = K.sb(es, [128, 32, 128], F32, "kcf"); vcf = K.sb(es, [128, 32, 128], F32, "vcf")
                kcb = K.sb(es, [128, 32, 128], BF16, "kcb"); vcb = K.sb(es, [128, 32, 128], BF16, "vcb")
                kcT = K.sb(es, [128, T], BF16, "kcT")
                z_r = K.ring(es, 3, [128, 512], F32, "z", psum=True); cum_r = K.ring(es, 2, [128, 512], F32, "cum", psum=True)
                o_r = K.ring(es, 2, [128, 512], F32, "o", psum=True); tpB_r = K.ring(es, 1, [128, 8, 128], BF16, "tpB", psum=True)
                ctr = {"blk": 0, "qb": 0, "tp": 0}

                def head_loads(h):
                    qh = qh_r[h % 2]; kh = kh_r[h % 2]; vh = vh_r[h % 2]; sgh = sgh_r[h % 2]
                    K.dma("sp", qh.t[:], qT_s[h], [], [qh.r], "qh%d" % (h % 2))
                    K.dma("sp", kh.t[:], kT_s[h], [], [kh.r], "kh%d" % (h % 2))
                    K.dma("sp", vh.t[:, 0:32, :], v_s[0:T, h * 128:(h + 1) * 128].rearrange("(t p) d -> p t d", p=128),
                          [], [vh.r], "vh%d" % (h % 2))
                    K.dma("sp", vh.t[0:TS, 32, :], v_s[T:NT, h * 128:(h + 1) * 128], [], [vh.r], "vh%d" % (h % 2))
                    K.dma("sp", sgh.t[:], sgT_s[h], [], [sgh.r], "sgh%d" % (h % 2))

                def cache_prep(h):
                    K.dma("sp", kcf.t[:], ck[:, h, :].rearrange("(t p) d -> p t d", p=128), [], [kcf.r], "kcf")
                    K.dma("sp", vcf.t[:], cv[:, h, :].rearrange("(t p) d -> p t d", p=128), [], [vcf.r], "vcf")
                    K.cp("pool", kcb.t[:], kcf.t[:], [kcf.r], [kcb.r])
                    K.cp("pool", vcb.t[:], vcf.t[:], [vcf.r], [vcb.r])
                    for gq in range(8):
                        tp = tpB_r[0]
                        for i4 in range(4):
                            K.tr(tp.t[:, i4, :], kcb.t[:, gq * 4 + i4, :], identb, [kcb.r, cb.r], [tp.r])
                        K.cp("dve", kcT.t[:, gq * 512:(gq + 1) * 512], tp.t[:, 0:4, :], [tp.r], [kcT.r])

                items = []
                for h in range(H):
                    qh = qh_r[h % 2]; kh = kh_r[h % 2]; vh = vh_r[h % 2]; sgh = sgh_r[h % 2]
                    qbs = []
                    for i in range(NBLK):
                        q0 = i * 512
                        kl = []
                        for kb in range(4 * i + 3, -1, -1):
                            m = kb - 4 * i
                            off = 128 * m if m > 0 else 0
                            kl.append((kh.t[:, kb * 128:(kb + 1) * 128], vh.t[:, kb, :], [kh.r, vh.r], 128, off, m >= 0))
                        qbs.append((q0, 512, kl, ogT_s[h, :, q0:q0 + 512]))
                    kl = [(kh.t[:, T:NT], vh.t[0:TS, 32, :], [kh.r, vh.r], TS, 0, True)]
                    for kb in range(31, -1, -1):
                        kl.append((kcT.t[:, kb * 128:(kb + 1) * 128], vcb.t[:, kb, :], [kcT.r, vcb.r], 128, 0, False))
                    qbs.append((T, TS, kl, ogT_s[h, :, T:NT]))
                    for qi, (q0, nq, kl, out_dram) in enumerate(qbs):
                        qb = {"q0": q0, "nq": nq, "out": out_dram, "qh": qh, "sgh": sgh, "n": len(kl), "slot": None}
                        for idx, (kT_ap, v_ap, rds, nk, off, diag) in enumerate(kl):
                            items.append({"qb": qb, "idx": idx, "kT": kT_ap, "v": v_ap, "rds": rds, "nk": nk, "off": off, "diag": diag,
                                          "hstart": h if (qi == 0 and idx == 0) else None})

                def s0(it):
                    qb = it["qb"]; nq = qb["nq"]; q0 = qb["q0"]; off = it["off"]; nk = it["nk"]; n = nq - off
                    if it["idx"] == 0:
                        s = ctr["qb"] % 2; ctr["qb"] += 1
                        qb["R"] = [R_r[2 * s], R_r[2 * s + 1]]; qb["o"] = o_r[s]; qb["og"] = og_r[s]; qb["s"] = s
                        K.memset("pool", qb["R"][0].t[:, 0:nq], 0.0, [qb["R"][0].r])
                        K.memset("pool", qb["R"][1].t[:, 0:nq], 0.0, [qb["R"][1].r])
                    j = ctr["blk"]; ctr["blk"] += 1
                    it["z"] = z_r[j % 3]; it["e"] = e_r[j % 6]; it["sp"] = spb_r[j % 3]; it["cum"] = cum_r[j % 2]
                    it["C"] = C_r[j % 3]; it["a"] = a_r[j % 3]
                    zb = it["z"]
                    K.mm(zb.t[0:nk, 0:n], it["kT"], qb["qh"].t[:, q0 + off:q0 + nq], True, True, it["rds"] + [qb["qh"].r], [zb.r])

                def s1(it):
                    qb = it["qb"]; nq = qb["nq"]; off = it["off"]; nk = it["nk"]; n = nq - off
                    zb = it["z"]; eb = it["e"]
                    K.act(eb.t[0:nk, 0:n], zb.t[0:nk, 0:n], AF.Exp, [zb.r], [eb.r])
                    if it["diag"]:
                        w = min(128, n)
                        K.tt("dve", eb.t[0:nk, 0:w], eb.t[0:nk, 0:w], maskST[0:nk, 0:w], ALU.mult, [eb.r, cf.r], [eb.r])

                def s2(it):
                    qb = it["qb"]; nq = qb["nq"]; off = it["off"]; nk = it["nk"]; n = nq - off
                    K.act(it["sp"].t[0:nk, 0:n], it["e"].t[0:nk, 0:n], AF.Ln, [it["e"].r], [it["sp"].r], bias=1.0)

                def s3(it):
                    qb = it["qb"]; nq = qb["nq"]; off = it["off"]; nk = it["nk"]; n = nq - off
                    sb_ = it["sp"]; cb_ = it["cum"]; Rb = qb["R"][it["idx"] % 2]; Rn = qb["R"][(it["idx"] + 1) % 2]
                    K.mm(cb_.t[0:nk, 0:n], lmat[0:nk, 0:nk], sb_.t[0:nk, 0:n], True, False, [cb.r, sb_.r], [cb_.r])
                    K.mm(cb_.t[0:nk, 0:n], ones_b[:, 0:nk], Rb.t[:, off:nq], False, True, [cb.r, Rb.r], [cb_.r])
                    if it["idx"] < qb["n"] - 1:
                        if nk < 128:
                            K.tt("pool", Rn.t[0:nk, off:nq], Rb.t[0:nk, off:nq], sb_.t[0:nk, 0:n], ALU.add, [Rb.r, sb_.r], [Rn.r])
                        else:
                            K.tt("pool", Rn.t[:, off:nq], Rb.t[:, off:nq], sb_.t[:, 0:n], ALU.add, [Rb.r, sb_.r], [Rn.r])

                def s4(it):
                    qb = it["qb"]; nq = qb["nq"]; off = it["off"]; nk = it["nk"]; n = nq - off
                    K.act(it["C"].t[0:nk, 0:n], it["cum"].t[0:nk, 0:n], AF.Exp, [it["cum"].r], [it["C"].r], scale=-1.0)

                def s5(it):
                    qb = it["qb"]; nq = qb["nq"]; off = it["off"]; nk = it["nk"]; n = nq - off
                    eb = it["e"]; Cb = it["C"]; ab = it["a"]
                    if it["idx"] == 0 and off > 0:
                        K.memset("pool", ab.t[0:nk, 0:off], 0.0, [ab.r])
                        K.tt("dve", ab.t[0:nk, off:nq], eb.t[0:nk, 0:n], Cb.t[0:nk, 0:n], ALU.mult, [eb.r, Cb.r], [ab.r])
                    else:
                        K.tt("dve", ab.t[0:nk, 0:n], eb.t[0:nk, 0:n], Cb.t[0:nk, 0:n], ALU.mult, [eb.r, Cb.r], [ab.r])

                def s6(it):
                    qb = it["qb"]; nq = qb["nq"]; off = it["off"]; nk = it["nk"]; n = nq - off
                    ab = it["a"]; ob = qb["o"]; idx = it["idx"]; nblk = qb["n"]
                    if idx == 0 and off > 0:
                        K.mm(ob.t[:, 0:nq], it["v"], ab.t[0:nk, 0:nq], True, nblk == 1, it["rds"] + [ab.r], [ob.r])
                    else:
                        K.mm(ob.t[:, off:nq], it["v"], ab.t[0:nk, 0:n], idx == 0, idx == nblk - 1, it["rds"] + [ab.r], [ob.r])

                def s7(it):
                    qb = it["qb"]; nq = qb["nq"]; q0 = qb["q0"]
                    if it["idx"] == qb["n"] - 1:
                        ogb = qb["og"]; ob = qb["o"]
                        K.tt("dve", ogb.t[:, 0:nq], ob.t[:, 0:nq], qb["sgh"].t[:, q0:q0 + nq], ALU.mult, [ob.r, qb["sgh"].r], [ogb.r])
                        K.dma("sp", qb["out"], ogb.t[:, 0:nq], [ogb.r], [], "og%d" % qb["s"])

                stages = [s0, s1, s2, s3, s4, s5, s6, s7]
                NS = len(stages)
                head_loads(0)
                nit = len(items)
                for i in range(nit + NS):
                    k = i - NS
                    if 0 <= k < nit and items[k]["hstart"] is not None:
                        hh_ = items[k]["hstart"]
                        if hh_ + 1 < H:
                            head_loads(hh_ + 1)
                        cache_prep(hh_)
                    for si, fn in enumerate(stages):
                        if 0 <= i - si < nit:
                            fn(items[i - si])
                P.barrier()

        if STAGE >= 3:
            with ExitStack() as es:
                wo = K.sb(es, [128, 16, D], BF16, "wo")
                ogb_r = K.ring(es, 2, [128, 16, 512], BF16, "ogblk")
                xt_r = K.ring(es, 2, [128, D], F32, "xtC"); hb_r = K.ring(es, 2, [128, D], BF16, "hbC")
                ss_r = K.ring(es, 2, [128, 4], F32, "ssC"); tmp_r = K.ring(es, 2, [128, 512], F32, "tmpC")
                h1st_r = K.ring(es, 2, [128, 16, 128], BF16, "h1st")
                y_r = K.ring(es, 4, [128, 512], F32, "yC", psum=True)
                tp_r = K.ring(es, 2, [128, 8, 128], BF16, "tpC", psum=True)
                tpi = [0]; ci = {"x": 0, "y": 0, "t": 0}
                K.dma("sp", wo.t[:], wo0_s.rearrange("(kc p) n -> p kc n", p=128), R_wo0, [wo.r], "woC")
                ctiles = [(tok0 + t * 128, min(128, ntokb - t * 128)) for (tok0, ntokb, g) in blocks for t in range((ntokb + 127) // 128)]

                def c_xload(k):
                    if k < len(ctiles):
                        K.dma("sp", xt_r[k % 2].t[0:ctiles[k][1], :], xrows(ctiles[k][0], ctiles[k][1]), [], [xt_r[k % 2].r], "xtC%d" % (k % 2))

                def c_ogload(bi):
                    if bi < len(blocks):
                        tok0_, ntokb_, g_ = blocks[bi]
                        K.dma("sp", ogb_r[bi % 2].t[:, :, 0:ntokb_], ogT_s[:, :, tok0_:tok0_ + ntokb_].rearrange("h d t -> d h t"), [],
                              [ogb_r[bi % 2].r], "ogblk%d" % (bi % 2))

                c_ogload(0); c_xload(0)
                pend = [None]
                for bi, (tok0, ntokb, g) in enumerate(blocks):
                    ogb = ogb_r[bi % 2]
                    c_ogload(bi + 1)
                    ntl = (ntokb + 127) // 128
                    for t in range(ntl):
                        ntok = min(128, ntokb - t * 128)
                        s = ci["x"] % 2; ci["x"] += 1
                        xt = xt_r[s]; hb = hb_r[s]; ssb = ss_r[s]; h1st = h1st_r[s]
                        c_xload(ci["x"])
                        for n in range(4):
                            y = y_r[ci["y"] % 4]; ci["y"] += 1
                            tmp = tmp_r[ci["t"] % 2]; ci["t"] += 1
                            for kc in range(16):
                                K.mm(y.t[0:ntok, :], ogb.t[:, kc, t * 128:t * 128 + ntok], wo.t[:, kc, n * 512:(n + 1) * 512], kc == 0, kc == 15,
                                     [ogb.r, wo.r], [y.r])
                            K.tt("dve", tmp.t[0:ntok, :], y.t[0:ntok, :], gate_b[0][g].t[0:ntok, n * 512:(n + 1) * 512], ALU.mult,
                                 [y.r, gate_b[0][g].r], [tmp.r])
                            K.tt("pool", xt.t[0:ntok, n * 512:(n + 1) * 512], tmp.t[0:ntok, :], xt.t[0:ntok, n * 512:(n + 1) * 512], ALU.add,
                                 [tmp.r, xt.r], [xt.r])
                        K.dma("sp", x1_s[tok0 + t * 128:tok0 + t * 128 + ntok, :], xt.t[0:ntok, :], [xt.r], [], "x1o%d" % s)
                        norm_p1(xt.t[0:ntok, :], xt.r, ntok, hb, ssb)
                        if pend[0] is not None:
                            pend[0]()
                        def _p2(ntok=ntok, g=g, hb=hb, h1st=h1st, t=t, tok0=tok0, s=s):
                            norm_p2(ntok, 1, g, hb, tp_r, tpi, (lambda kc: h1st.t[:, kc, 0:ntok]), h1st.r, "dve" if t % 2 == 0 else "act")
                            K.dma("sp", h1T_s[:, :, tok0 + t * 128:tok0 + t * 128 + ntok].rearrange("c p t -> p c t"), h1st.t[:, :, 0:ntok],
                                  [h1st.r], [], "h1o%d" % s)
                        pend[0] = _p2
                pend[0]()
                P.barrier()

        esL0.close()
        if STAGE >= 4:
            with ExitStack() as es:
                h1b_r = K.ring(es, 2, [128, 16, 512], BF16, "h1b")
                wa2_b = K.sb(es, [16, 1024], BF16, "wa2b"); ba_b = K.sb(es, [1, 1024], BF16, "bab")
                K.dma("pool", wa2_b.t[:], wa2[:, :], [], [wa2_b.r], "wa2")
                K.dma("pool", ba_b.t[:], ba[:, :], [], [ba_b.r], "bab")
                wt_r = K.ring(es, 3, [128, 16, 512], BF16, "wtD")
                wta = K.sb(es, [128, 16, 16], BF16, "wta")
                alrT = K.sb(es, [16, 512], BF16, "alrT")
                eg_r = K.ring(es, 2, [128, 1024], F32, "eg")
                EbT = K.sb(es, [128, 8, 512], F32, "EbT"); EnbT = K.sb(es, [128, 8, 512], F32, "EnbT")
                st_r = K.ring(es, 2, [128, 512], BF16, "stD"); stmp_r = K.ring(es, 2, [128, 512], F32, "stmpD")
                kdst_r = K.ring(es, 2, [128, 4, 1024], BF16, "kdst")
                ps_r = K.ring(es, 4, [128, 512], F32, "psD", psum=True)
                bT_r = K.ring(es, 2, [128, 4, 128], F32, "bT", psum=True)
                tp_r = K.ring(es, 2, [128, 8, 128], BF16, "tpD", psum=True)
                ci = {"wt": 0, "ps": 0, "st": 0, "tp": 0, "eg": 0, "bT": 0}
                K.dma("sp", wta.t[:], wta_s[:, :, :], [R_wg[12]], [wta.r], "wta")
                tile_idx = 0
                def d_hload(bi):
                    if bi < len(blocks):
                        tok0_, ntokb_, g_ = blocks[bi]
                        K.dma("sp", h1b_r[bi % 2].t[:, :, 0:ntokb_], h1T_s[:, :, tok0_:tok0_ + ntokb_].rearrange("c p t -> p c t"), [],
                              [h1b_r[bi % 2].r], "h1b%d" % (bi % 2))

                def d_wload(si):
                    if si < 12 * len(blocks):
                        wc_ = si % 12
                        K.dma("sp", wt_r[si % 3].t[:], wg_s[wc_], [R_wg[wc_]], [wt_r[si % 3].r], "wtD%d" % (si % 3))

                d_hload(0); d_wload(0); d_wload(1)
                for bi, (tok0, ntokb, g) in enumerate(blocks):
                    h1b = h1b_r[bi % 2]
                    d_hload(bi + 1)
                    ntl = (ntokb + 127) // 128
                    ps = ps_r[ci["ps"] % 4]; ci["ps"] += 1
                    for kc in range(16):
                        K.mm(ps.t[0:16, 0:ntokb], wta.t[:, kc, :], h1b.t[:, kc, 0:ntokb], kc == 0, kc == 15, [wta.r, h1b.r], [ps.r])
                    K.cp("act", alrT.t[:, 0:ntokb], ps.t[0:16, 0:ntokb], [ps.r], [alrT.r])
                    for t in range(ntl):
                        ntok = min(128, ntokb - t * 128)
                        eg = eg_r[ci["eg"] % 2]; ci["eg"] += 1
                        for n in range(2):
                            ps = ps_r[ci["ps"] % 4]; ci["ps"] += 1
                            K.mm(ps.t[0:ntok, :], alrT.t[0:16, t * 128:t * 128 + ntok], wa2_b.t[0:16, n * 512:(n + 1) * 512], True, False,
                                 [alrT.r, wa2_b.r], [ps.r])
                            K.mm(ps.t[0:ntok, :], ones_b[0:1, 0:ntok], ba_b.t[0:1, n * 512:(n + 1) * 512], False, True, [cb.r, ba_b.r], [ps.r])
                            K.act(eg.t[0:ntok, n * 512:(n + 1) * 512], ps.t[0:ntok, :], AF.Exp, [ps.r], [eg.r], scale=-1.0)
                        K.act(eg.t[0:ntok, :], eg.t[0:ntok, :], AF.Ln, [eg.r], [eg.r], bias=1.0)
                        for half in range(2):
                            bT = bT_r[ci["bT"] % 2]; ci["bT"] += 1
                            for i4 in range(4):
                                dc = half * 4 + i4
                                K.mm(bT.t[:, i4, 0:ntok], eg.t[0:ntok, dc * 128:(dc + 1) * 128], uincl[0:ntok, 0:ntok], True, True,
                                     [eg.r, cf.r], [bT.r])
                            K.act(EbT.t[:, half * 4:half * 4 + 4, t * 128:t * 128 + ntok], bT.t[:, :, 0:ntok], AF.Exp, [bT.r], [EbT.r])
                            K.act(EnbT.t[:, half * 4:half * 4 + 4, t * 128:t * 128 + ntok], bT.t[:, :, 0:ntok], AF.Exp, [bT.r], [EnbT.r], scale=-1.0)
                        K.cp("dve", dec_all.t[:, tile_idx, :], EbT.t[:, :, t * 128 + ntok - 1], [EbT.r], [dec_all.r])
                        tile_idx += 1
                    kdst = kdst_r[bi % 2]
                    for wc in range(12):
                        si = bi * 12 + wc
                        wt = wt_r[si % 3]
                        d_wload(si + 2)
                        if wc < 4:
                            for i4 in range(4):
                                dc = (wc % 2) * 4 + i4
                                ps = ps_r[ci["ps"] % 4]; ci["ps"] += 1
                                for kc in range(16):
                                    K.mm(ps.t[:, 0:ntokb], wt.t[:, kc, i4 * 128:(i4 + 1) * 128], h1b.t[:, kc, 0:ntokb],
                                         kc == 0, kc == 15, [wt.r, h1b.r], [ps.r])
                                st = st_r[ci["st"] % 2]; ci["st"] += 1
                                if wc < 2:
                                    K.stt(st.t[:, 0:ntokb], ps.t[:, 0:ntokb], 256 ** -0.5, EbT.t[:, dc, 0:ntokb], ALU.mult, ALU.mult,
                                          [ps.r, EbT.r], [st.r])
                                    K.dma("sp", qinT_s[dc, :, tok0:tok0 + ntokb], st.t[:, 0:ntokb], [st.r], [], "stD%d" % (ci["st"] % 2))
                                else:
                                    K.tt("dve", st.t[:, 0:ntokb], ps.t[:, 0:ntokb], EnbT.t[:, dc, 0:ntokb], ALU.mult, [ps.r, EnbT.r], [st.r])
                                    K.dma("sp", kdT_s[dc, :, tok0:tok0 + ntokb], st.t[:, 0:ntokb], [st.r], [], "stD%d" % (ci["st"] % 2))
                                    for t in range(ntl):
                                        ntok = min(128, ntokb - t * 128)
                                        tp = tp_r[ci["tp"] % 2]; ci["tp"] += 1
                                        K.tr(tp.t[0:ntok, 0, :], st.t[:, t * 128:t * 128 + ntok], identb, [st.r, cb.r], [tp.r])
                                        K.cp("act", kdst.t[0:ntok, t, dc * 128:(dc + 1) * 128], tp.t[0:ntok, 0, :], [tp.r], [kdst.r])
                            if wc == 3:
                                for t in range(ntl):
                                    ntok = min(128, ntokb - t * 128)
                                    K.dma("sp", kd_s[tok0 + t * 128:tok0 + t * 128 + ntok, :], kdst.t[0:ntok, t, :], [kdst.r], [],
                                          "kdst%d" % (bi % 2))
                        else:
                            isv = wc < 8
                            col0 = ((wc - 4) % 4) * 512
                            for t in range(ntl):
                                ntok = min(128, ntokb - t * 128)
                                ps = ps_r[ci["ps"] % 4]; ci["ps"] += 1
                                for kc in range(16):
                                    K.mm(ps.t[0:ntok, :], h1b.t[:, kc, t * 128:t * 128 + ntok], wt.t[:, kc, :], kc == 0, kc == 15, [wt.r, h1b.r], [ps.r])
                                st = st_r[ci["st"] % 2]; stmp = stmp_r[ci["st"] % 2]; ci["st"] += 1
                                if isv:
                                    K.cp("act", st.t[0:ntok, :], ps.t[0:ntok, :], [ps.r], [st.r])
                                    K.dma("sp", vg_s[tok0 + t * 128:tok0 + t * 128 + ntok, col0:col0 + 512], st.t[0:ntok, :], [st.r], [],
                                          "stD%d" % (ci["st"] % 2))
                                else:
                                    silu_from_psum(ps.t[0:ntok, :], ps.r, ntok, 512, st.t[0:ntok, :], st.r, stmp)
                                    K.dma("sp", sr_s[tok0 + t * 128:tok0 + t * 128 + ntok, col0:col0 + 512], st.t[0:ntok, :], [st.r], [],
                                          "stD%d" % (ci["st"] % 2))
                P.barrier()

            with ExitStack() as es:
                wo = K.sb(es, [128, 16, D], BF16, "wo1")
                S = K.sb(es, [128, 8, 512], F32, "S"); Sb = K.sb(es, [128, 8, 512], BF16, "Sb")
                qin_r = K.ring(es, 2, [128, 8, 128], BF16, "qin"); kdT_r = K.ring(es, 2, [128, 8, 128], BF16, "kdT")
                kd_r = K.ring(es, 2, [128, 1024], BF16, "kd"); vg_r = K.ring(es, 2, [128, D], BF16, "vg")
                sr_r = K.ring(es, 1, [128, D], BF16, "sr"); x1_r = K.ring(es, 3, [128, D], F32, "x1")
                gsr_r = K.ring(es, 1, [128, D], F32, "gsr"); aT_r = K.ring(es, 4, [128, 128], BF16, "aT")
                osb_r = K.ring(es, 2, [128, 512], F32, "osb"); ss_r = K.ring(es, 8, [128, 4], F32, "ssD")
                og1_r = K.ring(es, 2, [128, D], BF16, "og1"); og1T_r = K.ring(es, 1, [128, 16, 128], BF16, "og1T")
                tmp_r = K.ring(es, 2, [128, 512], F32, "tmpE")
                aps_r = K.ring(es, 1, [128, 512], F32, "aps", psum=True)
                ops_r = K.ring(es, 2, [128, 512], F32, "ops", psum=True)
                sps_r = K.ring(es, 2, [128, 512], F32, "sps", psum=True)
                tp_r = K.ring(es, 1, [128, 8, 128], BF16, "tpE", psum=True)
                y_r = K.ring(es, 2, [128, 512], F32, "yE", psum=True)
                ci = {"ops": 0, "sps": 0, "ss": 0, "y": 0, "t": 0, "osb": 0, "aT": 0}
                S_res = [Res("S%d" % i) for i in range(8)]; Sb_res = [Res("Sb%d" % i) for i in range(8)]
                K.dma("sp", wo.t[:], wo1_s.rearrange("(kc p) n -> p kc n", p=128), R_wo1, [wo.r], "woE")
                K.memset("pool", S.t[:], 0.0, S_res)
                K.memset("pool", Sb.t[:], 0.0, Sb_res)
                tiles = [(t * 128, 128, 0) for t in range(NBLK * 4)] + [(T, TS, 1)]

                def e_loads(k):
                    if k >= len(tiles):
                        return
                    tok0_, ntok_, g_ = tiles[k]
                    s_ = k % 2
                    K.dma("sp", qin_r[s_].t[:, :, 0:ntok_], qinT_s[:, :, tok0_:tok0_ + ntok_].rearrange("c p t -> p c t"), [], [qin_r[s_].r], "qin%d" % s_)
                    K.dma("sp", kdT_r[s_].t[:, :, 0:ntok_], kdT_s[:, :, tok0_:tok0_ + ntok_].rearrange("c p t -> p c t"), [], [kdT_r[s_].r], "kdT%d" % s_)
                    K.dma("sp", kd_r[s_].t[0:ntok_, :], kd_s[tok0_:tok0_ + ntok_, :], [], [kd_r[s_].r], "kd%d" % s_)
                    K.dma("sp", vg_r[s_].t[0:ntok_, :], vg_s[tok0_:tok0_ + ntok_, :], [], [vg_r[s_].r], "vg%d" % s_)
                    K.dma("sp", x1_r[k % 3].t[0:ntok_, :], x1_s[tok0_:tok0_ + ntok_, :], [], [x1_r[k % 3].r], "x1i%d" % (k % 3))

                def e_srload(k):
                    if k < len(tiles):
                        tok0_, ntok_, g_ = tiles[k]
                        K.dma("sp", sr_r[0].t[0:ntok_, :], sr_s[tok0_:tok0_ + ntok_, :], [], [sr_r[0].r], "sr0")

                def e_xyz(ti):
                    tok0, ntok, g = tiles[ti]
                    s = ti % 2
                    qin = qin_r[s]; kdT = kdT_r[s]; kd = kd_r[s]; vg = vg_r[s]; sr = sr_r[0]; gsr = gsr_r[0]; og1 = og1_r[s]
                    K.tt("pool", gsr.t[0:ntok, :], sr.t[0:ntok, :], glag_b.t[0:ntok, :], ALU.mult, [sr.r, glag_b.r], [gsr.r])
                    e_srload(ti + 1)
                    aTs = []
                    for h in range(4):
                        aps = aps_r[0]; aT = aT_r[h]
                        for dc in range(2):
                            K.mm(aps.t[0:ntok, 0:ntok], kdT.t[:, h * 2 + dc, 0:ntok], qin.t[:, h * 2 + dc, 0:ntok], dc == 0, dc == 1,
                                 [kdT.r, qin.r], [aps.r])
                        K.tt("dve", aT.t[0:ntok, 0:ntok], aps.t[0:ntok, 0:ntok], maskLE[0:ntok, 0:ntok], ALU.mult, [aps.r, cf.r], [aT.r])
                        aTs.append(aT)
                    for h in range(4):
                        aT = aTs[h]
                        ops = ops_r[ci["ops"] % 2]; ci["ops"] += 1
                        K.mm(ops.t[0:ntok, :], aT.t[0:ntok, 0:ntok], vg.t[0:ntok, h * 512:(h + 1) * 512], True, False, [aT.r, vg.r], [ops.r])
                        for dc in range(2):
                            K.mm(ops.t[0:ntok, :], qin.t[:, h * 2 + dc, 0:ntok], Sb.t[:, h * 2 + dc, :], False, dc == 1,
                                 [qin.r, Sb_res[h * 2 + dc]], [ops.r])
                        osb = osb_r[ci["osb"] % 2]; ci["osb"] += 1
                        ssb = ss_r[ci["ss"] % 8]; ci["ss"] += 1
                        K.cp("act", osb.t[0:ntok, :], ops.t[0:ntok, :], [ops.r], [osb.r])
                        K.ttr(junk.t[0:ntok, 0:512], osb.t[0:ntok, :], osb.t[0:ntok, :], ssb.t[0:ntok, 0:1], [osb.r], [ssb.r])
                        K.act(ssb.t[0:ntok, 1:2], ssb.t[0:ntok, 0:1], AF.Ln, [ssb.r], [ssb.r], bias=EPS, scale=1.0 / 512)
                        K.act(ssb.t[0:ntok, 2:3], ssb.t[0:ntok, 1:2], AF.Exp, [ssb.r], [ssb.r], scale=-0.5)
                        K.stt(og1.t[0:ntok, h * 512:(h + 1) * 512], osb.t[0:ntok, :], ssb.t[0:ntok, 2:3], gsr.t[0:ntok, h * 512:(h + 1) * 512],
                              ALU.mult, ALU.mult, [osb.r, ssb.r, gsr.r], [og1.r])
                    for h in range(4):
                        for dc in range(2):
                            hd = h * 2 + dc
                            sps = sps_r[ci["sps"] % 2]; ci["sps"] += 1
                            K.mm(sps.t[:, :], kd.t[0:ntok, hd * 128:(hd + 1) * 128], vg.t[0:ntok, h * 512:(h + 1) * 512], True, True,
                                 [kd.r, vg.r], [sps.r])
                            K.tt("dve", S.t[:, hd, :], sps.t[:, :], S.t[:, hd, :], ALU.add, [sps.r, S_res[hd]], [S_res[hd]])
                            K.act(S.t[:, hd, :], S.t[:, hd, :], AF.Copy, [S_res[hd], dec_all.r], [S_res[hd]], scale=dec_all.t[:, ti, hd:hd + 1])
                            K.cp("pool", Sb.t[:, hd, :], S.t[:, hd, :], [S_res[hd]], [Sb_res[hd]])

                def e_w(ti):
                    tok0, ntok, g = tiles[ti]
                    og1 = og1_r[ti % 2]; og1T = og1T_r[0]; x1 = x1_r[ti % 3]
                    for grp in range(4):
                        tp = tp_r[0]
                        for i4 in range(4):
                            kc = grp * 4 + i4
                            K.tr(tp.t[:, i4, 0:ntok], og1.t[0:ntok, kc * 128:(kc + 1) * 128], identb[0:ntok, 0:ntok], [og1.r, cb.r], [tp.r])
                        K.cp("act" if grp % 2 else "dve", og1T.t[:, grp * 4:grp * 4 + 4, 0:ntok], tp.t[:, 0:4, 0:ntok], [tp.r], [og1T.r])
                    for n in range(4):
                        y = y_r[ci["y"] % 2]; ci["y"] += 1
                        tmp = tmp_r[ci["t"] % 2]; ci["t"] += 1
                        for kc in range(16):
                            K.mm(y.t[0:ntok, :], og1T.t[:, kc, 0:ntok], wo.t[:, kc, n * 512:(n + 1) * 512], kc == 0, kc == 15, [og1T.r, wo.r], [y.r])
                        K.tt("dve", tmp.t[0:ntok, :], y.t[0:ntok, :], gate_b[1][g].t[0:ntok, n * 512:(n + 1) * 512], ALU.mult,
                             [y.r, gate_b[1][g].r], [tmp.r])
                        K.tt("pool", x1.t[0:ntok, n * 512:(n + 1) * 512], tmp.t[0:ntok, :], x1.t[0:ntok, n * 512:(n + 1) * 512], ALU.add,
                             [tmp.r, x1.r], [x1.r])
                    ssb = ss_r[ci["ss"] % 8]; ci["ss"] += 1
                    K.ttr(junk.t[0:ntok, :], x1.t[0:ntok, :], x1.t[0:ntok, :], ssb.t[0:ntok, 0:1], [x1.r], [ssb.r])
                    K.act(ssb.t[0:ntok, 1:2], ssb.t[0:ntok, 0:1], AF.Ln, [ssb.r], [ssb.r], bias=EPS, scale=1.0 / D)
                    K.act(ssb.t[0:ntok, 2:3], ssb.t[0:ntok, 1:2], AF.Exp, [ssb.r], [ssb.r], scale=-0.5)
                    K.stt(x1.t[0:ntok, :], x1.t[0:ntok, :], ssb.t[0:ntok, 2:3], fing_b.t[0:ntok, :], ALU.mult, ALU.mult,
                          [x1.r, ssb.r, fing_b.r], [x1.r])
                    dst = yp[tok0:tok0 + ntok, :] if g == 0 else ys[0:ntok, :]
                    K.dma("sp", dst, x1.t[0:ntok, :], [x1.r], [Res()], "yo%d" % (ti % 3))

                e_loads(0); e_srload(0)
                for ti, (tok0, ntok, g) in enumerate(tiles):
                    if g == 1:
                        K.dma("sp", sp_o.rearrange("h (c p) e -> p (h c) e", p=128), S.t[:], S_res, [Res()], "Sout")
                        K.dma("sp", S.t[:], sg_in.rearrange("h (c p) e -> p (h c) e", p=128), [], S_res, "Sin")
                        K.cp("pool", Sb.t[:], S.t[:], S_res, Sb_res)
                    e_loads(ti + 1)
                    e_xyz(ti)
                    if ti >= 1:
                        e_w(ti - 1)
                e_w(len(tiles) - 1)
                K.dma("sp", ss_o.rearrange("h (c p) e -> p (h c) e", p=128), S.t[:], S_res, [Res()], "Sout2")
                P.barrier()

        P.barrier()
        for e in ENG:
            if e == "pe":
                continue
            P.op(e, lambda eng: eng.nop(), (), ())
        with ExitStack() as es2:
            P.emit(nc, es2)
    return nc


_NC = None


def _consts():
    i = np.arange(128)
    cfv = np.zeros((128, NCF), np.float32)
    cfv[:, 0:128] = np.eye(128, dtype=np.float32)
    cfv[:, 128:256] = (i[:, None] < i[None, :]).astype(np.float32)
    cfv[:, 256:384] = (i[:, None] <= i[None, :]).astype(np.float32) * (-1.0 / 16.0)
    cfv[:, 384:512] = (i[:, None] <= i[None, :]).astype(np.float32)
    cfv[:, 512:640] = 1.0
    cbv = np.zeros((128, NCB), np.float32)
    cbv[:, 0:128] = np.eye(128, dtype=np.float32)
    cbv[:, 128:256] = (i[:, None] >= i[None, :]).astype(np.float32)
    cbv[:, 256:384] = 1.0
    return cfv, cbv.astype(ml_dtypes.bfloat16)


def kernel(x_prompt, x_sample, cache_sb_k, cache_sb_v, state_gla, c_prompt, c_sample,
           w_ada, b_ada, norm_g, sb_w_in, sb_w_out, gla_w_in, gla_w_a2, gla_b_a,
           gla_norm_g, gla_w_out, final_norm_g):
    global _NC
    if _NC is None:
        _NC = build()
    f = lambda a: np.ascontiguousarray(np.asarray(a), dtype=np.float32)
    cfv, cbv = _consts()
    shared = {
        "wada": f(w_ada), "bada": f(b_ada).reshape(96, 128), "brow": f(b_ada), "ng": f(norm_g).reshape(32, 128),
        "wq": f(sb_w_in)[0], "wo0": f(sb_w_out)[0], "wg": f(gla_w_in)[0], "wa2": f(gla_w_a2)[0], "ba": f(gla_b_a),
        "glag": f(gla_norm_g), "wo1": f(gla_w_out)[0], "fing": f(final_norm_g).reshape(1, D), "cf": cfv, "cb": cbv,
    }
    x_prompt = f(x_prompt); x_sample = f(x_sample); cache_sb_k = f(cache_sb_k); cache_sb_v = f(cache_sb_v)
    state_gla = f(state_gla); c_prompt = f(c_prompt); c_sample = f(c_sample)
    in_maps = []
    for b in range(8):
        m = dict(shared)
        m["xp"] = x_prompt[b]; m["xs"] = x_sample[b]
        m["ck"] = cache_sb_k[0, b]; m["cv"] = cache_sb_v[0, b]; m["sg"] = state_gla[0, b]
        m["c32"] = np.ascontiguousarray(np.stack([c_prompt[b], c_sample[b]]).reshape(32, 128))
        in_maps.append(m)
    res = run_bass_kernel_spmd(_NC, in_maps, core_ids=list(range(8)))
    r = res.results
    y_p = np.stack([r[b]["yp"] for b in range(8)])
    y_s = np.stack([r[b]["ys"] for b in range(8)])
    k_p = np.stack([r[b]["kp"].reshape(T, H, DH) for b in range(8)])[None]
    v_p = np.stack([r[b]["vp"].reshape(T, H, DH) for b in range(8)])[None]
    k_s = np.stack([r[b]["ks"].reshape(TS, H, DH) for b in range(8)])[None]
    v_s = np.stack([r[b]["vs"].reshape(TS, H, DH) for b in range(8)])[None]
    s_p = np.stack([r[b]["spo"] for b in range(8)])[None]
    s_s = np.stack([r[b]["sso"] for b in range(8)])[None]
    return (y_p, y_s, k_p, v_p, k_s, v_s, s_p, s_s)
```

```python
import numpy as np
import ml_dtypes
from contextlib import ExitStack
import concourse.bass as bass
import concourse.mybir as mybir
from concourse.bass_utils import run_bass_kernel_spmd

F32 = mybir.dt.float32
BF16 = mybir.dt.bfloat16
AF = mybir.ActivationFunctionType
ALU = mybir.AluOpType

D = 2048
T = 4096
TS = 16
NT = T + TS
H = 16
DH = 128
EPS = 1e-6
NCF = 640
NCB = 384
STAGE = 9
NBLK = 8
ENG = ["pe", "act", "dve", "pool", "sp"]
SEM_CAP = 30000


class Res:
    __slots__ = ("name", "w", "rc", "rd", "psum")

    def __init__(self, name=""):
        self.name = name
        self.psum = False
        self.w = None
        self.rc = {}
        self.rd = []


class Op:
    __slots__ = ("eng", "fn", "deps", "dma", "key", "sem", "val", "sig")


class Prog:
    def __init__(self):
        self.ops = []
        self.bar = {e: set() for e in ENG}
        self.last = {}
        self.dmas = []

    def op(self, eng, fn, reads=(), writes=(), key=None, nobar=False):
        o = Op()
        o.eng = eng
        o.fn = fn
        o.dma = key is not None
        o.key = key
        o.sig = False
        o.sem = None
        o.val = 0
        deps = set()
        for r in reads:
            if r.w is not None:
                deps.add(r.w)
            if r.psum:
                deps.update(v for k, v in r.rc.items() if k != eng)
        for w in writes:
            if w.w is not None:
                deps.add(w.w)
            deps.update(w.rc.values())
            deps.update(w.rd)
        deps |= self.bar[eng]
        self.bar[eng] = set()
        o.deps = [d for d in deps if d is not o and (d.dma or o.dma or d.eng != eng or eng != "pe")]
        for d in o.deps:
            d.sig = True
        for r in reads:
            if o.dma:
                r.rd.append(o)
            else:
                r.rc[eng] = o
        for w in writes:
            w.w = o
            w.rc = {}
            w.rd = []
        self.ops.append(o)
        self.last[eng] = o
        if o.dma and not nobar:
            self.dmas.append(o)
        return o

    def barrier(self):
        s = set(self.last.values()) | set(self.dmas)
        for e in ENG:
            self.bar[e] |= s
        self.dmas = []

    def emit(self, nc, es):
        keys = []
        for o in self.ops:
            if o.dma and o.key not in keys:
                keys.append(o.key)
        dsem = {k: es.enter_context(nc.semaphore("d_" + k)) for k in keys}
        cnt = {e: 0 for e in ENG}
        csem = {e: es.enter_context(nc.semaphore("c_" + e + "0")) for e in ENG}
        gen = {e: 0 for e in ENG}
        dcnt = {}
        for o in self.ops:
            if o.dma:
                v = dcnt.get(o.key, 0) + 16
                dcnt[o.key] = v
                o.sem = dsem[o.key]
                o.val = v
            elif o.sig:
                if cnt[o.eng] >= SEM_CAP:
                    gen[o.eng] += 1
                    csem[o.eng] = es.enter_context(nc.semaphore("c_%s%d" % (o.eng, gen[o.eng])))
                    cnt[o.eng] = 0
                cnt[o.eng] += 1
                o.sem = csem[o.eng]
                o.val = cnt[o.eng]
        ops = self.ops

        def run(engname, eng):
            waited = {}
            for o in ops:
                if o.eng != engname:
                    continue
                need = {}
                for d in o.deps:
                    k = id(d.sem)
                    if k not in need or need[k][1] < d.val:
                        need[k] = (d.sem, d.val)
                for k, (s, v) in need.items():
                    if waited.get(k, 0) < v:
                        eng.wait_ge(s, v)
                        waited[k] = v
                ins = o.fn(eng)
                if o.dma:
                    ins.then_inc(o.sem, 16)
                elif o.sig:
                    ins.then_inc(o.sem, 1)

        with nc.Block() as block:
            @block.tensor
            def _(e):
                run("pe", e)

            @block.scalar
            def _(e):
                run("act", e)

            @block.vector
            def _(e):
                run("dve", e)

            @block.gpsimd
            def _(e):
                run("pool", e)

            @block.sync
            def _(e):
                run("sp", e)


class Buf:
    __slots__ = ("t", "r")

    def __init__(self, t, name):
        self.t = t
        self.r = Res(name)


class KB:
    def __init__(self, nc):
        self.nc = nc
        self.P = Prog()
        self.uid = 0
        self.pool_dmas = []

    def sb(self, es, shape, dt, name=None):
        self.uid += 1
        nm = "%s_%d" % (name or "sb", self.uid)
        return Buf(es.enter_context(self.nc.sbuf_tensor(nm, list(shape), dt)), nm)

    def ps(self, es, shape, dt, name=None):
        self.uid += 1
        nm = "%s_%d" % (name or "ps", self.uid)
        b = Buf(es.enter_context(self.nc.psum_tensor(nm, list(shape), dt)), nm)
        b.r.psum = True
        return b

    def ring(self, es, n, shape, dt, name=None, psum=False):
        return [(self.ps if psum else self.sb)(es, shape, dt, name) for _ in range(n)]

    def mm(self, out, lhsT, rhs, start, stop, R, W):
        self.P.op("pe", lambda e: e.matmul(out, lhsT, rhs, start=start, stop=stop), R, W)

    def tr(self, out, in_, ident, R, W):
        self.P.op("pe", lambda e: e.transpose(out, in_, ident), R, W)

    def act(self, out, in_, func, R, W, bias=None, scale=None):
        kw = {}
        if bias is not None:
            kw["bias"] = bias
        if scale is not None:
            kw["scale"] = scale
        self.P.op("act", lambda e: e.activation(out=out, in_=in_, func=func, **kw), R, W)

    def tt(self, eng, out, in0, in1, op, R, W):
        self.P.op(eng, lambda e: e.tensor_tensor(out=out, in0=in0, in1=in1, op=op), R, W)

    def ts(self, eng, out, in0, s1, s2, op0, op1, R, W):
        if op1 is None:
            self.P.op(eng, lambda e: e.tensor_scalar(out=out, in0=in0, scalar1=s1, scalar2=None, op0=op0), R, W)
        else:
            self.P.op(eng, lambda e: e.tensor_scalar(out=out, in0=in0, scalar1=s1, scalar2=s2, op0=op0, op1=op1), R, W)

    def stt(self, out, in0, scalar, in1, op0, op1, R, W):
        self.P.op("dve", lambda e: e.scalar_tensor_tensor(out=out, in0=in0, scalar=scalar, in1=in1, op0=op0, op1=op1), R, W)

    def ttr(self, out, in0, in1, accum, R, W):
        self.P.op("act", lambda e: e.activation(out=out, in_=in0, func=AF.Square, accum_out=accum), R, W)

    def cp(self, eng, out, in_, R, W):
        if eng == "act":
            self.P.op("act", lambda e: e.activation(out=out, in_=in_, func=AF.Copy), R, W)
        else:
            self.P.op(eng, lambda e: e.tensor_copy(out=out, in_=in_), R, W)

    def memset(self, eng, ap, val, W):
        self.P.op(eng, lambda e: e.memset(ap, val), (), W)

    def recip(self, out, in_, R, W):
        self.P.op("dve", lambda e: e.reciprocal(out=out, in_=in_), R, W)

    def dma(self, eng, out, in_, R, W, key, nobar=False, **kw):
        if eng == "pool":
            key = "pl%d" % (len(self.pool_dmas) % 5)
        o = self.P.op(eng, lambda e: e.dma_start(out=out, in_=in_, **kw), R, W, key=key, nobar=nobar)
        if eng == "pool":
            self.pool_dmas.append(o)
            if len(self.pool_dmas) > 4:
                d = self.pool_dmas[-5]
                if d not in o.deps:
                    o.deps.append(d)
                    d.sig = True


def build():
    nc = bass.Bass("TRN2", target_bir_lowering=False)
    K = KB(nc)
    P = K.P

    def din(name, shape, dt=F32):
        return nc.dram_tensor(name, list(shape), dt, kind="ExternalInput").ap()

    def dout(name, shape):
        return nc.dram_tensor(name, list(shape), F32, kind="ExternalOutput").ap()

    def dscr(name, shape, dt):
        return nc.dram_tensor(name, list(shape), dt, kind="Internal").ap()

    xp = din("xp", [T, D]); xs = din("xs", [TS, D])
    ck = din("ck", [T, H, DH]); cv = din("cv", [T, H, DH]); sg_in = din("sg", [4, 256, 512])
    c32 = din("c32", [32, 128]); wada = din("wada", [2, D, 3 * D]); bada = din("bada", [96, 128])
    brow = din("brow", [2, 3 * D]); ng = din("ng", [32, 128])
    wq = din("wq", [D, 4 * D]); wo0 = din("wo0", [D, D]); wg = din("wg", [D, 6160])
    wa2 = din("wa2", [16, 1024]); ba = din("ba", [1, 1024]); glag = din("glag", [1, D])
    wo1 = din("wo1", [D, D]); fing = din("fing", [1, D])
    cf_d = din("cf", [128, NCF]); cb_d = din("cb", [128, NCB], BF16)

    yp = dout("yp", [T, D]); ys = dout("ys", [TS, D]); kp = dout("kp", [T, D]); vp = dout("vp", [T, D])
    ks = dout("ks", [TS, D]); vs = dout("vs", [TS, D]); sp_o = dout("spo", [4, 256, 512]); ss_o = dout("sso", [4, 256, 512])

    wq_s = dscr("wq_s", [16, 128, 16, 512], BF16); wo0_s = dscr("wo0_s", [D, D], BF16)
    wg_s = dscr("wg_s", [12, 128, 16, 512], BF16); wta_s = dscr("wta_s", [128, 16, 16], BF16); wo1_s = dscr("wo1_s", [D, D], BF16)
    qT_s = dscr("qT_s", [H, DH, NT], BF16); kT_s = dscr("kT_s", [H, DH, NT], BF16)
    sgT_s = dscr("sgT_s", [H, DH, NT], BF16); v_s = dscr("v_s", [NT, D], BF16)
    ogT_s = dscr("ogT_s", [H, DH, NT], BF16)
    x1_s = dscr("x1_s", [NT, D], F32); h1T_s = dscr("h1T_s", [16, 128, NT], BF16)
    qinT_s = dscr("qinT_s", [8, 128, NT], BF16); kdT_s = dscr("kdT_s", [8, 128, NT], BF16)
    kd_s = dscr("kd_s", [NT, 1024], BF16); vg_s = dscr("vg_s", [NT, D], BF16); sr_s = dscr("sr_s", [NT, D], BF16)

    R_wq = [Res("wq%d" % i) for i in range(16)]
    R_wo0 = [Res("wo0") for i in range(16)]; R_wg = [Res("wg") for i in range(16)]; R_wo1 = [Res("wo1") for i in range(16)]
    R_scr = {n: Res(n) for n in ["qT", "kT", "sgT", "v", "ogT", "x1", "h1T", "qinT", "kdT", "kd", "vg", "sr"]}

    top = ExitStack()
    with top:
        cf = K.sb(top, [128, NCF], F32, "cf"); cb = K.sb(top, [128, NCB], BF16, "cb")
        identf = cf.t[:, 0:128]; maskST = cf.t[:, 128:256]; uincl = cf.t[:, 256:384]; maskLE = cf.t[:, 384:512]; ones_f = cf.t[:, 512:640]
        identb = cb.t[:, 0:128]; lmat = cb.t[:, 128:256]; ones_b = cb.t[:, 256:384]
        esL0 = ExitStack()
        gate_b = [None, [K.sb(top, [128, D], F32, "gate") for g in range(2)]]
        glag_b = K.sb(top, [128, D], F32, "glag"); fing_b = K.sb(top, [128, D], F32, "fing")
        shiftT = [K.sb(top, [128, 16, 2], F32, "shT") for l in range(2)]
        gsT = [K.sb(top, [128, 16, 2], F32, "gsT") for l in range(2)]
        dec_all = K.sb(top, [128, 33, 8], F32, "dec")
        junk = K.sb(top, [128, D], BF16, "junk")

        K.dma("sp", cf.t[:], cf_d[:, :], [], [cf.r], "cf")
        K.dma("sp", cb.t[:], cb_d[:, :], [], [cb.r], "cb")
        K.dma("sp", glag_b.t[:], glag[0, :].partition_broadcast(128), [], [glag_b.r], "glag")
        K.dma("sp", fing_b.t[:], fing[0, :].partition_broadcast(128), [], [fing_b.r], "fing")

        gate_b[0] = [K.sb(esL0, [128, D], F32, "gate0") for g in range(2)]
        with ExitStack() as es:
            c32_t = K.sb(es, [32, 128], F32); bada_t = K.sb(es, [96, 128], F32); ng_t = K.sb(es, [32, 128], F32)
            brow_b = K.sb(es, [1, 2 * 3 * D], BF16)
            cT = K.sb(es, [128, 32], F32); cTb2 = K.sb(es, [128, 16, 2], BF16)
            cB = [K.sb(es, [128, 16, 128], BF16) for g in range(2)]
            badaT = K.sb(es, [128, 96], F32); ngT = K.sb(es, [128, 32], F32)
            tmpA = K.sb(es, [128, 16, 2], F32)
            wa_ring = K.ring(es, 2, [128, 16, 512], BF16, "wa")
            waf_ring = K.ring(es, 2, [128, 16, 512], F32, "waf")
            tps = K.ps(es, [128, 512], F32, "tps")
            adaps = K.ps(es, [128, 256, 2], F32, "adaps")
            gps = K.ring(es, 2, [128, 512], F32, "gps", psum=True)

            K.dma("sp", c32_t.t[:], c32[:, :], [], [c32_t.r], "c32")
            K.dma("sp", bada_t.t[:], bada[:, :], [], [bada_t.r], "bada")
            K.dma("sp", ng_t.t[:], ng[:, :], [], [ng_t.r], "ng")
            K.dma("pool", brow_b.t[:], brow.rearrange("l n -> (l n)").rearrange("(o n) -> o n", o=1), [], [brow_b.r], "browb",
                  max_dma_last_dim=4096)
            wq_v = wq.rearrange("(kc p) n -> p kc n", p=128)
            for wc in range(16):
                K.dma("pool", wq_s[wc], wq_v[:, :, wc * 512:(wc + 1) * 512], [], [R_wq[wc]], "pcq", nobar=True)
            K.tr(tps.t[:, 0:32], c32_t.t[:, :], identf[0:32, 0:32], [c32_t.r, cf.r], [tps.r])
            K.cp("dve", cT.t[:], tps.t[:, 0:32], [tps.r], [cT.r])
            K.tr(tps.t[:, 0:96], bada_t.t[:, :], identf[0:96, 0:96], [bada_t.r, cf.r], [tps.r])
            K.cp("dve", badaT.t[:], tps.t[:, 0:96], [tps.r], [badaT.r])
            K.tr(tps.t[:, 0:32], ng_t.t[:, :], identf[0:32, 0:32], [ng_t.r, cf.r], [tps.r])
            K.cp("dve", ngT.t[:], tps.t[:, 0:32], [tps.r], [ngT.r])
            for g in range(2):
                K.cp("dve", cTb2.t[:, :, g], cT.t[:, g * 16:(g + 1) * 16], [cT.r], [cTb2.r])
                for kc in range(16):
                    K.ts("dve", cB[g].t[:, kc, :], ones_f, cT.t[:, g * 16 + kc:g * 16 + kc + 1], None, ALU.mult, None,
                         [cT.r, cf.r], [cB[g].r])
            wav = wada.rearrange("l (kc p) n -> l p kc n", p=128)
            wi = 0
            wa_r2 = [Res("wa2a"), Res("wa2b")]

            def ada_load(k):
                if k < 24:
                    K.dma("sp", waf_ring[k % 2].t[:], wav[k // 12, :, :, (k % 12) * 512:(k % 12 + 1) * 512], [], [waf_ring[k % 2].r], "waf%d" % (k % 2))

            ada_load(0)
            for l in range(2):
                for j in range(12):
                    wa = wa_ring[wi % 2]; waf = waf_ring[wi % 2]; wi += 1
                    ada_load(wi)
                    wa2r = wa_r2[(wi - 1) % 2]
                    K.cp("dve", wa.t[:, 0:8, :], waf.t[:, 0:8, :], [waf.r], [wa.r])
                    K.cp("act", wa.t[:, 8:16, :], waf.t[:, 8:16, :], [waf.r], [wa2r])
                    if j < 8:
                        for fi in range(4):
                            fc = j * 4 + fi
                            for kc in range(16):
                                K.mm(adaps.t[:, fc, :], wa.t[:, kc, fi * 128:(fi + 1) * 128], cTb2.t[:, kc, :],
                                     kc == 0, kc == 15, [wa.r, wa2r, cTb2.r], [adaps.r])
                    else:
                        for g in range(2):
                            gp = gps[g]
                            for kc in range(16):
                                K.mm(gp.t[:, :], cB[g].t[:, kc, :], wa.t[:, kc, :], kc == 0, False, [wa.r, wa2r, cB[g].r], [gp.r])
                            K.mm(gp.t[:, :], ones_b[0:1, :], brow_b.t[0:1, l * 6144 + j * 512: l * 6144 + (j + 1) * 512],
                                 False, True, [cb.r, brow_b.r], [gp.r])
                            K.cp("act", gate_b[l][g].t[:, (j - 8) * 512:(j - 7) * 512], gp.t[:, :], [gp.r], [gate_b[l][g].r])
                    if j == 7:
                        for g in range(2):
                            K.tt("dve", shiftT[l].t[:, :, g], adaps.t[:, 0:16, g], badaT.t[:, l * 48:l * 48 + 16], ALU.add,
                                 [adaps.r, badaT.r], [shiftT[l].r])
                            K.tt("dve", tmpA.t[:, :, g], adaps.t[:, 16:32, g], badaT.t[:, l * 48 + 16:l * 48 + 32], ALU.add,
                                 [adaps.r, badaT.r], [tmpA.r])
                            K.stt(gsT[l].t[:, :, g], tmpA.t[:, :, g], 1.0, ngT.t[:, l * 16:(l + 1) * 16], ALU.add, ALU.mult,
                                  [tmpA.r, ngT.r], [gsT[l].r])
            P.barrier()
        wg_v = wg.rearrange("(kc p) n -> p kc n", p=128)
        for (src, dst, rl) in [(wo0, wo0_s, R_wo0)]:
            for rb in range(16):
                K.dma("pool", dst[rb * 128:(rb + 1) * 128, :], src[rb * 128:(rb + 1) * 128, :], [], [rl[rb]], "pc", nobar=True,
                      max_dma_last_dim=8192)
        for wc in range(12):
            K.dma("pool", wg_s[wc], wg_v[:, :, wc * 512:(wc + 1) * 512], [], [R_wg[wc]], "pcg", nobar=True)
        K.dma("pool", wta_s[:, :, :], wg_v[:, :, 6144:6160], [], [R_wg[12]], "pcg", nobar=True)
        for (src, dst, rl) in [(wo1, wo1_s, R_wo1)]:
            for rb in range(16):
                K.dma("pool", dst[rb * 128:(rb + 1) * 128, :], src[rb * 128:(rb + 1) * 128, :], [], [rl[rb]], "pc", nobar=True,
                      max_dma_last_dim=8192)

        def norm_p1(xt_ap, xt_res, ntok, hb, ssb):
            K.ttr(junk.t[0:ntok, :], xt_ap, xt_ap, ssb.t[0:ntok, 0:1], [xt_res], [ssb.r])
            K.act(ssb.t[0:ntok, 1:2], ssb.t[0:ntok, 0:1], AF.Ln, [ssb.r], [ssb.r], bias=EPS, scale=1.0 / D)
            K.act(ssb.t[0:ntok, 2:3], ssb.t[0:ntok, 1:2], AF.Exp, [ssb.r], [ssb.r], scale=-0.5)
            K.ts("dve", hb.t[0:ntok, :], xt_ap, ssb.t[0:ntok, 2:3], None, ALU.mult, None, [xt_res, ssb.r], [hb.r])

        def norm_tile(xt_ap, xt_res, ntok, l, g, hb, ssb, tpr, tpi, hT_ap_fn, hT_res, evac_eng):
            norm_p1(xt_ap, xt_res, ntok, hb, ssb)
            norm_p2(ntok, l, g, hb, tpr, tpi, hT_ap_fn, hT_res, evac_eng)

        def norm_p2(ntok, l, g, hb, tpr, tpi, hT_ap_fn, hT_res, evac_eng):
            for grp in range(4):
                tp = tpr[tpi[0] % len(tpr)]; tpi[0] += 1
                for i in range(4):
                    kc = grp * 4 + i
                    K.tr(tp.t[:, i, 0:ntok], hb.t[0:ntok, kc * 128:(kc + 1) * 128], identb[0:ntok, 0:ntok], [hb.r, cb.r], [tp.r])
                for i in range(4):
                    kc = grp * 4 + i
                    if evac_eng == "dve":
                        K.ts("dve", hT_ap_fn(kc), tp.t[:, i, 0:ntok], gsT[l].t[:, kc, g:g + 1], shiftT[l].t[:, kc, g:g + 1],
                             ALU.mult, ALU.add, [tp.r, gsT[l].r, shiftT[l].r], [hT_res])
                    else:
                        K.act(hT_ap_fn(kc), tp.t[:, i, 0:ntok], AF.Identity, [tp.r, gsT[l].r, shiftT[l].r], [hT_res],
                              bias=shiftT[l].t[:, kc, g:g + 1], scale=gsT[l].t[:, kc, g:g + 1])

        def silu_from_psum(ps_ap, ps_res, n_p, n_f, out_ap, out_res, tmp):
            K.act(tmp.t[0:n_p, 0:n_f], ps_ap, AF.Exp, [ps_res], [tmp.r], scale=-1.0)
            K.ts("dve", tmp.t[0:n_p, 0:n_f], tmp.t[0:n_p, 0:n_f], 1.0, None, ALU.add, None, [tmp.r], [tmp.r])
            K.recip(tmp.t[0:n_p, 0:n_f], tmp.t[0:n_p, 0:n_f], [tmp.r], [tmp.r])
            K.tt("dve", out_ap, ps_ap, tmp.t[0:n_p, 0:n_f], ALU.mult, [ps_res, tmp.r], [out_res])

        blocks = [(bi * 512, 512, 0) for bi in range(NBLK)] + [(T, TS, 1)]

        def xrows(tok0, n):
            return xp[tok0:tok0 + n, :] if tok0 < T else xs[tok0 - T:tok0 - T + n, :]

        with ExitStack() as es:
          if STAGE >= 1:
                xt_r = K.ring(es, 4, [128, D], F32, "xt"); hb_r = K.ring(es, 4, [128, D], BF16, "hb")
                ss_r = K.ring(es, 4, [128, 4], F32, "ss")
                hT_r = K.ring(es, 2, [128, 16, 512], BF16, "hT")
                hT_res = [[Res("hTr") for t in range(4)] for s in range(2)]
                wt_r = K.ring(es, 3, [128, 16, 512], BF16, "wt")
                qst_r = K.ring(es, 2, [128, 512], BF16, "qst"); stmp_r = K.ring(es, 1, [128, 512], F32, "stmp")
                kf_r = K.ring(es, 2, [128, 512], F32, "kf"); kb_r = K.ring(es, 2, [128, 512], BF16, "kb")
                kTst_r = K.ring(es, 1, [128, 4, 512], BF16, "kTst")
                tp_r = K.ring(es, 2, [128, 8, 128], BF16, "tp", psum=True)
                ps_r = K.ring(es, 4, [128, 512], F32, "psA", psum=True)
                tp2_r = K.ring(es, 2, [128, 8, 128], BF16, "tp2", psum=True)
                tpi = [0]; ci = {"xt": 0, "wt": 0, "ps": 0, "st": 0, "kf": 0, "kT": 0, "tp2": 0}
                def a_xload(bi):
                    tok0, ntokb, g = blocks[bi]
                    for t in range((ntokb + 127) // 128):
                        ntok = min(128, ntokb - t * 128)
                        K.dma("sp", xt_r[t].t[0:ntok, :], xrows(tok0 + t * 128, ntok), [], [xt_r[t].r], "xt%d" % t)

                def a_p1(bi):
                    tok0, ntokb, g = blocks[bi]
                    for t in range((ntokb + 127) // 128):
                        ntok = min(128, ntokb - t * 128)
                        norm_p1(xt_r[t].t[0:ntok, :], xt_r[t].r, ntok, hb_r[t], ss_r[t])

                def a_p2(bi):
                    tok0, ntokb, g = blocks[bi]
                    hTb = hT_r[bi % 2]; hTres = hT_res[bi % 2]
                    for t in range((ntokb + 127) // 128):
                        ntok = min(128, ntokb - t * 128)
                        norm_p2(ntok, 0, g, hb_r[t], tp_r, tpi,
                                (lambda kc, hTb=hTb, ntok=ntok, t=t: hTb.t[:, kc, t * 128:t * 128 + ntok]), hTres[t], "dve" if t % 2 == 0 else "act")

                steps = [(bi, wc) for bi in range(len(blocks)) for wc in range(16)]
                kpend = [None]

                def a_wload(si):
                    if si < len(steps):
                        wc = steps[si][1]
                        K.dma("sp", wt_r[si % 3].t[:], wq_s[wc], [R_wq[wc]], [wt_r[si % 3].r], "wt%d" % (si % 3))

                a_xload(0); a_p1(0); a_p2(0)
                a_wload(0); a_wload(1)
                for bi, (tok0, ntokb, g) in enumerate(blocks):
                    ntl = (ntokb + 127) // 128
                    hTb = hT_r[bi % 2]; hTres = hT_res[bi % 2]
                    hres = [hTres[t] for t in range(ntl)]
                    for wc in range(16):
                        si = bi * 16 + wc
                        wt = wt_r[si % 3]
                        a_wload(si + 2)
                        if bi + 1 < len(blocks):
                            if wc == 2:
                                a_xload(bi + 1)
                            if wc == 5:
                                a_p1(bi + 1)
                            if wc == 9:
                                a_p2(bi + 1)
                        kind = wc // 4; hbase = (wc % 4) * 4
                        if kind in (0, 3):
                            for hh in range(4):
                                ps = ps_r[ci["ps"] % 4]; ci["ps"] += 1
                                for kc in range(16):
                                    K.mm(ps.t[:, 0:ntokb], wt.t[:, kc, hh * 128:(hh + 1) * 128], hTb.t[:, kc, 0:ntokb],
                                         kc == 0, kc == 15, [wt.r] + hres, [ps.r])
                                qst = qst_r[ci["st"] % 2]; stmp = stmp_r[0]; ci["st"] += 1
                                if kind == 0:
                                    K.act(qst.t[:, 0:ntokb], ps.t[:, 0:ntokb], AF.Copy, [ps.r], [qst.r], scale=DH ** -0.5)
                                    K.dma("sp", qT_s[hbase + hh, :, tok0:tok0 + ntokb], qst.t[:, 0:ntokb], [qst.r], [],
                                          "qst%d" % (ci["st"] % 2))
                                else:
                                    silu_from_psum(ps.t[:, 0:ntokb], ps.r, 128, ntokb, qst.t[:, 0:ntokb], qst.r, stmp)
                                    K.dma("sp", sgT_s[hbase + hh, :, tok0:tok0 + ntokb], qst.t[:, 0:ntokb], [qst.r], [],
                                          "qst%d" % (ci["st"] % 2))
                        else:
                            kTst = kTst_r[0]; ci["kT"] += 1
                            for t in range(ntl):
                                ntok = min(128, ntokb - t * 128)
                                ps = ps_r[ci["ps"] % 4]; ci["ps"] += 1
                                for kc in range(16):
                                    K.mm(ps.t[0:ntok, :], hTb.t[:, kc, t * 128:t * 128 + ntok], wt.t[:, kc, :], kc == 0, kc == 15, [wt.r, hTres[t]], [ps.r])
                                kf = kf_r[ci["kf"] % 2]; kb = kb_r[ci["kf"] % 2]; ci["kf"] += 1
                                K.cp("act", kf.t[0:ntok, :], ps.t[0:ntok, :], [ps.r], [kf.r])
                                if tok0 < T:
                                    dst = (kp if kind == 1 else vp)[tok0 + t * 128:tok0 + t * 128 + ntok, hbase * 128:hbase * 128 + 512]
                                else:
                                    dst = (ks if kind == 1 else vs)[0:ntok, hbase * 128:hbase * 128 + 512]
                                K.dma("sp", dst, kf.t[0:ntok, :], [kf.r], [Res()], "kf%d" % (ci["kf"] % 2))
                                K.cp("dve", kb.t[0:ntok, :], ps.t[0:ntok, :], [ps.r], [kb.r])
                                if kind == 1:
                                    def _ktr(kb=kb, ntok=ntok, t=t, kTst=kTst):
                                        tp2 = tp2_r[ci["tp2"] % 2]; ci["tp2"] += 1
                                        for hh in range(4):
                                            K.tr(tp2.t[:, hh, 0:ntok], kb.t[0:ntok, hh * 128:(hh + 1) * 128], identb[0:ntok, 0:ntok],
                                                 [kb.r, cb.r], [tp2.r])
                                        K.cp("act" if t % 2 else "dve", kTst.t[:, :, t * 128:t * 128 + ntok], tp2.t[:, 0:4, 0:ntok], [tp2.r], [kTst.r])
                                    if kpend[0] is not None:
                                        kpend[0]()
                                    kpend[0] = _ktr
                                else:
                                    K.dma("sp", v_s[tok0 + t * 128:tok0 + t * 128 + ntok, hbase * 128:hbase * 128 + 512], kb.t[0:ntok, :],
                                          [kb.r], [], "kb%d" % (ci["kf"] % 2))
                            if kind == 1:
                                kpend[0](); kpend[0] = None
                                K.dma("sp", kT_s[hbase:hbase + 4, :, tok0:tok0 + ntokb].rearrange("h d t -> d h t"), kTst.t[:, :, 0:ntokb],
                                      [kTst.r], [], "kTst0")
                P.barrier()

        if STAGE >= 2:
            with ExitStack() as es:
                qh_r = K.ring(es, 2, [128, NT], BF16, "qh"); kh_r = K.ring(es, 2, [128, NT], BF16, "kh")
                vh_r = K.ring(es, 2, [128, 33, 128], BF16, "vh"); sgh_r = K.ring(es, 2, [128, NT], BF16, "sgh")
                e_r = K.ring(es, 6, [128, 512], F32, "e"); spb_r = K.ring(es, 3, [128, 512], BF16, "spb")
                R_r = K.ring(es, 4, [128, 512], BF16, "Rr"); C_r = K.ring(es, 3, [128, 512], BF16, "C")
                a_r = K.ring(es, 3, [128, 512], BF16, "a"); og_r = K.ring(es, 2, [128, 512], BF16, "og")
                kcf = K.sb(es, [128, 32, 128], F32, "kcf"); vcf = K.sb(es, [128, 32, 128], F32, "vcf")
                kcb = K.sb(es, [128, 32, 128], BF16, "kcb"); vcb = K.sb(es, [128, 32, 128], BF16, "vcb")
                kcT = K.sb(es, [128, T], BF16, "kcT")
                z_r = K.ring(es, 3, [128, 512], F32, "z", psum=True); cum_r = K.ring(es, 2, [128, 512], F32, "cum", psum=True)
                o_r = K.ring(es, 2, [128, 512], F32, "o", psum=True); tpB_r = K.ring(es, 1, [128, 8, 128], BF16, "tpB", psum=True)
                ctr = {"blk": 0, "qb": 0, "tp": 0}

                def head_loads(h):
                    qh = qh_r[h % 2]; kh = kh_r[h % 2]; vh = vh_r[h % 2]; sgh = sgh_r[h % 2]
                    K.dma("sp", qh.t[:], qT_s[h], [], [qh.r], "qh%d" % (h % 2))
                    K.dma("sp", kh.t[:], kT_s[h], [], [kh.r], "kh%d" % (h % 2))
                    K.dma("sp", vh.t[:, 0:32, :], v_s[0:T, h * 128:(h + 1) * 128].rearrange("(t p) d -> p t d", p=128),
                          [], [vh.r], "vh%d" % (h % 2))
                    K.dma("sp", vh.t[0:TS, 32, :], v_s[T:NT, h * 128:(h + 1) * 128], [], [vh.r], "vh%d" % (h % 2))
                    K.dma("sp", sgh.t[:], sgT_s[h], [], [sgh.r], "sgh%d" % (h % 2))

                def cache_prep(h):
                    K.dma("sp", kcf.t[:], ck[:, h, :].rearrange("(t p) d -> p t d", p=128), [], [kcf.r], "kcf")
                    K.dma("sp", vcf.t[:], cv[:, h, :].rearrange("(t p) d -> p t d", p=128), [], [vcf.r], "vcf")
                    K.cp("pool", kcb.t[:], kcf.t[:], [kcf.r], [kcb.r])
                    K.cp("pool", vcb.t[:], vcf.t[:], [vcf.r], [vcb.r])
                    for gq in range(8):
                        tp = tpB_r[0]
                        for i4 in range(4):
                            K.tr(tp.t[:, i4, :], kcb.t[:, gq * 4 + i4, :], identb, [kcb.r, cb.r], [tp.r])
                        K.cp("dve", kcT.t[:, gq * 512:(gq + 1) * 512], tp.t[:, 0:4, :], [tp.r], [kcT.r])

                items = []
                for h in range(H):
                    qh = qh_r[h % 2]; kh = kh_r[h % 2]; vh = vh_r[h % 2]; sgh = sgh_r[h % 2]
                    qbs = []
                    for i in range(NBLK):
                        q0 = i * 512
                        kl = []
                        for kb in range(4 * i + 3, -1, -1):
                            m = kb - 4 * i
                            off = 128 * m if m > 0 else 0
                            kl.append((kh.t[:, kb * 128:(kb + 1) * 128], vh.t[:, kb, :], [kh.r, vh.r], 128, off, m >= 0))
                        qbs.append((q0, 512, kl, ogT_s[h, :, q0:q0 + 512]))
                    kl = [(kh.t[:, T:NT], vh.t[0:TS, 32, :], [kh.r, vh.r], TS, 0, True)]
                    for kb in range(31, -1, -1):
                        kl.append((kcT.t[:, kb * 128:(kb + 1) * 128], vcb.t[:, kb, :], [kcT.r, vcb.r], 128, 0, False))
                    qbs.append((T, TS, kl, ogT_s[h, :, T:NT]))
                    for qi, (q0, nq, kl, out_dram) in enumerate(qbs):
                        qb = {"q0": q0, "nq": nq, "out": out_dram, "qh": qh, "sgh": sgh, "n": len(kl), "slot": None}
                        for idx, (kT_ap, v_ap, rds, nk, off, diag) in enumerate(kl):
                            items.append({"qb": qb, "idx": idx, "kT": kT_ap, "v": v_ap, "rds": rds, "nk": nk, "off": off, "diag": diag,
                                          "hstart": h if (qi == 0 and idx == 0) else None})

                def s0(it):
                    qb = it["qb"]; nq = qb["nq"]; q0 = qb["q0"]; off = it["off"]; nk = it["nk"]; n = nq - off
                    if it["idx"] == 0:
                        s = ctr["qb"] % 2; ctr["qb"] += 1
                        qb["R"] = [R_r[2 * s], R_r[2 * s + 1]]; qb["o"] = o_r[s]; qb["og"] = og_r[s]; qb["s"] = s
                        K.memset("pool", qb["R"][0].t[:, 0:nq], 0.0, [qb["R"][0].r])
                        K.memset("pool", qb["R"][1].t[:, 0:nq], 0.0, [qb["R"][1].r])
                    j = ctr["blk"]; ctr["blk"] += 1
                    it["z"] = z_r[j % 3]; it["e"] = e_r[j % 6]; it["sp"] = spb_r[j % 3]; it["cum"] = cum_r[j % 2]
                    it["C"] = C_r[j % 3]; it["a"] = a_r[j % 3]
                    zb = it["z"]
                    K.mm(zb.t[0:nk, 0:n], it["kT"], qb["qh"].t[:, q0 + off:q0 + nq], True, True, it["rds"] + [qb["qh"].r], [zb.r])

                def s1(it):
                    qb = it["qb"]; nq = qb["nq"]; off = it["off"]; nk = it["nk"]; n = nq - off
                    zb = it["z"]; eb = it["e"]
                    K.act(eb.t[0:nk, 0:n], zb.t[0:nk, 0:n], AF.Exp, [zb.r], [eb.r])
                    if it["diag"]:
                        w = min(128, n)
                        K.tt("dve", eb.t[0:nk, 0:w], eb.t[0:nk, 0:w], maskST[0:nk, 0:w], ALU.mult, [eb.r, cf.r], [eb.r])

                def s2(it):
                    qb = it["qb"]; nq = qb["nq"]; off = it["off"]; nk = it["nk"]; n = nq - off
                    K.act(it["sp"].t[0:nk, 0:n], it["e"].t[0:nk, 0:n], AF.Ln, [it["e"].r], [it["sp"].r], bias=1.0)

                def s3(it):
                    qb = it["qb"]; nq = qb["nq"]; off = it["off"]; nk = it["nk"]; n = nq - off
                    sb_ = it["sp"]; cb_ = it["cum"]; Rb = qb["R"][it["idx"] % 2]; Rn = qb["R"][(it["idx"] + 1) % 2]
                    K.mm(cb_.t[0:nk, 0:n], lmat[0:nk, 0:nk], sb_.t[0:nk, 0:n], True, False, [cb.r, sb_.r], [cb_.r])
                    K.mm(cb_.t[0:nk, 0:n], ones_b[:, 0:nk], Rb.t[:, off:nq], False, True, [cb.r, Rb.r], [cb_.r])
                    if it["idx"] < qb["n"] - 1:
                        if nk < 128:
                            K.tt("pool", Rn.t[0:nk, off:nq], Rb.t[0:nk, off:nq], sb_.t[0:nk, 0:n], ALU.add, [Rb.r, sb_.r], [Rn.r])
                        else:
                            K.tt("pool", Rn.t[:, off:nq], Rb.t[:, off:nq], sb_.t[:, 0:n], ALU.add, [Rb.r, sb_.r], [Rn.r])

                def s4(it):
                    qb = it["qb"]; nq = qb["nq"]; off = it["off"]; nk = it["nk"]; n = nq - off
                    K.act(it["C"].t[0:nk, 0:n], it["cum"].t[0:nk, 0:n], AF.Exp, [it["cum"].r], [it["C"].r], scale=-1.0)

                def s5(it):
                    qb = it["qb"]; nq = qb["nq"]; off = it["off"]; nk = it["nk"]; n = nq - off
                    eb = it["e"]; Cb = it["C"]; ab = it["a"]
                    if it["idx"] == 0 and off > 0:
                        K.memset("pool", ab.t[0:nk, 0:off], 0.0, [ab.r])
                        K.tt("dve", ab.t[0:nk, off:nq], eb.t[0:nk, 0:n], Cb.t[0:nk, 0:n], ALU.mult, [eb.r, Cb.r], [ab.r])
                    else:
                        K.tt("dve", ab.t[0:nk, 0:n], eb.t[0:nk, 0:n], Cb.t[0:nk, 0:n], ALU.mult, [eb.r, Cb.r], [ab.r])

                def s6(it):
                    qb = it["qb"]; nq = qb["nq"]; off = it["off"]; nk = it["nk"]; n = nq - off
                    ab = it["a"]; ob = qb["o"]; idx = it["idx"]; nblk = qb["n"]
                    if idx == 0 and off > 0:
                        K.mm(ob.t[:, 0:nq], it["v"], ab.t[0:nk, 0:nq], True, nblk == 1, it["rds"] + [ab.r], [ob.r])
                    else:
                        K.mm(ob.t[:, off:nq], it["v"], ab.t[0:nk, 0:n], idx == 0, idx == nblk - 1, it["rds"] + [ab.r], [ob.r])

                def s7(it):
                    qb = it["qb"]; nq = qb["nq"]; q0 = qb["q0"]
                    if it["idx"] == qb["n"] - 1:
                        ogb = qb["og"]; ob = qb["o"]
                        K.tt("dve", ogb.t[:, 0:nq], ob.t[:, 0:nq], qb["sgh"].t[:, q0:q0 + nq], ALU.mult, [ob.r, qb["sgh"].r], [ogb.r])
                        K.dma("sp", qb["out"], ogb.t[:, 0:nq], [ogb.r], [], "og%d" % qb["s"])

                stages = [s0, s1, s2, s3, s4, s5, s6, s7]
                NS = len(stages)
                head_loads(0)
                nit = len(items)
                for i in range(nit + NS):
                    k = i - NS
                    if 0 <= k < nit and items[k]["hstart"] is not None:
                        hh_ = items[k]["hstart"]
                        if hh_ + 1 < H:
                            head_loads(hh_ + 1)
                        cache_prep(hh_)
                    for si, fn in enumerate(stages):
                        if 0 <= i - si < nit:
                            fn(items[i - si])
                P.barrier()

        if STAGE >= 3:
            with ExitStack() as es:
                wo = K.sb(es, [128, 16, D], BF16, "wo")
                ogb_r = K.ring(es, 2, [128, 16, 512], BF16, "ogblk")
                xt_r = K.ring(es, 2, [128, D], F32, "xtC"); hb_r = K.ring(es, 2, [128, D], BF16, "hbC")
                ss_r = K.ring(es, 2, [128, 4], F32, "ssC"); tmp_r = K.ring(es, 2, [128, 512], F32, "tmpC")
                h1st_r = K.ring(es, 2, [128, 16, 128], BF16, "h1st")
                y_r = K.ring(es, 4, [128, 512], F32, "yC", psum=True)
                tp_r = K.ring(es, 2, [128, 8, 128], BF16, "tpC", psum=True)
                tpi = [0]; ci = {"x": 0, "y": 0, "t": 0}
                K.dma("sp", wo.t[:], wo0_s.rearrange("(kc p) n -> p kc n", p=128), R_wo0, [wo.r], "woC")
                ctiles = [(tok0 + t * 128, min(128, ntokb - t * 128)) for (tok0, ntokb, g) in blocks for t in range((ntokb + 127) // 128)]

                def c_xload(k):
                    if k < len(ctiles):
                        K.dma("sp", xt_r[k % 2].t[0:ctiles[k][1], :], xrows(ctiles[k][0], ctiles[k][1]), [], [xt_r[k % 2].r], "xtC%d" % (k % 2))

                def c_ogload(bi):
                    if bi < len(blocks):
                        tok0_, ntokb_, g_ = blocks[bi]
                        K.dma("sp", ogb_r[bi % 2].t[:, :, 0:ntokb_], ogT_s[:, :, tok0_:tok0_ + ntokb_].rearrange("h d t -> d h t"), [],
                              [ogb_r[bi % 2].r], "ogblk%d" % (bi % 2))

                c_ogload(0); c_xload(0)
                pend = [None]
                for bi, (tok0, ntokb, g) in enumerate(blocks):
                    ogb = ogb_r[bi % 2]
                    c_ogload(bi + 1)
                    ntl = (ntokb + 127) // 128
                    for t in range(ntl):
                        ntok = min(128, ntokb - t * 128)
                        s = ci["x"] % 2; ci["x"] += 1
                        xt = xt_r[s]; hb = hb_r[s]; ssb = ss_r[s]; h1st = h1st_r[s]
                        c_xload(ci["x"])
                        for n in range(4):
                            y = y_r[ci["y"] % 4]; ci["y"] += 1
                            tmp = tmp_r[ci["t"] % 2]; ci["t"] += 1
                            for kc in range(16):
                                K.mm(y.t[0:ntok, :], ogb.t[:, kc, t * 128:t * 128 + ntok], wo.t[:, kc, n * 512:(n + 1) * 512], kc == 0, kc == 15,
                                     [ogb.r, wo.r], [y.r])
                            K.tt("dve", tmp.t[0:ntok, :], y.t[0:ntok, :], gate_b[0][g].t[0:ntok, n * 512:(n + 1) * 512], ALU.mult,
                                 [y.r, gate_b[0][g].r], [tmp.r])
                            K.tt("pool", xt.t[0:ntok, n * 512:(n + 1) * 512], tmp.t[0:ntok, :], xt.t[0:ntok, n * 512:(n + 1) * 512], ALU.add,
                                 [tmp.r, xt.r], [xt.r])
                        K.dma("sp", x1_s[tok0 + t * 128:tok0 + t * 128 + ntok, :], xt.t[0:ntok, :], [xt.r], [], "x1o%d" % s)
                        norm_p1(xt.t[0:ntok, :], xt.r, ntok, hb, ssb)
                        if pend[0] is not None:
                            pend[0]()
                        def _p2(ntok=ntok, g=g, hb=hb, h1st=h1st, t=t, tok0=tok0, s=s):
                            norm_p2(ntok, 1, g, hb, tp_r, tpi, (lambda kc: h1st.t[:, kc, 0:ntok]), h1st.r, "dve" if t % 2 == 0 else "act")
                            K.dma("sp", h1T_s[:, :, tok0 + t * 128:tok0 + t * 128 + ntok].rearrange("c p t -> p c t"), h1st.t[:, :, 0:ntok],
                                  [h1st.r], [], "h1o%d" % s)
                        pend[0] = _p2
                pend[0]()
                P.barrier()

        esL0.close()
        if STAGE >= 4:
            with ExitStack() as es:
                h1b_r = K.ring(es, 2, [128, 16, 512], BF16, "h1b")
                wa2_b = K.sb(es, [16, 1024], BF16, "wa2b"); ba_b = K.sb(es, [1, 1024], BF16, "bab")
                K.dma("pool", wa2_b.t[:], wa2[:, :], [], [wa2_b.r], "wa2")
                K.dma("pool", ba_b.t[:], ba[:, :], [], [ba_b.r], "bab")
                wt_r = K.ring(es, 3, [128, 16, 512], BF16, "wtD")
                wta = K.sb(es, [128, 16, 16], BF16, "wta")
                alrT = K.sb(es, [16, 512], BF16, "alrT")
                eg_r = K.ring(es, 2, [128, 1024], F32, "eg")
                EbT = K.sb(es, [128, 8, 512], F32, "EbT"); EnbT = K.sb(es, [128, 8, 512], F32, "EnbT")
                st_r = K.ring(es, 2, [128, 512], BF16, "stD"); stmp_r = K.ring(es, 2, [128, 512], F32, "stmpD")
                kdst_r = K.ring(es, 2, [128, 4, 1024], BF16, "kdst")
                ps_r = K.ring(es, 4, [128, 512], F32, "psD", psum=True)
                bT_r = K.ring(es, 2, [128, 4, 128], F32, "bT", psum=True)
                tp_r = K.ring(es, 2, [128, 8, 128], BF16, "tpD", psum=True)
                ci = {"wt": 0, "ps": 0, "st": 0, "tp": 0, "eg": 0, "bT": 0}
                K.dma("sp", wta.t[:], wta_s[:, :, :], [R_wg[12]], [wta.r], "wta")
                tile_idx = 0
                def d_hload(bi):
                    if bi < len(blocks):
                        tok0_, ntokb_, g_ = blocks[bi]
                        K.dma("sp", h1b_r[bi % 2].t[:, :, 0:ntokb_], h1T_s[:, :, tok0_:tok0_ + ntokb_].rearrange("c p t -> p c t"), [],
                              [h1b_r[bi % 2].r], "h1b%d" % (bi % 2))

                def d_wload(si):
                    if si < 12 * len(blocks):
                        wc_ = si % 12
                        K.dma("sp", wt_r[si % 3].t[:], wg_s[wc_], [R_wg[wc_]], [wt_r[si % 3].r], "wtD%d" % (si % 3))

                d_hload(0); d_wload(0); d_wload(1)
                for bi, (tok0, ntokb, g) in enumerate(blocks):
                    h1b = h1b_r[bi % 2]
                    d_hload(bi + 1)
                    ntl = (ntokb + 127) // 128
                    ps = ps_r[ci["ps"] % 4]; ci["ps"] += 1
                    for kc in range(16):
                        K.mm(ps.t[0:16, 0:ntokb], wta.t[:, kc, :], h1b.t[:, kc, 0:ntokb], kc == 0, kc == 15, [wta.r, h1b.r], [ps.r])
                    K.cp("act", alrT.t[:, 0:ntokb], ps.t[0:16, 0:ntokb], [ps.r], [alrT.r])
                    for t in range(ntl):
                        ntok = min(128, ntokb - t * 128)
                        eg = eg_r[ci["eg"] % 2]; ci["eg"] += 1
                        for n in range(2):
                            ps = ps_r[ci["ps"] % 4]; ci["ps"] += 1
                            K.mm(ps.t[0:ntok, :], alrT.t[0:16, t * 128:t * 128 + ntok], wa2_b.t[0:16, n * 512:(n + 1) * 512], True, False,
                                 [alrT.r, wa2_b.r], [ps.r])
                            K.mm(ps.t[0:ntok, :], ones_b[0:1, 0:ntok], ba_b.t[0:1, n * 512:(n + 1) * 512], False, True, [cb.r, ba_b.r], [ps.r])
                            K.act(eg.t[0:ntok, n * 512:(n + 1) * 512], ps.t[0:ntok, :], AF.Exp, [ps.r], [eg.r], scale=-1.0)
                        K.act(eg.t[0:ntok, :], eg.t[0:ntok, :], AF.Ln, [eg.r], [eg.r], bias=1.0)
                        for half in range(2):
                            bT = bT_r[ci["bT"] % 2]; ci["bT"] += 1
                            for i4 in range(4):
                                dc = half * 4 + i4
                                K.mm(bT.t[:, i4, 0:ntok], eg.t[0:ntok, dc * 128:(dc + 1) * 128], uincl[0:ntok, 0:ntok], True, True,
                                     [eg.r, cf.r], [bT.r])
                            K.act(EbT.t[:, half * 4:half * 4 + 4, t * 128:t * 128 + ntok], bT.t[:, :, 0:ntok], AF.Exp, [bT.r], [EbT.r])
                            K.act(EnbT.t[:, half * 4:half * 4 + 4, t * 128:t * 128 + ntok], bT.t[:, :, 0:ntok], AF.Exp, [bT.r], [EnbT.r], scale=-1.0)
                        K.cp("dve", dec_all.t[:, tile_idx, :], EbT.t[:, :, t * 128 + ntok - 1], [EbT.r], [dec_all.r])
                        tile_idx += 1
                    kdst = kdst_r[bi % 2]
                    for wc in range(12):
                        si = bi * 12 + wc
                        wt = wt_r[si % 3]
                        d_wload(si + 2)
                        if wc < 4:
                            for i4 in range(4):
                                dc = (wc % 2) * 4 + i4
                                ps = ps_r[ci["ps"] % 4]; ci["ps"] += 1
                                for kc in range(16):
                                    K.mm(ps.t[:, 0:ntokb], wt.t[:, kc, i4 * 128:(i4 + 1) * 128], h1b.t[:, kc, 0:ntokb],
                                         kc == 0, kc == 15, [wt.r, h1b.r], [ps.r])
                                st = st_r[ci["st"] % 2]; ci["st"] += 1
                                if wc < 2:
                                    K.stt(st.t[:, 0:ntokb], ps.t[:, 0:ntokb], 256 ** -0.5, EbT.t[:, dc, 0:ntokb], ALU.mult, ALU.mult,
                                          [ps.r, EbT.r], [st.r])
                                    K.dma("sp", qinT_s[dc, :, tok0:tok0 + ntokb], st.t[:, 0:ntokb], [st.r], [], "stD%d" % (ci["st"] % 2))
                                else:
                                    K.tt("dve", st.t[:, 0:ntokb], ps.t[:, 0:ntokb], EnbT.t[:, dc, 0:ntokb], ALU.mult, [ps.r, EnbT.r], [st.r])
                                    K.dma("sp", kdT_s[dc, :, tok0:tok0 + ntokb], st.t[:, 0:ntokb], [st.r], [], "stD%d" % (ci["st"] % 2))
                                    for t in range(ntl):
                                        ntok = min(128, ntokb - t * 128)
                                        tp = tp_r[ci["tp"] % 2]; ci["tp"] += 1
                                        K.tr(tp.t[0:ntok, 0, :], st.t[:, t * 128:t * 128 + ntok], identb, [st.r, cb.r], [tp.r])
                                        K.cp("act", kdst.t[0:ntok, t, dc * 128:(dc + 1) * 128], tp.t[0:ntok, 0, :], [tp.r], [kdst.r])
                            if wc == 3:
                                for t in range(ntl):
                                    ntok = min(128, ntokb - t * 128)
                                    K.dma("sp", kd_s[tok0 + t * 128:tok0 + t * 128 + ntok, :], kdst.t[0:ntok, t, :], [kdst.r], [],
                                          "kdst%d" % (bi % 2))
                        else:
                            isv = wc < 8
                            col0 = ((wc - 4) % 4) * 512
                            for t in range(ntl):
                                ntok = min(128, ntokb - t * 128)
                                ps = ps_r[ci["ps"] % 4]; ci["ps"] += 1
                                for kc in range(16):
                                    K.mm(ps.t[0:ntok, :], h1b.t[:, kc, t * 128:t * 128 + ntok], wt.t[:, kc, :], kc == 0, kc == 15, [wt.r, h1b.r], [ps.r])
                                st = st_r[ci["st"] % 2]; stmp = stmp_r[ci["st"] % 2]; ci["st"] += 1
                                if isv:
                                    K.cp("act", st.t[0:ntok, :], ps.t[0:ntok, :], [ps.r], [st.r])
                                    K.dma("sp", vg_s[tok0 + t * 128:tok0 + t * 128 + ntok, col0:col0 + 512], st.t[0:ntok, :], [st.r], [],
                                          "stD%d" % (ci["st"] % 2))
                                else:
                                    silu_from_psum(ps.t[0:ntok, :], ps.r, ntok, 512, st.t[0:ntok, :], st.r, stmp)
                                    K.dma("sp", sr_s[tok0 + t * 128:tok0 + t * 128 + ntok, col0:col0 + 512], st.t[0:ntok, :], [st.r], [],
                                          "stD%d" % (ci["st"] % 2))
                P.barrier()

            with ExitStack() as es:
                wo = K.sb(es, [128, 16, D], BF16, "wo1")
                S = K.sb(es, [128, 8, 512], F32, "S"); Sb = K.sb(es, [128, 8, 512], BF16, "Sb")
                qin_r = K.ring(es, 2, [128, 8, 128], BF16, "qin"); kdT_r = K.ring(es, 2, [128, 8, 128], BF16, "kdT")
                kd_r = K.ring(es, 2, [128, 1024], BF16, "kd"); vg_r = K.ring(es, 2, [128, D], BF16, "vg")
                sr_r = K.ring(es, 1, [128, D], BF16, "sr"); x1_r = K.ring(es, 3, [128, D], F32, "x1")
                gsr_r = K.ring(es, 1, [128, D], F32, "gsr"); aT_r = K.ring(es, 4, [128, 128], BF16, "aT")
                osb_r = K.ring(es, 2, [128, 512], F32, "osb"); ss_r = K.ring(es, 8, [128, 4], F32, "ssD")
                og1_r = K.ring(es, 2, [128, D], BF16, "og1"); og1T_r = K.ring(es, 1, [128, 16, 128], BF16, "og1T")
                tmp_r = K.ring(es, 2, [128, 512], F32, "tmpE")
                aps_r = K.ring(es, 1, [128, 512], F32, "aps", psum=True)
                ops_r = K.ring(es, 2, [128, 512], F32, "ops", psum=True)
                sps_r = K.ring(es, 2, [128, 512], F32, "sps", psum=True)
                tp_r = K.ring(es, 1, [128, 8, 128], BF16, "tpE", psum=True)
                y_r = K.ring(es, 2, [128, 512], F32, "yE", psum=True)
                ci = {"ops": 0, "sps": 0, "ss": 0, "y": 0, "t": 0, "osb": 0, "aT": 0}
                S_res = [Res("S%d" % i) for i in range(8)]; Sb_res = [Res("Sb%d" % i) for i in range(8)]
                K.dma("sp", wo.t[:], wo1_s.rearrange("(kc p) n -> p kc n", p=128), R_wo1, [wo.r], "woE")
                K.memset("pool", S.t[:], 0.0, S_res)
                K.memset("pool", Sb.t[:], 0.0, Sb_res)
                tiles = [(t * 128, 128, 0) for t in range(NBLK * 4)] + [(T, TS, 1)]

                def e_loads(k):
                    if k >= len(tiles):
                        return
                    tok0_, ntok_, g_ = tiles[k]
                    s_ = k % 2
                    K.dma("sp", qin_r[s_].t[:, :, 0:ntok_], qinT_s[:, :, tok0_:tok0_ + ntok_].rearrange("c p t -> p c t"), [], [qin_r[s_].r], "qin%d" % s_)
                    K.dma("sp", kdT_r[s_].t[:, :, 0:ntok_], kdT_s[:, :, tok0_:tok0_ + ntok_].rearrange("c p t -> p c t"), [], [kdT_r[s_].r], "kdT%d" % s_)
                    K.dma("sp", kd_r[s_].t[0:ntok_, :], kd_s[tok0_:tok0_ + ntok_, :], [], [kd_r[s_].r], "kd%d" % s_)
                    K.dma("sp", vg_r[s_].t[0:ntok_, :], vg_s[tok0_:tok0_ + ntok_, :], [], [vg_r[s_].r], "vg%d" % s_)
                    K.dma("sp", x1_r[k % 3].t[0:ntok_, :], x1_s[tok0_:tok0_ + ntok_, :], [], [x1_r[k % 3].r], "x1i%d" % (k % 3))

                def e_srload(k):
                    if k < len(tiles):
                        tok0_, ntok_, g_ = tiles[k]
                        K.dma("sp", sr_r[0].t[0:ntok_, :], sr_s[tok0_:tok0_ + ntok_, :], [], [sr_r[0].r], "sr0")

                def e_xyz(ti):
                    tok0, ntok, g = tiles[ti]
                    s = ti % 2
                    qin = qin_r[s]; kdT = kdT_r[s]; kd = kd_r[s]; vg = vg_r[s]; sr = sr_r[0]; gsr = gsr_r[0]; og1 = og1_r[s]
                    K.tt("pool", gsr.t[0:ntok, :], sr.t[0:ntok, :], glag_b.t[0:ntok, :], ALU.mult, [sr.r, glag_b.r], [gsr.r])
                    e_srload(ti + 1)
                    aTs = []
                    for h in range(4):
                        aps = aps_r[0]; aT = aT_r[h]
                        for dc in range(2):
                            K.mm(aps.t[0:ntok, 0:ntok], kdT.t[:, h * 2 + dc, 0:ntok], qin.t[:, h * 2 + dc, 0:ntok], dc == 0, dc == 1,
                                 [kdT.r, qin.r], [aps.r])
                        K.tt("dve", aT.t[0:ntok, 0:ntok], aps.t[0:ntok, 0:ntok], maskLE[0:ntok, 0:ntok], ALU.mult, [aps.r, cf.r], [aT.r])
                        aTs.append(aT)
                    for h in range(4):
                        aT = aTs[h]
                        ops = ops_r[ci["ops"] % 2]; ci["ops"] += 1
                        K.mm(ops.t[0:ntok, :], aT.t[0:ntok, 0:ntok], vg.t[0:ntok, h * 512:(h + 1) * 512], True, False, [aT.r, vg.r], [ops.r])
                        for dc in range(2):
                            K.mm(ops.t[0:ntok, :], qin.t[:, h * 2 + dc, 0:ntok], Sb.t[:, h * 2 + dc, :], False, dc == 1,
                                 [qin.r, Sb_res[h * 2 + dc]], [ops.r])
                        osb = osb_r[ci["osb"] % 2]; ci["osb"] += 1
                        ssb = ss_r[ci["ss"] % 8]; ci["ss"] += 1
                        K.cp("act", osb.t[0:ntok, :], ops.t[0:ntok, :], [ops.r], [osb.r])
                        K.ttr(junk.t[0:ntok, 0:512], osb.t[0:ntok, :], osb.t[0:ntok, :], ssb.t[0:ntok, 0:1], [osb.r], [ssb.r])
                        K.act(ssb.t[0:ntok, 1:2], ssb.t[0:ntok, 0:1], AF.Ln, [ssb.r], [ssb.r], bias=EPS, scale=1.0 / 512)
                        K.act(ssb.t[0:ntok, 2:3], ssb.t[0:ntok, 1:2], AF.Exp, [ssb.r], [ssb.r], scale=-0.5)
                        K.stt(og1.t[0:ntok, h * 512:(h + 1) * 512], osb.t[0:ntok, :], ssb.t[0:ntok, 2:3], gsr.t[0:ntok, h * 512:(h + 1) * 512],
                              ALU.mult, ALU.mult, [osb.r, ssb.r, gsr.r], [og1.r])
                    for h in range(4):
                        for dc in range(2):
                            hd = h * 2 + dc
                            sps = sps_r[ci["sps"] % 2]; ci["sps"] += 1
                            K.mm(sps.t[:, :], kd.t[0:ntok, hd * 128:(hd + 1) * 128], vg.t[0:ntok, h * 512:(h + 1) * 512], True, True,
                                 [kd.r, vg.r], [sps.r])
                            K.tt("dve", S.t[:, hd, :], sps.t[:, :], S.t[:, hd, :], ALU.add, [sps.r, S_res[hd]], [S_res[hd]])
                            K.act(S.t[:, hd, :], S.t[:, hd, :], AF.Copy, [S_res[hd], dec_all.r], [S_res[hd]], scale=dec_all.t[:, ti, hd:hd + 1])
                            K.cp("pool", Sb.t[:, hd, :], S.t[:, hd, :], [S_res[hd]], [Sb_res[hd]])

                def e_w(ti):
                    tok0, ntok, g = tiles[ti]
                    og1 = og1_r[ti % 2]; og1T = og1T_r[0]; x1 = x1_r[ti % 3]
                    for grp in range(4):
                        tp = tp_r[0]
                        for i4 in range(4):
                            kc = grp * 4 + i4
                            K.tr(tp.t[:, i4, 0:ntok], og1.t[0:ntok, kc * 128:(kc + 1) * 128], identb[0:ntok, 0:ntok], [og1.r, cb.r], [tp.r])
                        K.cp("act" if grp % 2 else "dve", og1T.t[:, grp * 4:grp * 4 + 4, 0:ntok], tp.t[:, 0:4, 0:ntok], [tp.r], [og1T.r])
                    for n in range(4):
                        y = y_r[ci["y"] % 2]; ci["y"] += 1
                        tmp = tmp_r[ci["t"] % 2]; ci["t"] += 1
                        for kc in range(16):
                            K.mm(y.t[0:ntok, :], og1T.t[:, kc, 0:ntok], wo.t[:, kc, n * 512:(n + 1) * 512], kc == 0, kc == 15, [og1T.r, wo.r], [y.r])
                        K.tt("dve", tmp.t[0:ntok, :], y.t[0:ntok, :], gate_b[1][g].t[0:ntok, n * 512:(n + 1) * 512], ALU.mult,
                             [y.r, gate_b[1][g].r], [tmp.r])
                        K.tt("pool", x1.t[0:ntok, n * 512:(n + 1) * 512], tmp.t[0:ntok, :], x1.t[0:ntok, n * 512:(n + 1) * 512], ALU.add,
                             [tmp.r, x1.r], [x1.r])
                    ssb = ss_r[ci["ss"] % 8]; ci["ss"] += 1
                    K.ttr(junk.t[0:ntok, :], x1.t[0:ntok, :], x1.t[0:ntok, :], ssb.t[0:ntok, 0:1], [x1.r], [ssb.r])
                    K.act(ssb.t[0:ntok, 1:2], ssb.t[0:ntok, 0:1], AF.Ln, [ssb.r], [ssb.r], bias=EPS, scale=1.0 / D)
                    K.act(ssb.t[0:ntok, 2:3], ssb.t[0:ntok, 1:2], AF.Exp, [ssb.r], [ssb.r], scale=-0.5)
                    K.stt(x1.t[0:ntok, :], x1.t[0:ntok, :], ssb.t[0:ntok, 2:3], fing_b.t[0:ntok, :], ALU.mult, ALU.mult,
                          [x1.r, ssb.r, fing_b.r], [x1.r])
                    dst = yp[tok0:tok0 + ntok, :] if g == 0 else ys[0:ntok, :]
                    K.dma("sp", dst, x1.t[0:ntok, :], [x1.r], [Res()], "yo%d" % (ti % 3))

                e_loads(0); e_srload(0)
                for ti, (tok0, ntok, g) in enumerate(tiles):
                    if g == 1:
                        K.dma("sp", sp_o.rearrange("h (c p) e -> p (h c) e", p=128), S.t[:], S_res, [Res()], "Sout")
                        K.dma("sp", S.t[:], sg_in.rearrange("h (c p) e -> p (h c) e", p=128), [], S_res, "Sin")
                        K.cp("pool", Sb.t[:], S.t[:], S_res, Sb_res)
                    e_loads(ti + 1)
                    e_xyz(ti)
                    if ti >= 1:
                        e_w(ti - 1)
                e_w(len(tiles) - 1)
                K.dma("sp", ss_o.rearrange("h (c p) e -> p (h c) e", p=128), S.t[:], S_res, [Res()], "Sout2")
                P.barrier()

        P.barrier()
        for e in ENG:
            if e == "pe":
                continue
            P.op(e, lambda eng: eng.nop(), (), ())
        with ExitStack() as es2:
            P.emit(nc, es2)
    return nc


_NC = None


def _consts():
    i = np.arange(128)
    cfv = np.zeros((128, NCF), np.float32)
    cfv[:, 0:128] = np.eye(128, dtype=np.float32)
    cfv[:, 128:256] = (i[:, None] < i[None, :]).astype(np.float32)
    cfv[:, 256:384] = (i[:, None] <= i[None, :]).astype(np.float32) * (-1.0 / 16.0)
    cfv[:, 384:512] = (i[:, None] <= i[None, :]).astype(np.float32)
    cfv[:, 512:640] = 1.0
    cbv = np.zeros((128, NCB), np.float32)
    cbv[:, 0:128] = np.eye(128, dtype=np.float32)
    cbv[:, 128:256] = (i[:, None] >= i[None, :]).astype(np.float32)
    cbv[:, 256:384] = 1.0
    return cfv, cbv.astype(ml_dtypes.bfloat16)


def kernel(x_prompt, x_sample, cache_sb_k, cache_sb_v, state_gla, c_prompt, c_sample,
           w_ada, b_ada, norm_g, sb_w_in, sb_w_out, gla_w_in, gla_w_a2, gla_b_a,
           gla_norm_g, gla_w_out, final_norm_g):
    global _NC
    if _NC is None:
        _NC = build()
    f = lambda a: np.ascontiguousarray(np.asarray(a), dtype=np.float32)
    cfv, cbv = _consts()
    shared = {
        "wada": f(w_ada), "bada": f(b_ada).reshape(96, 128), "brow": f(b_ada), "ng": f(norm_g).reshape(32, 128),
        "wq": f(sb_w_in)[0], "wo0": f(sb_w_out)[0], "wg": f(gla_w_in)[0], "wa2": f(gla_w_a2)[0], "ba": f(gla_b_a),
        "glag": f(gla_norm_g), "wo1": f(gla_w_out)[0], "fing": f(final_norm_g).reshape(1, D), "cf": cfv, "cb": cbv,
    }
    x_prompt = f(x_prompt); x_sample = f(x_sample); cache_sb_k = f(cache_sb_k); cache_sb_v = f(cache_sb_v)
    state_gla = f(state_gla); c_prompt = f(c_prompt); c_sample = f(c_sample)
    in_maps = []
    for b in range(8):
        m = dict(shared)
        m["xp"] = x_prompt[b]; m["xs"] = x_sample[b]
        m["ck"] = cache_sb_k[0, b]; m["cv"] = cache_sb_v[0, b]; m["sg"] = state_gla[0, b]
        m["c32"] = np.ascontiguousarray(np.stack([c_prompt[b], c_sample[b]]).reshape(32, 128))
        in_maps.append(m)
    res = run_bass_kernel_spmd(_NC, in_maps, core_ids=list(range(8)))
    r = res.results
    y_p = np.stack([r[b]["yp"] for b in range(8)])
    y_s = np.stack([r[b]["ys"] for b in range(8)])
    k_p = np.stack([r[b]["kp"].reshape(T, H, DH) for b in range(8)])[None]
    v_p = np.stack([r[b]["vp"].reshape(T, H, DH) for b in range(8)])[None]
    k_s = np.stack([r[b]["ks"].reshape(TS, H, DH) for b in range(8)])[None]
    v_s = np.stack([r[b]["vs"].reshape(TS, H, DH) for b in range(8)])[None]
    s_p = np.stack([r[b]["spo"] for b in range(8)])[None]
    s_s = np.stack([r[b]["sso"] for b in range(8)])[None]
    return (y_p, y_s, k_p, v_p, k_s, v_s, s_p, s_s)
```

```python
import numpy as np
import ml_dtypes
from contextlib import ExitStack
import concourse.bass as bass
import concourse.mybir as mybir
from concourse.bass_utils import run_bass_kernel_spmd

F32 = mybir.dt.float32
BF16 = mybir.dt.bfloat16
AF = mybir.ActivationFunctionType
ALU = mybir.AluOpType

D = 2048
T = 4096
TS = 16
NT = T + TS
H = 16
DH = 128
EPS = 1e-6
NCF = 640
NCB = 640
STAGE = 9
NBLK = 8
ENG = ["pe", "act", "dve", "pool", "sp"]
SEM_CAP = 30000


class Res:
    __slots__ = ("name", "w", "rc", "rd", "psum")

    def __init__(self, name=""):
        self.name = name
        self.psum = False
        self.w = None
        self.rc = {}
        self.rd = []


class Op:
    __slots__ = ("eng", "fn", "deps", "dma", "key", "sem", "val", "sig")


class Prog:
    def __init__(self):
        self.ops = []
        self.bar = {e: set() for e in ENG}
        self.last = {}
        self.dmas = []

    def op(self, eng, fn, reads=(), writes=(), key=None, nobar=False):
        o = Op()
        o.eng = eng
        o.fn = fn
        o.dma = key is not None
        o.key = key
        o.sig = False
        o.sem = None
        o.val = 0
        deps = set()
        for r in reads:
            if r.w is not None:
                deps.add(r.w)
            if r.psum:
                deps.update(v for k, v in r.rc.items() if k != eng)
        for w in writes:
            if w.w is not None:
                deps.add(w.w)
            deps.update(w.rc.values())
            deps.update(w.rd)
        deps |= self.bar[eng]
        self.bar[eng] = set()
        o.deps = [d for d in deps if d is not o and (d.dma or o.dma or d.eng != eng or eng != "pe")]
        for d in o.deps:
            d.sig = True
        for r in reads:
            if o.dma:
                r.rd.append(o)
            else:
                r.rc[eng] = o
        for w in writes:
            w.w = o
            w.rc = {}
            w.rd = []
        self.ops.append(o)
        self.last[eng] = o
        if o.dma and not nobar:
            self.dmas.append(o)
        return o

    def barrier(self):
        s = set(self.last.values()) | set(self.dmas)
        for e in ENG:
            self.bar[e] |= s
        self.dmas = []

    def emit(self, nc, es):
        keys = []
        for o in self.ops:
            if o.dma and o.key not in keys:
                keys.append(o.key)
        dsem = {k: es.enter_context(nc.semaphore("d_" + k)) for k in keys}
        cnt = {e: 0 for e in ENG}
        csem = {e: es.enter_context(nc.semaphore("c_" + e + "0")) for e in ENG}
        gen = {e: 0 for e in ENG}
        dcnt = {}
        for o in self.ops:
            if o.dma:
                v = dcnt.get(o.key, 0) + 16
                dcnt[o.key] = v
                o.sem = dsem[o.key]
                o.val = v
            elif o.sig:
                if cnt[o.eng] >= SEM_CAP:
                    gen[o.eng] += 1
                    csem[o.eng] = es.enter_context(nc.semaphore("c_%s%d" % (o.eng, gen[o.eng])))
                    cnt[o.eng] = 0
                cnt[o.eng] += 1
                o.sem = csem[o.eng]
                o.val = cnt[o.eng]
        ops = self.ops

        def run(engname, eng):
            waited = {}
            for o in ops:
                if o.eng != engname:
                    continue
                need = {}
                for d in o.deps:
                    k = id(d.sem)
                    if k not in need or need[k][1] < d.val:
                        need[k] = (d.sem, d.val)
                for k, (s, v) in need.items():
                    if waited.get(k, 0) < v:
                        eng.wait_ge(s, v)
                        waited[k] = v
                ins = o.fn(eng)
                if o.dma:
                    ins.then_inc(o.sem, 16)
                elif o.sig:
                    ins.then_inc(o.sem, 1)

        with nc.Block() as block:
            @block.tensor
            def _(e):
                run("pe", e)

            @block.scalar
            def _(e):
                run("act", e)

            @block.vector
            def _(e):
                run("dve", e)

            @block.gpsimd
            def _(e):
                run("pool", e)

            @block.sync
            def _(e):
                run("sp", e)


class Buf:
    __slots__ = ("t", "r")

    def __init__(self, t, name):
        self.t = t
        self.r = Res(name)


class KB:
    def __init__(self, nc):
        self.nc = nc
        self.P = Prog()
        self.uid = 0
        self.pool_dmas = []

    def sb(self, es, shape, dt, name=None):
        self.uid += 1
        nm = "%s_%d" % (name or "sb", self.uid)
        return Buf(es.enter_context(self.nc.sbuf_tensor(nm, list(shape), dt)), nm)

    def ps(self, es, shape, dt, name=None):
        self.uid += 1
        nm = "%s_%d" % (name or "ps", self.uid)
        b = Buf(es.enter_context(self.nc.psum_tensor(nm, list(shape), dt)), nm)
        b.r.psum = True
        return b

    def ring(self, es, n, shape, dt, name=None, psum=False):
        return [(self.ps if psum else self.sb)(es, shape, dt, name) for _ in range(n)]

    def mm(self, out, lhsT, rhs, start, stop, R, W):
        self.P.op("pe", lambda e: e.matmul(out, lhsT, rhs, start=start, stop=stop), R, W)

    def tr(self, out, in_, ident, R, W):
        self.P.op("pe", lambda e: e.transpose(out, in_, ident), R, W)

    def act(self, out, in_, func, R, W, bias=None, scale=None):
        kw = {}
        if bias is not None:
            kw["bias"] = bias
        if scale is not None:
            kw["scale"] = scale
        self.P.op("act", lambda e: e.activation(out=out, in_=in_, func=func, **kw), R, W)

    def tt(self, eng, out, in0, in1, op, R, W):
        self.P.op(eng, lambda e: e.tensor_tensor(out=out, in0=in0, in1=in1, op=op), R, W)

    def ts(self, eng, out, in0, s1, s2, op0, op1, R, W):
        if op1 is None:
            self.P.op(eng, lambda e: e.tensor_scalar(out=out, in0=in0, scalar1=s1, scalar2=None, op0=op0), R, W)
        else:
            self.P.op(eng, lambda e: e.tensor_scalar(out=out, in0=in0, scalar1=s1, scalar2=s2, op0=op0, op1=op1), R, W)

    def stt(self, out, in0, scalar, in1, op0, op1, R, W):
        self.P.op("dve", lambda e: e.scalar_tensor_tensor(out=out, in0=in0, scalar=scalar, in1=in1, op0=op0, op1=op1), R, W)

    def ttr(self, out, in0, in1, accum, R, W):
        self.P.op("act", lambda e: e.activation(out=out, in_=in0, func=AF.Square, accum_out=accum), R, W)

    def cp(self, eng, out, in_, R, W):
        if eng == "act":
            self.P.op("act", lambda e: e.activation(out=out, in_=in_, func=AF.Copy), R, W)
        else:
            self.P.op(eng, lambda e: e.tensor_copy(out=out, in_=in_), R, W)

    def memset(self, eng, ap, val, W):
        self.P.op(eng, lambda e: e.memset(ap, val), (), W)

    def recip(self, out, in_, R, W):
        self.P.op("dve", lambda e: e.reciprocal(out=out, in_=in_), R, W)

    def dma(self, eng, out, in_, R, W, key, nobar=False, **kw):
        if eng == "pool":
            key = "pl%d" % (len(self.pool_dmas) % 5)
        o = self.P.op(eng, lambda e: e.dma_start(out=out, in_=in_, **kw), R, W, key=key, nobar=nobar)
        if eng == "pool":
            self.pool_dmas.append(o)
            if len(self.pool_dmas) > 4:
                d = self.pool_dmas[-5]
                if d not in o.deps:
                    o.deps.append(d)
                    d.sig = True


def build():
    nc = bass.Bass("TRN2", target_bir_lowering=False)
    K = KB(nc)
    P = K.P

    def din(name, shape, dt=F32):
        return nc.dram_tensor(name, list(shape), dt, kind="ExternalInput").ap()

    def dout(name, shape):
        return nc.dram_tensor(name, list(shape), F32, kind="ExternalOutput").ap()

    def dscr(name, shape, dt):
        return nc.dram_tensor(name, list(shape), dt, kind="Internal").ap()

    xp = din("xp", [T, D]); xs = din("xs", [TS, D])
    ck = din("ck", [T, H, DH]); cv = din("cv", [T, H, DH]); sg_in = din("sg", [4, 256, 512])
    c32 = din("c32", [32, 128]); wada = din("wada", [2, D, 3 * D]); bada = din("bada", [96, 128])
    brow = din("brow", [2, 3 * D]); ng = din("ng", [32, 128])
    wq = din("wq", [D, 4 * D]); wo0 = din("wo0", [D, D]); wg = din("wg", [D, 6160])
    wa2 = din("wa2", [16, 1024]); ba = din("ba", [1, 1024]); glag = din("glag", [1, D])
    wo1 = din("wo1", [D, D]); fing = din("fing", [1, D])
    cf_d = din("cf", [128, NCF]); cb_d = din("cb", [128, NCB], BF16)

    yp = dout("yp", [T, D]); ys = dout("ys", [TS, D]); kp = dout("kp", [T, D]); vp = dout("vp", [T, D])
    ks = dout("ks", [TS, D]); vs = dout("vs", [TS, D]); sp_o = dout("spo", [4, 256, 512]); ss_o = dout("sso", [4, 256, 512])

    wq_s = dscr("wq_s", [16, 128, 16, 512], BF16); wo0_s = dscr("wo0_s", [D, D], BF16)
    wg_s = dscr("wg_s", [12, 128, 16, 512], BF16); wta_s = dscr("wta_s", [128, 16, 16], BF16); wo1_s = dscr("wo1_s", [D, D], BF16)
    qT_s = dscr("qT_s", [H, DH, NT], BF16); kT_s = dscr("kT_s", [H, DH, NT], BF16)
    sgT_s = dscr("sgT_s", [H, DH, NT], BF16); v_s = dscr("v_s", [NT, D], BF16)
    ogT_s = dscr("ogT_s", [H, DH, NT], BF16)
    x1_s = dscr("x1_s", [NT, D], F32); h1T_s = dscr("h1T_s", [16, 128, NT], BF16)
    qinT_s = dscr("qinT_s", [8, 128, NT], BF16); kdT_s = dscr("kdT_s", [8, 128, NT], BF16)
    kd_s = dscr("kd_s", [NT, 1024], BF16); vg_s = dscr("vg_s", [NT, D], BF16); sr_s = dscr("sr_s", [NT, D], BF16)

    R_wq = [Res("wq%d" % i) for i in range(16)]
    R_wo0 = [Res("wo0") for i in range(16)]; R_wg = [Res("wg") for i in range(16)]; R_wo1 = [Res("wo1") for i in range(16)]
    R_scr = {n: Res(n) for n in ["qT", "kT", "sgT", "v", "ogT", "x1", "h1T", "qinT", "kdT", "kd", "vg", "sr"]}

    top = ExitStack()
    with top:
        cf = K.sb(top, [128, NCF], F32, "cf"); cb = K.sb(top, [128, NCB], BF16, "cb")
        identf = cf.t[:, 0:128]; maskST = cf.t[:, 128:256]; uincl = cf.t[:, 256:384]; maskLE = cf.t[:, 384:512]; ones_f = cf.t[:, 512:640]
        identb = cb.t[:, 0:128]; lmat = cb.t[:, 128:256]; ones_b = cb.t[:, 256:384]
        lmat_n = cb.t[:, 384:512]; ones_n = cb.t[:, 512:640]
        esL0 = ExitStack()
        gate_b = [None, [K.sb(top, [128, D], F32, "gate") for g in range(2)]]
        glag_b = K.sb(top, [128, D], F32, "glag"); fing_b = K.sb(top, [128, D], F32, "fing")
        shiftT = [K.sb(top, [128, 16, 2], F32, "shT") for l in range(2)]
        gsT = [K.sb(top, [128, 16, 2], F32, "gsT") for l in range(2)]
        dec_all = K.sb(top, [128, 33, 8], F32, "dec")
        junk = K.sb(top, [128, D], BF16, "junk")

        K.dma("sp", cf.t[:], cf_d[:, :], [], [cf.r], "cf")
        K.dma("sp", cb.t[:], cb_d[:, :], [], [cb.r], "cb")
        K.dma("sp", glag_b.t[:], glag[0, :].partition_broadcast(128), [], [glag_b.r], "glag")
        K.dma("sp", fing_b.t[:], fing[0, :].partition_broadcast(128), [], [fing_b.r], "fing")

        gate_b[0] = [K.sb(esL0, [128, D], F32, "gate0") for g in range(2)]
        with ExitStack() as es:
            c32_t = K.sb(es, [32, 128], F32); bada_t = K.sb(es, [96, 128], F32); ng_t = K.sb(es, [32, 128], F32)
            brow_b = K.sb(es, [1, 2 * 3 * D], BF16)
            cT = K.sb(es, [128, 32], F32); cTb2 = K.sb(es, [128, 16, 2], BF16)
            cB = [K.sb(es, [128, 16, 128], BF16) for g in range(2)]
            badaT = K.sb(es, [128, 96], F32); ngT = K.sb(es, [128, 32], F32)
            tmpA = K.sb(es, [128, 16, 2], F32)
            wa_ring = K.ring(es, 2, [128, 16, 512], BF16, "wa")
            waf_ring = K.ring(es, 2, [128, 16, 512], F32, "waf")
            tps = K.ps(es, [128, 512], F32, "tps")
            adaps = K.ps(es, [128, 256, 2], F32, "adaps")
            gps = K.ring(es, 2, [128, 512], F32, "gps", psum=True)

            K.dma("sp", c32_t.t[:], c32[:, :], [], [c32_t.r], "c32")
            K.dma("sp", bada_t.t[:], bada[:, :], [], [bada_t.r], "bada")
            K.dma("sp", ng_t.t[:], ng[:, :], [], [ng_t.r], "ng")
            K.dma("pool", brow_b.t[:], brow.rearrange("l n -> (l n)").rearrange("(o n) -> o n", o=1), [], [brow_b.r], "browb",
                  max_dma_last_dim=4096)
            wq_v = wq.rearrange("(kc p) n -> p kc n", p=128)
            for wc in range(16):
                K.dma("pool", wq_s[wc], wq_v[:, :, wc * 512:(wc + 1) * 512], [], [R_wq[wc]], "pcq", nobar=True)
            K.tr(tps.t[:, 0:32], c32_t.t[:, :], identf[0:32, 0:32], [c32_t.r, cf.r], [tps.r])
            K.cp("dve", cT.t[:], tps.t[:, 0:32], [tps.r], [cT.r])
            K.tr(tps.t[:, 0:96], bada_t.t[:, :], identf[0:96, 0:96], [bada_t.r, cf.r], [tps.r])
            K.cp("dve", badaT.t[:], tps.t[:, 0:96], [tps.r], [badaT.r])
            K.tr(tps.t[:, 0:32], ng_t.t[:, :], identf[0:32, 0:32], [ng_t.r, cf.r], [tps.r])
            K.cp("dve", ngT.t[:], tps.t[:, 0:32], [tps.r], [ngT.r])
            for g in range(2):
                K.cp("dve", cTb2.t[:, :, g], cT.t[:, g * 16:(g + 1) * 16], [cT.r], [cTb2.r])
                for kc in range(16):
                    K.ts("dve", cB[g].t[:, kc, :], ones_f, cT.t[:, g * 16 + kc:g * 16 + kc + 1], None, ALU.mult, None,
                         [cT.r, cf.r], [cB[g].r])
            wav = wada.rearrange("l (kc p) n -> l p kc n", p=128)
            wi = 0
            wa_r2 = [Res("wa2a"), Res("wa2b")]

            def ada_load(k):
                if k < 24:
                    K.dma("sp", waf_ring[k % 2].t[:], wav[k // 12, :, :, (k % 12) * 512:(k % 12 + 1) * 512], [], [waf_ring[k % 2].r], "waf%d" % (k % 2))

            ada_load(0)
            for l in range(2):
                for j in range(12):
                    wa = wa_ring[wi % 2]; waf = waf_ring[wi % 2]; wi += 1
                    ada_load(wi)
                    wa2r = wa_r2[(wi - 1) % 2]
                    K.cp("dve", wa.t[:, 0:8, :], waf.t[:, 0:8, :], [waf.r], [wa.r])
                    K.cp("act", wa.t[:, 8:16, :], waf.t[:, 8:16, :], [waf.r], [wa2r])
                    if j < 8:
                        for fi in range(4):
                            fc = j * 4 + fi
                            for kc in range(16):
                                K.mm(adaps.t[:, fc, :], wa.t[:, kc, fi * 128:(fi + 1) * 128], cTb2.t[:, kc, :],
                                     kc == 0, kc == 15, [wa.r, wa2r, cTb2.r], [adaps.r])
                    else:
                        for g in range(2):
                            gp = gps[g]
                            for kc in range(16):
                                K.mm(gp.t[:, :], cB[g].t[:, kc, :], wa.t[:, kc, :], kc == 0, False, [wa.r, wa2r, cB[g].r], [gp.r])
                            K.mm(gp.t[:, :], ones_b[0:1, :], brow_b.t[0:1, l * 6144 + j * 512: l * 6144 + (j + 1) * 512],
                                 False, True, [cb.r, brow_b.r], [gp.r])
                            K.cp("act", gate_b[l][g].t[:, (j - 8) * 512:(j - 7) * 512], gp.t[:, :], [gp.r], [gate_b[l][g].r])
                    if j == 7:
                        for g in range(2):
                            K.tt("dve", shiftT[l].t[:, :, g], adaps.t[:, 0:16, g], badaT.t[:, l * 48:l * 48 + 16], ALU.add,
                                 [adaps.r, badaT.r], [shiftT[l].r])
                            K.tt("dve", tmpA.t[:, :, g], adaps.t[:, 16:32, g], badaT.t[:, l * 48 + 16:l * 48 + 32], ALU.add,
                                 [adaps.r, badaT.r], [tmpA.r])
                            K.stt(gsT[l].t[:, :, g], tmpA.t[:, :, g], 1.0, ngT.t[:, l * 16:(l + 1) * 16], ALU.add, ALU.mult,
                                  [tmpA.r, ngT.r], [gsT[l].r])
            P.barrier()
        wg_v = wg.rearrange("(kc p) n -> p kc n", p=128)
        for (src, dst, rl) in [(wo0, wo0_s, R_wo0)]:
            for rb in range(16):
                K.dma("pool", dst[rb * 128:(rb + 1) * 128, :], src[rb * 128:(rb + 1) * 128, :], [], [rl[rb]], "pc", nobar=True,
                      max_dma_last_dim=8192)
        for wc in range(12):
            K.dma("pool", wg_s[wc], wg_v[:, :, wc * 512:(wc + 1) * 512], [], [R_wg[wc]], "pcg", nobar=True)
        K.dma("pool", wta_s[:, :, :], wg_v[:, :, 6144:6160], [], [R_wg[12]], "pcg", nobar=True)
        for (src, dst, rl) in [(wo1, wo1_s, R_wo1)]:
            for rb in range(16):
                K.dma("pool", dst[rb * 128:(rb + 1) * 128, :], src[rb * 128:(rb + 1) * 128, :], [], [rl[rb]], "pc", nobar=True,
                      max_dma_last_dim=8192)

        def norm_p1(xt_ap, xt_res, ntok, hb, ssb):
            K.ttr(junk.t[0:ntok, :], xt_ap, xt_ap, ssb.t[0:ntok, 0:1], [xt_res], [ssb.r])
            K.act(ssb.t[0:ntok, 1:2], ssb.t[0:ntok, 0:1], AF.Ln, [ssb.r], [ssb.r], bias=EPS, scale=1.0 / D)
            K.act(ssb.t[0:ntok, 2:3], ssb.t[0:ntok, 1:2], AF.Exp, [ssb.r], [ssb.r], scale=-0.5)
            K.ts("dve", hb.t[0:ntok, :], xt_ap, ssb.t[0:ntok, 2:3], None, ALU.mult, None, [xt_res, ssb.r], [hb.r])

        def norm_tile(xt_ap, xt_res, ntok, l, g, hb, ssb, tpr, tpi, hT_ap_fn, hT_res, evac_eng):
            norm_p1(xt_ap, xt_res, ntok, hb, ssb)
            norm_p2(ntok, l, g, hb, tpr, tpi, hT_ap_fn, hT_res, evac_eng)

        def norm_p2(ntok, l, g, hb, tpr, tpi, hT_ap_fn, hT_res, evac_eng):
            for grp in range(4):
                tp = tpr[tpi[0] % len(tpr)]; tpi[0] += 1
                for i in range(4):
                    kc = grp * 4 + i
                    K.tr(tp.t[:, i, 0:ntok], hb.t[0:ntok, kc * 128:(kc + 1) * 128], identb[0:ntok, 0:ntok], [hb.r, cb.r], [tp.r])
                for i in range(4):
                    kc = grp * 4 + i
                    if evac_eng == "dve":
                        K.ts("dve", hT_ap_fn(kc), tp.t[:, i, 0:ntok], gsT[l].t[:, kc, g:g + 1], shiftT[l].t[:, kc, g:g + 1],
                             ALU.mult, ALU.add, [tp.r, gsT[l].r, shiftT[l].r], [hT_res])
                    else:
                        K.act(hT_ap_fn(kc), tp.t[:, i, 0:ntok], AF.Identity, [tp.r, gsT[l].r, shiftT[l].r], [hT_res],
                              bias=shiftT[l].t[:, kc, g:g + 1], scale=gsT[l].t[:, kc, g:g + 1])

        def silu_from_psum(ps_ap, ps_res, n_p, n_f, out_ap, out_res, tmp):
            K.act(tmp.t[0:n_p, 0:n_f], ps_ap, AF.Exp, [ps_res], [tmp.r], scale=-1.0)
            K.ts("dve", tmp.t[0:n_p, 0:n_f], tmp.t[0:n_p, 0:n_f], 1.0, None, ALU.add, None, [tmp.r], [tmp.r])
            K.recip(tmp.t[0:n_p, 0:n_f], tmp.t[0:n_p, 0:n_f], [tmp.r], [tmp.r])
            K.tt("dve", out_ap, ps_ap, tmp.t[0:n_p, 0:n_f], ALU.mult, [ps_res, tmp.r], [out_res])

        blocks = [(bi * 512, 512, 0) for bi in range(NBLK)] + [(T, TS, 1)]

        def xrows(tok0, n):
            return xp[tok0:tok0 + n, :] if tok0 < T else xs[tok0 - T:tok0 - T + n, :]

        with ExitStack() as es:
          if STAGE >= 1:
                xt_r = K.ring(es, 4, [128, D], F32, "xt"); hb_r = K.ring(es, 4, [128, D], BF16, "hb")
                ss_r = K.ring(es, 4, [128, 4], F32, "ss")
                hT_r = K.ring(es, 2, [128, 16, 512], BF16, "hT")
                hT_res = [[Res("hTr") for t in range(4)] for s in range(2)]
                wt_r = K.ring(es, 3, [128, 16, 512], BF16, "wt")
                qst_r = K.ring(es, 2, [128, 512], BF16, "qst"); stmp_r = K.ring(es, 1, [128, 512], F32, "stmp")
                kf_r = K.ring(es, 2, [128, 512], F32, "kf"); kb_r = K.ring(es, 2, [128, 512], BF16, "kb")
                kTst_r = K.ring(es, 1, [128, 4, 512], BF16, "kTst")
                tp_r = K.ring(es, 2, [128, 8, 128], BF16, "tp", psum=True)
                ps_r = K.ring(es, 4, [128, 512], F32, "psA", psum=True)
                tp2_r = K.ring(es, 2, [128, 8, 128], BF16, "tp2", psum=True)
                tpi = [0]; ci = {"xt": 0, "wt": 0, "ps": 0, "st": 0, "kf": 0, "kT": 0, "tp2": 0}
                def a_xload(bi):
                    tok0, ntokb, g = blocks[bi]
                    for t in range((ntokb + 127) // 128):
                        ntok = min(128, ntokb - t * 128)
                        K.dma("sp", xt_r[t].t[0:ntok, :], xrows(tok0 + t * 128, ntok), [], [xt_r[t].r], "xt%d" % t)

                def a_p1(bi):
                    tok0, ntokb, g = blocks[bi]
                    for t in range((ntokb + 127) // 128):
                        ntok = min(128, ntokb - t * 128)
                        norm_p1(xt_r[t].t[0:ntok, :], xt_r[t].r, ntok, hb_r[t], ss_r[t])

                def a_p2(bi):
                    tok0, ntokb, g = blocks[bi]
                    hTb = hT_r[bi % 2]; hTres = hT_res[bi % 2]
                    for t in range((ntokb + 127) // 128):
                        ntok = min(128, ntokb - t * 128)
                        norm_p2(ntok, 0, g, hb_r[t], tp_r, tpi,
                                (lambda kc, hTb=hTb, ntok=ntok, t=t: hTb.t[:, kc, t * 128:t * 128 + ntok]), hTres[t], "dve" if t % 2 == 0 else "act")

                steps = [(bi, wc) for bi in range(len(blocks)) for wc in range(16)]
                kpend = [None]

                def a_wload(si):
                    if si < len(steps):
                        wc = steps[si][1]
                        K.dma("sp", wt_r[si % 3].t[:], wq_s[wc], [R_wq[wc]], [wt_r[si % 3].r], "wt%d" % (si % 3))

                a_xload(0); a_p1(0); a_p2(0)
                a_wload(0); a_wload(1)
                for bi, (tok0, ntokb, g) in enumerate(blocks):
                    ntl = (ntokb + 127) // 128
                    hTb = hT_r[bi % 2]; hTres = hT_res[bi % 2]
                    hres = [hTres[t] for t in range(ntl)]
                    for wc in range(16):
                        si = bi * 16 + wc
                        wt = wt_r[si % 3]
                        a_wload(si + 2)
                        if bi + 1 < len(blocks):
                            if wc == 2:
                                a_xload(bi + 1)
                            if wc == 5:
                                a_p1(bi + 1)
                            if wc == 9:
                                a_p2(bi + 1)
                        kind = wc // 4; hbase = (wc % 4) * 4
                        if kind in (0, 3):
                            for hh in range(4):
                                ps = ps_r[ci["ps"] % 4]; ci["ps"] += 1
                                for kc in range(16):
                                    K.mm(ps.t[:, 0:ntokb], wt.t[:, kc, hh * 128:(hh + 1) * 128], hTb.t[:, kc, 0:ntokb],
                                         kc == 0, kc == 15, [wt.r] + hres, [ps.r])
                                qst = qst_r[ci["st"] % 2]; stmp = stmp_r[0]; ci["st"] += 1
                                if kind == 0:
                                    K.act(qst.t[:, 0:ntokb], ps.t[:, 0:ntokb], AF.Copy, [ps.r], [qst.r], scale=DH ** -0.5)
                                    K.dma("sp", qT_s[hbase + hh, :, tok0:tok0 + ntokb], qst.t[:, 0:ntokb], [qst.r], [],
                                          "qst%d" % (ci["st"] % 2))
                                else:
                                    silu_from_psum(ps.t[:, 0:ntokb], ps.r, 128, ntokb, qst.t[:, 0:ntokb], qst.r, stmp)
                                    K.dma("sp", sgT_s[hbase + hh, :, tok0:tok0 + ntokb], qst.t[:, 0:ntokb], [qst.r], [],
                                          "qst%d" % (ci["st"] % 2))
                        else:
                            kTst = kTst_r[0]; ci["kT"] += 1
                            for t in range(ntl):
                                ntok = min(128, ntokb - t * 128)
                                ps = ps_r[ci["ps"] % 4]; ci["ps"] += 1
                                for kc in range(16):
                                    K.mm(ps.t[0:ntok, :], hTb.t[:, kc, t * 128:t * 128 + ntok], wt.t[:, kc, :], kc == 0, kc == 15, [wt.r, hTres[t]], [ps.r])
                                kf = kf_r[ci["kf"] % 2]; kb = kb_r[ci["kf"] % 2]; ci["kf"] += 1
                                K.cp("act", kf.t[0:ntok, :], ps.t[0:ntok, :], [ps.r], [kf.r])
                                if tok0 < T:
                                    dst = (kp if kind == 1 else vp)[tok0 + t * 128:tok0 + t * 128 + ntok, hbase * 128:hbase * 128 + 512]
                                else:
                                    dst = (ks if kind == 1 else vs)[0:ntok, hbase * 128:hbase * 128 + 512]
                                K.dma("sp", dst, kf.t[0:ntok, :], [kf.r], [Res()], "kf%d" % (ci["kf"] % 2))
                                K.cp("dve", kb.t[0:ntok, :], ps.t[0:ntok, :], [ps.r], [kb.r])
                                if kind == 1:
                                    def _ktr(kb=kb, ntok=ntok, t=t, kTst=kTst):
                                        tp2 = tp2_r[ci["tp2"] % 2]; ci["tp2"] += 1
                                        for hh in range(4):
                                            K.tr(tp2.t[:, hh, 0:ntok], kb.t[0:ntok, hh * 128:(hh + 1) * 128], identb[0:ntok, 0:ntok],
                                                 [kb.r, cb.r], [tp2.r])
                                        K.cp("act" if t % 2 else "dve", kTst.t[:, :, t * 128:t * 128 + ntok], tp2.t[:, 0:4, 0:ntok], [tp2.r], [kTst.r])
                                    if kpend[0] is not None:
                                        kpend[0]()
                                    kpend[0] = _ktr
                                else:
                                    K.dma("sp", v_s[tok0 + t * 128:tok0 + t * 128 + ntok, hbase * 128:hbase * 128 + 512], kb.t[0:ntok, :],
                                          [kb.r], [], "kb%d" % (ci["kf"] % 2))
                            if kind == 1:
                                kpend[0](); kpend[0] = None
                                K.dma("sp", kT_s[hbase:hbase + 4, :, tok0:tok0 + ntokb].rearrange("h d t -> d h t"), kTst.t[:, :, 0:ntokb],
                                      [kTst.r], [], "kTst0")
                P.barrier()

        if STAGE >= 2:
            with ExitStack() as es:
                qh_r = K.ring(es, 2, [128, NT], BF16, "qh"); kh_r = K.ring(es, 2, [128, NT], BF16, "kh")
                vh_r = K.ring(es, 2, [128, 33, 128], BF16, "vh"); sgh_r = K.ring(es, 2, [128, NT], BF16, "sgh")
                e_r = K.ring(es, 4, [128, 512], F32, "e"); spb_r = K.ring(es, 3, [128, 512], BF16, "spb")
                R_r = K.ring(es, 4, [128, 512], BF16, "Rr"); C_r = K.ring(es, 3, [128, 512], BF16, "C")
                a_r = K.ring(es, 3, [128, 512], BF16, "a"); og_r = K.ring(es, 2, [128, 512], BF16, "og")
                kcf = K.sb(es, [128, 32, 128], F32, "kcf"); vcf = K.sb(es, [128, 32, 128], F32, "vcf")
                kcb = K.sb(es, [128, 32, 128], BF16, "kcb"); vcb = K.sb(es, [128, 32, 128], BF16, "vcb")
                kcT = K.sb(es, [128, T], BF16, "kcT")
                z_r = K.ring(es, 3, [128, 512], F32, "z", psum=True); cum_r = K.ring(es, 2, [128, 512], F32, "cum", psum=True)
                o_r = K.ring(es, 2, [128, 512], F32, "o", psum=True); tpB_r = K.ring(es, 1, [128, 8, 128], BF16, "tpB", psum=True)
                ctr = {"blk": 0, "qb": 0, "tp": 0}

                def head_loads(h):
                    qh = qh_r[h % 2]; kh = kh_r[h % 2]; vh = vh_r[h % 2]; sgh = sgh_r[h % 2]
                    K.dma("sp", qh.t[:], qT_s[h], [], [qh.r], "qh%d" % (h % 2))
                    K.dma("sp", kh.t[:], kT_s[h], [], [kh.r], "kh%d" % (h % 2))
                    K.dma("sp", vh.t[:, 0:32, :], v_s[0:T, h * 128:(h + 1) * 128].rearrange("(t p) d -> p t d", p=128),
                          [], [vh.r], "vh%d" % (h % 2))
                    K.dma("sp", vh.t[0:TS, 32, :], v_s[T:NT, h * 128:(h + 1) * 128], [], [vh.r], "vh%d" % (h % 2))
                    K.dma("sp", sgh.t[:], sgT_s[h], [], [sgh.r], "sgh%d" % (h % 2))

                def cache_prep(h):
                    K.dma("sp", kcf.t[:], ck[:, h, :].rearrange("(t p) d -> p t d", p=128), [], [kcf.r], "kcf")
                    K.dma("sp", vcf.t[:], cv[:, h, :].rearrange("(t p) d -> p t d", p=128), [], [vcf.r], "vcf")
                    K.cp("pool", kcb.t[:], kcf.t[:], [kcf.r], [kcb.r])
                    K.cp("pool", vcb.t[:], vcf.t[:], [vcf.r], [vcb.r])
                    for gq in range(8):
                        tp = tpB_r[0]
                        for i4 in range(4):
                            K.tr(tp.t[:, i4, :], kcb.t[:, gq * 4 + i4, :], identb, [kcb.r, cb.r], [tp.r])
                        K.cp("dve", kcT.t[:, gq * 512:(gq + 1) * 512], tp.t[:, 0:4, :], [tp.r], [kcT.r])

                items = []
                for h in range(H):
                    qh = qh_r[h % 2]; kh = kh_r[h % 2]; vh = vh_r[h % 2]; sgh = sgh_r[h % 2]
                    qbs = []
                    for i in range(NBLK):
                        q0 = i * 512
                        kl = []
                        for kb in range(4 * i + 3, -1, -1):
                            m = kb - 4 * i
                            off = 128 * m if m > 0 else 0
                            kl.append((kh.t[:, kb * 128:(kb + 1) * 128], vh.t[:, kb, :], [kh.r, vh.r], 128, off, m >= 0))
                        qbs.append((q0, 512, kl, ogT_s[h, :, q0:q0 + 512]))
                    kl = [(kh.t[:, T:NT], vh.t[0:TS, 32, :], [kh.r, vh.r], TS, 0, True)]
                    for kb in range(31, -1, -1):
                        kl.append((kcT.t[:, kb * 128:(kb + 1) * 128], vcb.t[:, kb, :], [kcT.r, vcb.r], 128, 0, False))
                    qbs.append((T, TS, kl, ogT_s[h, :, T:NT]))
                    for qi, (q0, nq, kl, out_dram) in enumerate(qbs):
                        qb = {"q0": q0, "nq": nq, "out": out_dram, "qh": qh, "sgh": sgh, "n": len(kl), "slot": None}
                        for idx, (kT_ap, v_ap, rds, nk, off, diag) in enumerate(kl):
                            items.append({"qb": qb, "idx": idx, "kT": kT_ap, "v": v_ap, "rds": rds, "nk": nk, "off": off, "diag": diag,
                                          "hstart": h if (qi == 0 and idx == 0) else None})

                def s0(it):
                    qb = it["qb"]; nq = qb["nq"]; q0 = qb["q0"]; off = it["off"]; nk = it["nk"]; n = nq - off
                    if it["idx"] == 0:
                        s = ctr["qb"] % 2; ctr["qb"] += 1
                        qb["R"] = [R_r[2 * s], R_r[2 * s + 1]]; qb["o"] = o_r[s]; qb["og"] = og_r[s]; qb["s"] = s
                        K.memset("pool", qb["R"][0].t[:, 0:nq], 0.0, [qb["R"][0].r])
                        K.memset("pool", qb["R"][1].t[:, 0:nq], 0.0, [qb["R"][1].r])
                    j = ctr["blk"]; ctr["blk"] += 1
                    it["z"] = z_r[j % 3]; it["e"] = e_r[j % 4]; it["sp"] = spb_r[j % 3]; it["cum"] = cum_r[j % 2]
                    it["a"] = a_r[j % 3]
                    zb = it["z"]
                    K.mm(zb.t[0:nk, 0:n], it["kT"], qb["qh"].t[:, q0 + off:q0 + nq], True, True, it["rds"] + [qb["qh"].r], [zb.r])

                def s1(it):
                    qb = it["qb"]; nq = qb["nq"]; off = it["off"]; nk = it["nk"]; n = nq - off
                    zb = it["z"]; eb = it["e"]
                    K.act(eb.t[0:nk, 0:n], zb.t[0:nk, 0:n], AF.Exp, [zb.r], [eb.r])
                    if it["diag"]:
                        w = min(128, n)
                        K.tt("dve", eb.t[0:nk, 0:w], eb.t[0:nk, 0:w], maskST[0:nk, 0:w], ALU.mult, [eb.r, cf.r], [eb.r])

                def s2(it):
                    qb = it["qb"]; nq = qb["nq"]; off = it["off"]; nk = it["nk"]; n = nq - off
                    K.act(it["sp"].t[0:nk, 0:n], it["e"].t[0:nk, 0:n], AF.Ln, [it["e"].r], [it["sp"].r], bias=1.0)

                def s3(it):
                    qb = it["qb"]; nq = qb["nq"]; q0 = qb["q0"]; off = it["off"]; nk = it["nk"]; n = nq - off
                    sb_ = it["sp"]; cb_ = it["cum"]; Rb = qb["R"][it["idx"] % 2]; Rn = qb["R"][(it["idx"] + 1) % 2]
                    K.mm(cb_.t[0:nk, 0:n], lmat_n[0:nk, 0:nk], sb_.t[0:nk, 0:n], True, False, [cb.r, sb_.r], [cb_.r])
                    K.mm(cb_.t[0:nk, 0:n], ones_n[:, 0:nk], Rb.t[:, off:nq], False, False, [cb.r, Rb.r], [cb_.r])
                    K.mm(cb_.t[0:nk, 0:n], it["kT"], qb["qh"].t[:, q0 + off:q0 + nq], False, True, it["rds"] + [qb["qh"].r], [cb_.r])
                    if it["idx"] < qb["n"] - 1:
                        if nk < 128:
                            K.tt("pool", Rn.t[0:nk, off:nq], Rb.t[0:nk, off:nq], sb_.t[0:nk, 0:n], ALU.add, [Rb.r, sb_.r], [Rn.r])
                        else:
                            K.tt("pool", Rn.t[:, off:nq], Rb.t[:, off:nq], sb_.t[:, 0:n], ALU.add, [Rb.r, sb_.r], [Rn.r])

                def s4(it):
                    qb = it["qb"]; nq = qb["nq"]; off = it["off"]; nk = it["nk"]; n = nq - off
                    ab = it["a"]
                    if it["idx"] == 0 and off > 0:
                        K.memset("pool", ab.t[0:nk, 0:off], 0.0, [ab.r])
                        K.act(ab.t[0:nk, off:nq], it["cum"].t[0:nk, 0:n], AF.Exp, [it["cum"].r], [ab.r])
                    else:
                        K.act(ab.t[0:nk, 0:n], it["cum"].t[0:nk, 0:n], AF.Exp, [it["cum"].r], [ab.r])

                def s5(it):
                    qb = it["qb"]; nq = qb["nq"]; off = it["off"]; nk = it["nk"]; n = nq - off
                    ab = it["a"]
                    if it["diag"]:
                        w = min(128, n)
                        c0 = off if (it["idx"] == 0 and off > 0) else 0
                        K.tt("dve", ab.t[0:nk, c0:c0 + w], ab.t[0:nk, c0:c0 + w], maskST[0:nk, 0:w], ALU.mult, [ab.r, cf.r], [ab.r])

                def s6(it):
                    qb = it["qb"]; nq = qb["nq"]; off = it["off"]; nk = it["nk"]; n = nq - off
                    ab = it["a"]; ob = qb["o"]; idx = it["idx"]; nblk = qb["n"]
                    if idx == 0 and off > 0:
                        K.mm(ob.t[:, 0:nq], it["v"], ab.t[0:nk, 0:nq], True, nblk == 1, it["rds"] + [ab.r], [ob.r])
                    else:
                        K.mm(ob.t[:, off:nq], it["v"], ab.t[0:nk, 0:n], idx == 0, idx == nblk - 1, it["rds"] + [ab.r], [ob.r])

                def s7(it):
                    qb = it["qb"]; nq = qb["nq"]; q0 = qb["q0"]
                    if it["idx"] == qb["n"] - 1:
                        ogb = qb["og"]; ob = qb["o"]
                        K.tt("dve", ogb.t[:, 0:nq], ob.t[:, 0:nq], qb["sgh"].t[:, q0:q0 + nq], ALU.mult, [ob.r, qb["sgh"].r], [ogb.r])
                        K.dma("sp", qb["out"], ogb.t[:, 0:nq], [ogb.r], [], "og%d" % qb["s"])

                stages = [s0, s1, s2, s3, s4, s5, s6, s7]
                NS = len(stages)
                head_loads(0)
                nit = len(items)
                for i in range(nit + NS):
                    k = i - NS
                    if 0 <= k < nit and items[k]["hstart"] is not None:
                        hh_ = items[k]["hstart"]
                        if hh_ + 1 < H:
                            head_loads(hh_ + 1)
                        cache_prep(hh_)
                    for si, fn in enumerate(stages):
                        if 0 <= i - si < nit:
                            fn(items[i - si])
                P.barrier()

        if STAGE >= 3:
            with ExitStack() as es:
                wo = K.sb(es, [128, 16, D], BF16, "wo")
                ogb_r = K.ring(es, 2, [128, 16, 512], BF16, "ogblk")
                xt_r = K.ring(es, 2, [128, D], F32, "xtC"); hb_r = K.ring(es, 2, [128, D], BF16, "hbC")
                ss_r = K.ring(es, 2, [128, 4], F32, "ssC"); tmp_r = K.ring(es, 2, [128, 512], F32, "tmpC")
                h1st_r = K.ring(es, 2, [128, 16, 128], BF16, "h1st")
                y_r = K.ring(es, 4, [128, 512], F32, "yC", psum=True)
                tp_r = K.ring(es, 2, [128, 8, 128], BF16, "tpC", psum=True)
                tpi = [0]; ci = {"x": 0, "y": 0, "t": 0}
                K.dma("sp", wo.t[:], wo0_s.rearrange("(kc p) n -> p kc n", p=128), R_wo0, [wo.r], "woC")
                ctiles = [(tok0 + t * 128, min(128, ntokb - t * 128)) for (tok0, ntokb, g) in blocks for t in range((ntokb + 127) // 128)]

                def c_xload(k):
                    if k < len(ctiles):
                        K.dma("sp", xt_r[k % 2].t[0:ctiles[k][1], :], xrows(ctiles[k][0], ctiles[k][1]), [], [xt_r[k % 2].r], "xtC%d" % (k % 2))

                def c_ogload(bi):
                    if bi < len(blocks):
                        tok0_, ntokb_, g_ = blocks[bi]
                        K.dma("sp", ogb_r[bi % 2].t[:, :, 0:ntokb_], ogT_s[:, :, tok0_:tok0_ + ntokb_].rearrange("h d t -> d h t"), [],
                              [ogb_r[bi % 2].r], "ogblk%d" % (bi % 2))

                c_ogload(0); c_xload(0)
                pend = [None]
                for bi, (tok0, ntokb, g) in enumerate(blocks):
                    ogb = ogb_r[bi % 2]
                    c_ogload(bi + 1)
                    ntl = (ntokb + 127) // 128
                    for t in range(ntl):
                        ntok = min(128, ntokb - t * 128)
                        s = ci["x"] % 2; ci["x"] += 1
                        xt = xt_r[s]; hb = hb_r[s]; ssb = ss_r[s]; h1st = h1st_r[s]
                        c_xload(ci["x"])
                        for n in range(4):
                            y = y_r[ci["y"] % 4]; ci["y"] += 1
                            tmp = tmp_r[ci["t"] % 2]; ci["t"] += 1
                            for kc in range(16):
                                K.mm(y.t[0:ntok, :], ogb.t[:, kc, t * 128:t * 128 + ntok], wo.t[:, kc, n * 512:(n + 1) * 512], kc == 0, kc == 15,
                                     [ogb.r, wo.r], [y.r])
                            K.tt("dve", tmp.t[0:ntok, :], y.t[0:ntok, :], gate_b[0][g].t[0:ntok, n * 512:(n + 1) * 512], ALU.mult,
                                 [y.r, gate_b[0][g].r], [tmp.r])
                            K.tt("pool", xt.t[0:ntok, n * 512:(n + 1) * 512], tmp.t[0:ntok, :], xt.t[0:ntok, n * 512:(n + 1) * 512], ALU.add,
                                 [tmp.r, xt.r], [xt.r])
                        K.dma("sp", x1_s[tok0 + t * 128:tok0 + t * 128 + ntok, :], xt.t[0:ntok, :], [xt.r], [], "x1o%d" % s)
                        norm_p1(xt.t[0:ntok, :], xt.r, ntok, hb, ssb)
                        if pend[0] is not None:
                            pend[0]()
                        def _p2(ntok=ntok, g=g, hb=hb, h1st=h1st, t=t, tok0=tok0, s=s):
                            norm_p2(ntok, 1, g, hb, tp_r, tpi, (lambda kc: h1st.t[:, kc, 0:ntok]), h1st.r, "dve" if t % 2 == 0 else "act")
                            K.dma("sp", h1T_s[:, :, tok0 + t * 128:tok0 + t * 128 + ntok].rearrange("c p t -> p c t"), h1st.t[:, :, 0:ntok],
                                  [h1st.r], [], "h1o%d" % s)
                        pend[0] = _p2
                pend[0]()
                P.barrier()

        esL0.close()
        if STAGE >= 4:
            with ExitStack() as es:
                h1b_r = K.ring(es, 2, [128, 16, 512], BF16, "h1b")
                wa2_b = K.sb(es, [16, 1024], BF16, "wa2b"); ba_b = K.sb(es, [1, 1024], BF16, "bab")
                K.dma("pool", wa2_b.t[:], wa2[:, :], [], [wa2_b.r], "wa2")
                K.dma("pool", ba_b.t[:], ba[:, :], [], [ba_b.r], "bab")
                wt_r = K.ring(es, 3, [128, 16, 512], BF16, "wtD")
                wta = K.sb(es, [128, 16, 16], BF16, "wta")
                alrT = K.sb(es, [16, 512], BF16, "alrT")
                eg_r = K.ring(es, 2, [128, 1024], F32, "eg")
                EbT = K.sb(es, [128, 8, 512], F32, "EbT"); EnbT = K.sb(es, [128, 8, 512], F32, "EnbT")
                st_r = K.ring(es, 2, [128, 512], BF16, "stD"); stmp_r = K.ring(es, 2, [128, 512], F32, "stmpD")
                kdst_r = K.ring(es, 2, [128, 4, 1024], BF16, "kdst")
                ps_r = K.ring(es, 4, [128, 512], F32, "psD", psum=True)
                bT_r = K.ring(es, 2, [128, 4, 128], F32, "bT", psum=True)
                tp_r = K.ring(es, 2, [128, 8, 128], BF16, "tpD", psum=True)
                ci = {"wt": 0, "ps": 0, "st": 0, "tp": 0, "eg": 0, "bT": 0}
                K.dma("sp", wta.t[:], wta_s[:, :, :], [R_wg[12]], [wta.r], "wta")
                tile_idx = 0
                def d_hload(bi):
                    if bi < len(blocks):
                        tok0_, ntokb_, g_ = blocks[bi]
                        K.dma("sp", h1b_r[bi % 2].t[:, :, 0:ntokb_], h1T_s[:, :, tok0_:tok0_ + ntokb_].rearrange("c p t -> p c t"), [],
                              [h1b_r[bi % 2].r], "h1b%d" % (bi % 2))

                def d_wload(si):
                    if si < 12 * len(blocks):
                        wc_ = si % 12
                        K.dma("sp", wt_r[si % 3].t[:], wg_s[wc_], [R_wg[wc_]], [wt_r[si % 3].r], "wtD%d" % (si % 3))

                d_hload(0); d_wload(0); d_wload(1)
                for bi, (tok0, ntokb, g) in enumerate(blocks):
                    h1b = h1b_r[bi % 2]
                    d_hload(bi + 1)
                    ntl = (ntokb + 127) // 128
                    ps = ps_r[ci["ps"] % 4]; ci["ps"] += 1
                    for kc in range(16):
                        K.mm(ps.t[0:16, 0:ntokb], wta.t[:, kc, :], h1b.t[:, kc, 0:ntokb], kc == 0, kc == 15, [wta.r, h1b.r], [ps.r])
                    K.cp("act", alrT.t[:, 0:ntokb], ps.t[0:16, 0:ntokb], [ps.r], [alrT.r])
                    for t in range(ntl):
                        ntok = min(128, ntokb - t * 128)
                        eg = eg_r[ci["eg"] % 2]; ci["eg"] += 1
                        for n in range(2):
                            ps = ps_r[ci["ps"] % 4]; ci["ps"] += 1
                            K.mm(ps.t[0:ntok, :], alrT.t[0:16, t * 128:t * 128 + ntok], wa2_b.t[0:16, n * 512:(n + 1) * 512], True, False,
                                 [alrT.r, wa2_b.r], [ps.r])
                            K.mm(ps.t[0:ntok, :], ones_b[0:1, 0:ntok], ba_b.t[0:1, n * 512:(n + 1) * 512], False, True, [cb.r, ba_b.r], [ps.r])
                            K.act(eg.t[0:ntok, n * 512:(n + 1) * 512], ps.t[0:ntok, :], AF.Exp, [ps.r], [eg.r], scale=-1.0)
                        K.act(eg.t[0:ntok, :], eg.t[0:ntok, :], AF.Ln, [eg.r], [eg.r], bias=1.0)
                        for half in range(2):
                            bT = bT_r[ci["bT"] % 2]; ci["bT"] += 1
                            for i4 in range(4):
                                dc = half * 4 + i4
                                K.mm(bT.t[:, i4, 0:ntok], eg.t[0:ntok, dc * 128:(dc + 1) * 128], uincl[0:ntok, 0:ntok], True, True,
                                     [eg.r, cf.r], [bT.r])
                            K.act(EbT.t[:, half * 4:half * 4 + 4, t * 128:t * 128 + ntok], bT.t[:, :, 0:ntok], AF.Exp, [bT.r], [EbT.r])
                            K.act(EnbT.t[:, half * 4:half * 4 + 4, t * 128:t * 128 + ntok], bT.t[:, :, 0:ntok], AF.Exp, [bT.r], [EnbT.r], scale=-1.0)
                        K.cp("dve", dec_all.t[:, tile_idx, :], EbT.t[:, :, t * 128 + ntok - 1], [EbT.r], [dec_all.r])
                        tile_idx += 1
                    kdst = kdst_r[bi % 2]
                    for wc in range(12):
                        si = bi * 12 + wc
                        wt = wt_r[si % 3]
                        d_wload(si + 2)
                        if wc < 4:
                            for i4 in range(4):
                                dc = (wc % 2) * 4 + i4
                                ps = ps_r[ci["ps"] % 4]; ci["ps"] += 1
                                for kc in range(16):
                                    K.mm(ps.t[:, 0:ntokb], wt.t[:, kc, i4 * 128:(i4 + 1) * 128], h1b.t[:, kc, 0:ntokb],
                                         kc == 0, kc == 15, [wt.r, h1b.r], [ps.r])
                                st = st_r[ci["st"] % 2]; ci["st"] += 1
                                if wc < 2:
                                    K.stt(st.t[:, 0:ntokb], ps.t[:, 0:ntokb], 256 ** -0.5, EbT.t[:, dc, 0:ntokb], ALU.mult, ALU.mult,
                                          [ps.r, EbT.r], [st.r])
                                    K.dma("sp", qinT_s[dc, :, tok0:tok0 + ntokb], st.t[:, 0:ntokb], [st.r], [], "stD%d" % (ci["st"] % 2))
                                else:
                                    K.tt("dve", st.t[:, 0:ntokb], ps.t[:, 0:ntokb], EnbT.t[:, dc, 0:ntokb], ALU.mult, [ps.r, EnbT.r], [st.r])
                                    K.dma("sp", kdT_s[dc, :, tok0:tok0 + ntokb], st.t[:, 0:ntokb], [st.r], [], "stD%d" % (ci["st"] % 2))
                                    for t in range(ntl):
                                        ntok = min(128, ntokb - t * 128)
                                        tp = tp_r[ci["tp"] % 2]; ci["tp"] += 1
                                        K.tr(tp.t[0:ntok, 0, :], st.t[:, t * 128:t * 128 + ntok], identb, [st.r, cb.r], [tp.r])
                                        K.cp("act", kdst.t[0:ntok, t, dc * 128:(dc + 1) * 128], tp.t[0:ntok, 0, :], [tp.r], [kdst.r])
                            if wc == 3:
                                for t in range(ntl):
                                    ntok = min(128, ntokb - t * 128)
                                    K.dma("sp", kd_s[tok0 + t * 128:tok0 + t * 128 + ntok, :], kdst.t[0:ntok, t, :], [kdst.r], [],
                                          "kdst%d" % (bi % 2))
                        else:
                            isv = wc < 8
                            col0 = ((wc - 4) % 4) * 512
                            for t in range(ntl):
                                ntok = min(128, ntokb - t * 128)
                                ps = ps_r[ci["ps"] % 4]; ci["ps"] += 1
                                for kc in range(16):
                                    K.mm(ps.t[0:ntok, :], h1b.t[:, kc, t * 128:t * 128 + ntok], wt.t[:, kc, :], kc == 0, kc == 15, [wt.r, h1b.r], [ps.r])
                                st = st_r[ci["st"] % 2]; stmp = stmp_r[ci["st"] % 2]; ci["st"] += 1
                                if isv:
                                    K.cp("act", st.t[0:ntok, :], ps.t[0:ntok, :], [ps.r], [st.r])
                                    K.dma("sp", vg_s[tok0 + t * 128:tok0 + t * 128 + ntok, col0:col0 + 512], st.t[0:ntok, :], [st.r], [],
                                          "stD%d" % (ci["st"] % 2))
                                else:
                                    silu_from_psum(ps.t[0:ntok, :], ps.r, ntok, 512, st.t[0:ntok, :], st.r, stmp)
                                    K.dma("sp", sr_s[tok0 + t * 128:tok0 + t * 128 + ntok, col0:col0 + 512], st.t[0:ntok, :], [st.r], [],
                                          "stD%d" % (ci["st"] % 2))
                P.barrier()

            with ExitStack() as es:
                wo = K.sb(es, [128, 16, D], BF16, "wo1")
                S = K.sb(es, [128, 8, 512], F32, "S"); Sb = K.sb(es, [128, 8, 512], BF16, "Sb")
                qin_r = K.ring(es, 2, [128, 8, 128], BF16, "qin"); kdT_r = K.ring(es, 2, [128, 8, 128], BF16, "kdT")
                kd_r = K.ring(es, 2, [128, 1024], BF16, "kd"); vg_r = K.ring(es, 2, [128, D], BF16, "vg")
                sr_r = K.ring(es, 1, [128, D], BF16, "sr"); x1_r = K.ring(es, 3, [128, D], F32, "x1")
                gsr_r = K.ring(es, 1, [128, D], F32, "gsr"); aT_r = K.ring(es, 4, [128, 128], BF16, "aT")
                osb_r = K.ring(es, 2, [128, 512], F32, "osb"); ss_r = K.ring(es, 8, [128, 4], F32, "ssD")
                og1_r = K.ring(es, 2, [128, D], BF16, "og1"); og1T_r = K.ring(es, 1, [128, 16, 128], BF16, "og1T")
                tmp_r = K.ring(es, 2, [128, 512], F32, "tmpE")
                aps_r = K.ring(es, 1, [128, 512], F32, "aps", psum=True)
                ops_r = K.ring(es, 2, [128, 512], F32, "ops", psum=True)
                sps_r = K.ring(es, 2, [128, 512], F32, "sps", psum=True)
                tp_r = K.ring(es, 1, [128, 8, 128], BF16, "tpE", psum=True)
                y_r = K.ring(es, 2, [128, 512], F32, "yE", psum=True)
                ci = {"ops": 0, "sps": 0, "ss": 0, "y": 0, "t": 0, "osb": 0, "aT": 0}
                S_res = [Res("S%d" % i) for i in range(8)]; Sb_res = [Res("Sb%d" % i) for i in range(8)]
                K.dma("sp", wo.t[:], wo1_s.rearrange("(kc p) n -> p kc n", p=128), R_wo1, [wo.r], "woE")
                K.memset("pool", S.t[:], 0.0, S_res)
                K.memset("pool", Sb.t[:], 0.0, Sb_res)
                tiles = [(t * 128, 128, 0) for t in range(NBLK * 4)] + [(T, TS, 1)]

                def e_loads(k):
                    if k >= len(tiles):
                        return
                    tok0_, ntok_, g_ = tiles[k]
                    s_ = k % 2
                    K.dma("sp", qin_r[s_].t[:, :, 0:ntok_], qinT_s[:, :, tok0_:tok0_ + ntok_].rearrange("c p t -> p c t"), [], [qin_r[s_].r], "qin%d" % s_)
                    K.dma("sp", kdT_r[s_].t[:, :, 0:ntok_], kdT_s[:, :, tok0_:tok0_ + ntok_].rearrange("c p t -> p c t"), [], [kdT_r[s_].r], "kdT%d" % s_)
                    K.dma("sp", kd_r[s_].t[0:ntok_, :], kd_s[tok0_:tok0_ + ntok_, :], [], [kd_r[s_].r], "kd%d" % s_)
                    K.dma("sp", vg_r[s_].t[0:ntok_, :], vg_s[tok0_:tok0_ + ntok_, :], [], [vg_r[s_].r], "vg%d" % s_)
                    K.dma("sp", x1_r[k % 3].t[0:ntok_, :], x1_s[tok0_:tok0_ + ntok_, :], [], [x1_r[k % 3].r], "x1i%d" % (k % 3))

                def e_srload(k):
                    if k < len(tiles):
                        tok0_, ntok_, g_ = tiles[k]
                        K.dma("sp", sr_r[0].t[0:ntok_, :], sr_s[tok0_:tok0_ + ntok_, :], [], [sr_r[0].r], "sr0")

                def e_xyz(ti):
                    tok0, ntok, g = tiles[ti]
                    s = ti % 2
                    qin = qin_r[s]; kdT = kdT_r[s]; kd = kd_r[s]; vg = vg_r[s]; sr = sr_r[0]; gsr = gsr_r[0]; og1 = og1_r[s]
                    K.tt("pool", gsr.t[0:ntok, :], sr.t[0:ntok, :], glag_b.t[0:ntok, :], ALU.mult, [sr.r, glag_b.r], [gsr.r])
                    e_srload(ti + 1)
                    aTs = []
                    for h in range(4):
                        aps = aps_r[0]; aT = aT_r[h]
                        for dc in range(2):
                            K.mm(aps.t[0:ntok, 0:ntok], kdT.t[:, h * 2 + dc, 0:ntok], qin.t[:, h * 2 + dc, 0:ntok], dc == 0, dc == 1,
                                 [kdT.r, qin.r], [aps.r])
                        K.tt("dve", aT.t[0:ntok, 0:ntok], aps.t[0:ntok, 0:ntok], maskLE[0:ntok, 0:ntok], ALU.mult, [aps.r, cf.r], [aT.r])
                        aTs.append(aT)
                    for h in range(4):
                        aT = aTs[h]
                        ops = ops_r[ci["ops"] % 2]; ci["ops"] += 1
                        K.mm(ops.t[0:ntok, :], aT.t[0:ntok, 0:ntok], vg.t[0:ntok, h * 512:(h + 1) * 512], True, False, [aT.r, vg.r], [ops.r])
                        for dc in range(2):
                            K.mm(ops.t[0:ntok, :], qin.t[:, h * 2 + dc, 0:ntok], Sb.t[:, h * 2 + dc, :], False, dc == 1,
                                 [qin.r, Sb_res[h * 2 + dc]], [ops.r])
                        osb = osb_r[ci["osb"] % 2]; ci["osb"] += 1
                        ssb = ss_r[ci["ss"] % 8]; ci["ss"] += 1
                        K.cp("act", osb.t[0:ntok, :], ops.t[0:ntok, :], [ops.r], [osb.r])
                        K.ttr(junk.t[0:ntok, 0:512], osb.t[0:ntok, :], osb.t[0:ntok, :], ssb.t[0:ntok, 0:1], [osb.r], [ssb.r])
                        K.act(ssb.t[0:ntok, 1:2], ssb.t[0:ntok, 0:1], AF.Ln, [ssb.r], [ssb.r], bias=EPS, scale=1.0 / 512)
                        K.act(ssb.t[0:ntok, 2:3], ssb.t[0:ntok, 1:2], AF.Exp, [ssb.r], [ssb.r], scale=-0.5)
                        K.stt(og1.t[0:ntok, h * 512:(h + 1) * 512], osb.t[0:ntok, :], ssb.t[0:ntok, 2:3], gsr.t[0:ntok, h * 512:(h + 1) * 512],
                              ALU.mult, ALU.mult, [osb.r, ssb.r, gsr.r], [og1.r])
                    for h in range(4):
                        for dc in range(2):
                            hd = h * 2 + dc
                            sps = sps_r[ci["sps"] % 2]; ci["sps"] += 1
                            K.mm(sps.t[:, :], kd.t[0:ntok, hd * 128:(hd + 1) * 128], vg.t[0:ntok, h * 512:(h + 1) * 512], True, True,
                                 [kd.r, vg.r], [sps.r])
                            K.tt("dve", S.t[:, hd, :], sps.t[:, :], S.t[:, hd, :], ALU.add, [sps.r, S_res[hd]], [S_res[hd]])
                            K.act(S.t[:, hd, :], S.t[:, hd, :], AF.Copy, [S_res[hd], dec_all.r], [S_res[hd]], scale=dec_all.t[:, ti, hd:hd + 1])
                            K.cp("pool", Sb.t[:, hd, :], S.t[:, hd, :], [S_res[hd]], [Sb_res[hd]])

                def e_w(ti):
                    tok0, ntok, g = tiles[ti]
                    og1 = og1_r[ti % 2]; og1T = og1T_r[0]; x1 = x1_r[ti % 3]
                    for grp in range(4):
                        tp = tp_r[0]
                        for i4 in range(4):
                            kc = grp * 4 + i4
                            K.tr(tp.t[:, i4, 0:ntok], og1.t[0:ntok, kc * 128:(kc + 1) * 128], identb[0:ntok, 0:ntok], [og1.r, cb.r], [tp.r])
                        K.cp("act" if grp % 2 else "dve", og1T.t[:, grp * 4:grp * 4 + 4, 0:ntok], tp.t[:, 0:4, 0:ntok], [tp.r], [og1T.r])
                    for n in range(4):
                        y = y_r[ci["y"] % 2]; ci["y"] += 1
                        tmp = tmp_r[ci["t"] % 2]; ci["t"] += 1
                        for kc in range(16):
                            K.mm(y.t[0:ntok, :], og1T.t[:, kc, 0:ntok], wo.t[:, kc, n * 512:(n + 1) * 512], kc == 0, kc == 15, [og1T.r, wo.r], [y.r])
                        K.tt("dve", tmp.t[0:ntok, :], y.t[0:ntok, :], gate_b[1][g].t[0:ntok, n * 512:(n + 1) * 512], ALU.mult,
                             [y.r, gate_b[1][g].r], [tmp.r])
                        K.tt("pool", x1.t[0:ntok, n * 512:(n + 1) * 512], tmp.t[0:ntok, :], x1.t[0:ntok, n * 512:(n + 1) * 512], ALU.add,
                             [tmp.r, x1.r], [x1.r])
                    ssb = ss_r[ci["ss"] % 8]; ci["ss"] += 1
                    K.ttr(junk.t[0:ntok, :], x1.t[0:ntok, :], x1.t[0:ntok, :], ssb.t[0:ntok, 0:1], [x1.r], [ssb.r])
                    K.act(ssb.t[0:ntok, 1:2], ssb.t[0:ntok, 0:1], AF.Ln, [ssb.r], [ssb.r], bias=EPS, scale=1.0 / D)
                    K.act(ssb.t[0:ntok, 2:3], ssb.t[0:ntok, 1:2], AF.Exp, [ssb.r], [ssb.r], scale=-0.5)
                    K.stt(x1.t[0:ntok, :], x1.t[0:ntok, :], ssb.t[0:ntok, 2:3], fing_b.t[0:ntok, :], ALU.mult, ALU.mult,
                          [x1.r, ssb.r, fing_b.r], [x1.r])
                    dst = yp[tok0:tok0 + ntok, :] if g == 0 else ys[0:ntok, :]
                    K.dma("sp", dst, x1.t[0:ntok, :], [x1.r], [Res()], "yo%d" % (ti % 3))

                e_loads(0); e_srload(0)
                for ti, (tok0, ntok, g) in enumerate(tiles):
                    if g == 1:
                        K.dma("sp", sp_o.rearrange("h (c p) e -> p (h c) e", p=128), S.t[:], S_res, [Res()], "Sout")
                        K.dma("sp", S.t[:], sg_in.rearrange("h (c p) e -> p (h c) e", p=128), [], S_res, "Sin")
                        K.cp("pool", Sb.t[:], S.t[:], S_res, Sb_res)
                    e_loads(ti + 1)
                    e_xyz(ti)
                    if ti >= 1:
                        e_w(ti - 1)
                e_w(len(tiles) - 1)
                K.dma("sp", ss_o.rearrange("h (c p) e -> p (h c) e", p=128), S.t[:], S_res, [Res()], "Sout2")
                P.barrier()

        P.barrier()
        for e in ENG:
            if e == "pe":
                continue
            P.op(e, lambda eng: eng.nop(), (), ())
        with ExitStack() as es2:
            P.emit(nc, es2)
    return nc


_NC = None


def _consts():
    i = np.arange(128)
    cfv = np.zeros((128, NCF), np.float32)
    cfv[:, 0:128] = np.eye(128, dtype=np.float32)
    cfv[:, 128:256] = (i[:, None] < i[None, :]).astype(np.float32)
    cfv[:, 256:384] = (i[:, None] <= i[None, :]).astype(np.float32) * (-1.0 / 16.0)
    cfv[:, 384:512] = (i[:, None] <= i[None, :]).astype(np.float32)
    cfv[:, 512:640] = 1.0
    cbv = np.zeros((128, NCB), np.float32)
    cbv[:, 0:128] = np.eye(128, dtype=np.float32)
    cbv[:, 128:256] = (i[:, None] >= i[None, :]).astype(np.float32)
    cbv[:, 256:384] = 1.0
    cbv[:, 384:512] = -cbv[:, 128:256]
    cbv[:, 512:640] = -1.0
    return cfv, cbv.astype(ml_dtypes.bfloat16)


def kernel(x_prompt, x_sample, cache_sb_k, cache_sb_v, state_gla, c_prompt, c_sample,
           w_ada, b_ada, norm_g, sb_w_in, sb_w_out, gla_w_in, gla_w_a2, gla_b_a,
           gla_norm_g, gla_w_out, final_norm_g):
    global _NC
    if _NC is None:
        _NC = build()
    f = lambda a: np.ascontiguousarray(np.asarray(a), dtype=np.float32)
    cfv, cbv = _consts()
    shared = {
        "wada": f(w_ada), "bada": f(b_ada).reshape(96, 128), "brow": f(b_ada), "ng": f(norm_g).reshape(32, 128),
        "wq": f(sb_w_in)[0], "wo0": f(sb_w_out)[0], "wg": f(gla_w_in)[0], "wa2": f(gla_w_a2)[0], "ba": f(gla_b_a),
        "glag": f(gla_norm_g), "wo1": f(gla_w_out)[0], "fing": f(final_norm_g).reshape(1, D), "cf": cfv, "cb": cbv,
    }
    x_prompt = f(x_prompt); x_sample = f(x_sample); cache_sb_k = f(cache_sb_k); cache_sb_v = f(cache_sb_v)
    state_gla = f(state_gla); c_prompt = f(c_prompt); c_sample = f(c_sample)
    in_maps = []
    for b in range(8):
        m = dict(shared)
        m["xp"] = x_prompt[b]; m["xs"] = x_sample[b]
        m["ck"] = cache_sb_k[0, b]; m["cv"] = cache_sb_v[0, b]; m["sg"] = state_gla[0, b]
        m["c32"] = np.ascontiguousarray(np.stack([c_prompt[b], c_sample[b]]).reshape(32, 128))
        in_maps.append(m)
    res = run_bass_kernel_spmd(_NC, in_maps, core_ids=list(range(8)))
    r = res.results
    y_p = np.stack([r[b]["yp"] for b in range(8)])
    y_s = np.stack([r[b]["ys"] for b in range(8)])
    k_p = np.stack([r[b]["kp"].reshape(T, H, DH) for b in range(8)])[None]
    v_p = np.stack([r[b]["vp"].reshape(T, H, DH) for b in range(8)])[None]
    k_s = np.stack([r[b]["ks"].reshape(TS, H, DH) for b in range(8)])[None]
    v_s = np.stack([r[b]["vs"].reshape(TS, H, DH) for b in range(8)])[None]
    s_p = np.stack([r[b]["spo"] for b in range(8)])[None]
    s_s = np.stack([r[b]["sso"] for b in range(8)])[None]
    return (y_p, y_s, k_p, v_p, k_s, v_s, s_p, s_s)
```

```python
import numpy as np
import ml_dtypes
from contextlib import ExitStack
import concourse.bass as bass
import concourse.mybir as mybir
from concourse.bass_utils import run_bass_kernel_spmd

F32 = mybir.dt.float32
BF16 = mybir.dt.bfloat16
AF = mybir.ActivationFunctionType
ALU = mybir.AluOpType

D = 2048
T = 4096
TS = 16
NT = T + TS
H = 16
DH = 128
EPS = 1e-6
NCF = 640
NCB = 640
STAGE = 9
NBLK = 8
ENG = ["pe", "act", "dve", "pool", "sp"]
SEM_CAP = 30000


class Res:
    __slots__ = ("name", "w", "rc", "rd", "psum")

    def __init__(self, name=""):
        self.name = name
        self.psum = False
        self.w = None
        self.rc = {}
        self.rd = []


class Op:
    __slots__ = ("eng", "fn", "deps", "dma", "key", "sem", "val", "sig")


class Prog:
    def __init__(self):
        self.ops = []
        self.bar = {e: set() for e in ENG}
        self.last = {}
        self.dmas = []

    def op(self, eng, fn, reads=(), writes=(), key=None, nobar=False):
        o = Op()
        o.eng = eng
        o.fn = fn
        o.dma = key is not None
        o.key = key
        o.sig = False
        o.sem = None
        o.val = 0
        deps = set()
        for r in reads:
            if r.w is not None:
                deps.add(r.w)
            if r.psum:
                deps.update(v for k, v in r.rc.items() if k != eng)
        for w in writes:
            if w.w is not None:
                deps.add(w.w)
            deps.update(w.rc.values())
            deps.update(w.rd)
        deps |= self.bar[eng]
        self.bar[eng] = set()
        o.deps = [d for d in deps if d is not o and (d.dma or o.dma or d.eng != eng or eng != "pe")]
        for d in o.deps:
            d.sig = True
        for r in reads:
            if o.dma:
                r.rd.append(o)
            else:
                r.rc[eng] = o
        for w in writes:
            w.w = o
            w.rc = {}
            w.rd = []
        self.ops.append(o)
        self.last[eng] = o
        if o.dma and not nobar:
            self.dmas.append(o)
        return o

    def barrier(self):
        s = set(self.last.values()) | set(self.dmas)
        for e in ENG:
            self.bar[e] |= s
        self.dmas = []

    def emit(self, nc, es):
        keys = []
        for o in self.ops:
            if o.dma and o.key not in keys:
                keys.append(o.key)
        dsem = {k: es.enter_context(nc.semaphore("d_" + k)) for k in keys}
        cnt = {e: 0 for e in ENG}
        csem = {e: es.enter_context(nc.semaphore("c_" + e + "0")) for e in ENG}
        gen = {e: 0 for e in ENG}
        dcnt = {}
        for o in self.ops:
            if o.dma:
                v = dcnt.get(o.key, 0) + 16
                dcnt[o.key] = v
                o.sem = dsem[o.key]
                o.val = v
            elif o.sig:
                if cnt[o.eng] >= SEM_CAP:
                    gen[o.eng] += 1
                    csem[o.eng] = es.enter_context(nc.semaphore("c_%s%d" % (o.eng, gen[o.eng])))
                    cnt[o.eng] = 0
                cnt[o.eng] += 1
                o.sem = csem[o.eng]
                o.val = cnt[o.eng]
        ops = self.ops

        def run(engname, eng):
            waited = {}
            for o in ops:
                if o.eng != engname:
                    continue
                need = {}
                for d in o.deps:
                    k = id(d.sem)
                    if k not in need or need[k][1] < d.val:
                        need[k] = (d.sem, d.val)
                for k, (s, v) in need.items():
                    if waited.get(k, 0) < v:
                        eng.wait_ge(s, v)
                        waited[k] = v
                ins = o.fn(eng)
                if o.dma:
                    ins.then_inc(o.sem, 16)
                elif o.sig:
                    ins.then_inc(o.sem, 1)

        with nc.Block() as block:
            @block.tensor
            def _(e):
                run("pe", e)

            @block.scalar
            def _(e):
                run("act", e)

            @block.vector
            def _(e):
                run("dve", e)

            @block.gpsimd
            def _(e):
                run("pool", e)

            @block.sync
            def _(e):
                run("sp", e)


class Buf:
    __slots__ = ("t", "r")

    def __init__(self, t, name):
        self.t = t
        self.r = Res(name)


class KB:
    def __init__(self, nc):
        self.nc = nc
        self.P = Prog()
        self.uid = 0
        self.pool_dmas = []

    def sb(self, es, shape, dt, name=None):
        self.uid += 1
        nm = "%s_%d" % (name or "sb", self.uid)
        return Buf(es.enter_context(self.nc.sbuf_tensor(nm, list(shape), dt)), nm)

    def ps(self, es, shape, dt, name=None):
        self.uid += 1
        nm = "%s_%d" % (name or "ps", self.uid)
        b = Buf(es.enter_context(self.nc.psum_tensor(nm, list(shape), dt)), nm)
        b.r.psum = True
        return b

    def ring(self, es, n, shape, dt, name=None, psum=False):
        return [(self.ps if psum else self.sb)(es, shape, dt, name) for _ in range(n)]

    def mm(self, out, lhsT, rhs, start, stop, R, W):
        self.P.op("pe", lambda e: e.matmul(out, lhsT, rhs, start=start, stop=stop), R, W)

    def tr(self, out, in_, ident, R, W):
        self.P.op("pe", lambda e: e.transpose(out, in_, ident), R, W)

    def act(self, out, in_, func, R, W, bias=None, scale=None):
        kw = {}
        if bias is not None:
            kw["bias"] = bias
        if scale is not None:
            kw["scale"] = scale
        self.P.op("act", lambda e: e.activation(out=out, in_=in_, func=func, **kw), R, W)

    def tt(self, eng, out, in0, in1, op, R, W):
        self.P.op(eng, lambda e: e.tensor_tensor(out=out, in0=in0, in1=in1, op=op), R, W)

    def ts(self, eng, out, in0, s1, s2, op0, op1, R, W):
        if op1 is None:
            self.P.op(eng, lambda e: e.tensor_scalar(out=out, in0=in0, scalar1=s1, scalar2=None, op0=op0), R, W)
        else:
            self.P.op(eng, lambda e: e.tensor_scalar(out=out, in0=in0, scalar1=s1, scalar2=s2, op0=op0, op1=op1), R, W)

    def stt(self, out, in0, scalar, in1, op0, op1, R, W):
        self.P.op("dve", lambda e: e.scalar_tensor_tensor(out=out, in0=in0, scalar=scalar, in1=in1, op0=op0, op1=op1), R, W)

    def ttr(self, out, in0, in1, accum, R, W):
        self.P.op("act", lambda e: e.activation(out=out, in_=in0, func=AF.Square, accum_out=accum), R, W)

    def cp(self, eng, out, in_, R, W):
        if eng == "act":
            self.P.op("act", lambda e: e.activation(out=out, in_=in_, func=AF.Copy), R, W)
        else:
            self.P.op(eng, lambda e: e.tensor_copy(out=out, in_=in_), R, W)

    def memset(self, eng, ap, val, W):
        self.P.op(eng, lambda e: e.memset(ap, val), (), W)

    def recip(self, out, in_, R, W):
        self.P.op("dve", lambda e: e.reciprocal(out=out, in_=in_), R, W)

    def dma(self, eng, out, in_, R, W, key, nobar=False, **kw):
        if eng == "pool":
            key = "pl%d" % (len(self.pool_dmas) % 5)
        o = self.P.op(eng, lambda e: e.dma_start(out=out, in_=in_, **kw), R, W, key=key, nobar=nobar)
        if eng == "pool":
            self.pool_dmas.append(o)
            if len(self.pool_dmas) > 4:
                d = self.pool_dmas[-5]
                if d not in o.deps:
                    o.deps.append(d)
                    d.sig = True


def build():
    nc = bass.Bass("TRN2", target_bir_lowering=False)
    K = KB(nc)
    P = K.P

    def din(name, shape, dt=F32):
        return nc.dram_tensor(name, list(shape), dt, kind="ExternalInput").ap()

    def dout(name, shape):
        return nc.dram_tensor(name, list(shape), F32, kind="ExternalOutput").ap()

    def dscr(name, shape, dt):
        return nc.dram_tensor(name, list(shape), dt, kind="Internal").ap()

    xp = din("xp", [T, D]); xs = din("xs", [TS, D])
    ck = din("ck", [T, H, DH]); cv = din("cv", [T, H, DH]); sg_in = din("sg", [4, 256, 512])
    c32 = din("c32", [32, 128]); wada = din("wada", [2, D, 3 * D]); bada = din("bada", [96, 128])
    brow = din("brow", [2, 3 * D]); ng = din("ng", [32, 128])
    wq = din("wq", [D, 4 * D]); wo0 = din("wo0", [D, D]); wg = din("wg", [D, 6160])
    wa2 = din("wa2", [16, 1024]); ba = din("ba", [1, 1024]); glag = din("glag", [1, D])
    wo1 = din("wo1", [D, D]); fing = din("fing", [1, D])
    cf_d = din("cf", [128, NCF]); cb_d = din("cb", [128, NCB], BF16)

    yp = dout("yp", [T, D]); ys = dout("ys", [TS, D]); kp = dout("kp", [T, D]); vp = dout("vp", [T, D])
    ks = dout("ks", [TS, D]); vs = dout("vs", [TS, D]); sp_o = dout("spo", [4, 256, 512]); ss_o = dout("sso", [4, 256, 512])

    wq_s = dscr("wq_s", [16, 128, 16, 512], BF16); wo0_s = dscr("wo0_s", [D, D], BF16)
    wg_s = dscr("wg_s", [12, 128, 16, 512], BF16); wta_s = dscr("wta_s", [128, 16, 16], BF16); wo1_s = dscr("wo1_s", [D, D], BF16)
    qT_s = dscr("qT_s", [H, DH, NT], BF16); kT_s = dscr("kT_s", [H, DH, NT], BF16)
    sgT_s = dscr("sgT_s", [H, DH, NT], BF16); v_s = dscr("v_s", [NT, D], BF16)
    ogT_s = dscr("ogT_s", [H, DH, NT], BF16)
    x1_s = dscr("x1_s", [NT, D], F32); h1T_s = dscr("h1T_s", [16, 128, NT], BF16)
    qinT_s = dscr("qinT_s", [8, 128, NT], BF16); kdT_s = dscr("kdT_s", [8, 128, NT], BF16)
    kd_s = dscr("kd_s", [NT, 1024], BF16); vg_s = dscr("vg_s", [NT, D], BF16); sr_s = dscr("sr_s", [NT, D], BF16)

    R_wq = [Res("wq%d" % i) for i in range(16)]
    R_wo0 = [Res("wo0") for i in range(16)]; R_wg = [Res("wg") for i in range(16)]; R_wo1 = [Res("wo1") for i in range(16)]
    R_scr = {n: Res(n) for n in ["qT", "kT", "sgT", "v", "ogT", "x1", "h1T", "qinT", "kdT", "kd", "vg", "sr"]}

    top = ExitStack()
    with top:
        cf = K.sb(top, [128, NCF], F32, "cf"); cb = K.sb(top, [128, NCB], BF16, "cb")
        identf = cf.t[:, 0:128]; maskST = cf.t[:, 128:256]; uincl = cf.t[:, 256:384]; maskLE = cf.t[:, 384:512]; ones_f = cf.t[:, 512:640]
        identb = cb.t[:, 0:128]; lmat = cb.t[:, 128:256]; ones_b = cb.t[:, 256:384]
        lmat_n = cb.t[:, 384:512]; ones_n = cb.t[:, 512:640]
        esL0 = ExitStack()
        gate_b = [None, [K.sb(top, [128, D], F32, "gate") for g in range(2)]]
        glag_b = K.sb(top, [128, D], F32, "glag"); fing_b = K.sb(top, [128, D], F32, "fing")
        shiftT = [K.sb(top, [128, 16, 2], F32, "shT") for l in range(2)]
        gsT = [K.sb(top, [128, 16, 2], F32, "gsT") for l in range(2)]
        dec_all = K.sb(top, [128, 33, 8], F32, "dec")
        junk = K.sb(top, [128, D], BF16, "junk")

        K.dma("sp", cf.t[:], cf_d[:, :], [], [cf.r], "cf")
        K.dma("sp", cb.t[:], cb_d[:, :], [], [cb.r], "cb")
        K.dma("sp", glag_b.t[:], glag[0, :].partition_broadcast(128), [], [glag_b.r], "glag")
        K.dma("sp", fing_b.t[:], fing[0, :].partition_broadcast(128), [], [fing_b.r], "fing")

        gate_b[0] = [K.sb(esL0, [128, D], F32, "gate0") for g in range(2)]
        with ExitStack() as es:
            c32_t = K.sb(es, [32, 128], F32); bada_t = K.sb(es, [96, 128], F32); ng_t = K.sb(es, [32, 128], F32)
            brow_b = K.sb(es, [1, 2 * 3 * D], BF16)
            cT = K.sb(es, [128, 32], F32); cTb2 = K.sb(es, [128, 16, 2], BF16)
            cB = [K.sb(es, [128, 16, 128], BF16) for g in range(2)]
            badaT = K.sb(es, [128, 96], F32); ngT = K.sb(es, [128, 32], F32)
            tmpA = K.sb(es, [128, 16, 2], F32)
            wa_ring = K.ring(es, 2, [128, 16, 512], BF16, "wa")
            waf_ring = K.ring(es, 2, [128, 16, 512], F32, "waf")
            tps = K.ps(es, [128, 512], F32, "tps")
            adaps = K.ps(es, [128, 256, 2], F32, "adaps")
            gps = K.ring(es, 2, [128, 512], F32, "gps", psum=True)

            K.dma("sp", c32_t.t[:], c32[:, :], [], [c32_t.r], "c32")
            K.dma("sp", bada_t.t[:], bada[:, :], [], [bada_t.r], "bada")
            K.dma("sp", ng_t.t[:], ng[:, :], [], [ng_t.r], "ng")
            K.dma("pool", brow_b.t[:], brow.rearrange("l n -> (l n)").rearrange("(o n) -> o n", o=1), [], [brow_b.r], "browb",
                  max_dma_last_dim=4096)
            wq_v = wq.rearrange("(kc p) n -> p kc n", p=128)
            for wc in range(16):
                K.dma("pool", wq_s[wc], wq_v[:, :, wc * 512:(wc + 1) * 512], [], [R_wq[wc]], "pcq", nobar=True)
            K.tr(tps.t[:, 0:32], c32_t.t[:, :], identf[0:32, 0:32], [c32_t.r, cf.r], [tps.r])
            K.cp("dve", cT.t[:], tps.t[:, 0:32], [tps.r], [cT.r])
            K.tr(tps.t[:, 0:96], bada_t.t[:, :], identf[0:96, 0:96], [bada_t.r, cf.r], [tps.r])
            K.cp("dve", badaT.t[:], tps.t[:, 0:96], [tps.r], [badaT.r])
            K.tr(tps.t[:, 0:32], ng_t.t[:, :], identf[0:32, 0:32], [ng_t.r, cf.r], [tps.r])
            K.cp("dve", ngT.t[:], tps.t[:, 0:32], [tps.r], [ngT.r])
            for g in range(2):
                K.cp("dve", cTb2.t[:, :, g], cT.t[:, g * 16:(g + 1) * 16], [cT.r], [cTb2.r])
                for kc in range(16):
                    K.ts("dve", cB[g].t[:, kc, :], ones_f, cT.t[:, g * 16 + kc:g * 16 + kc + 1], None, ALU.mult, None,
                         [cT.r, cf.r], [cB[g].r])
            wav = wada.rearrange("l (kc p) n -> l p kc n", p=128)
            wi = 0
            wa_r2 = [Res("wa2a"), Res("wa2b")]

            def ada_load(k):
                if k < 24:
                    K.dma("sp", waf_ring[k % 2].t[:], wav[k // 12, :, :, (k % 12) * 512:(k % 12 + 1) * 512], [], [waf_ring[k % 2].r], "waf%d" % (k % 2))

            ada_load(0)
            for l in range(2):
                for j in range(12):
                    wa = wa_ring[wi % 2]; waf = waf_ring[wi % 2]; wi += 1
                    ada_load(wi)
                    wa2r = wa_r2[(wi - 1) % 2]
                    K.cp("dve", wa.t[:, 0:8, :], waf.t[:, 0:8, :], [waf.r], [wa.r])
                    K.cp("act", wa.t[:, 8:16, :], waf.t[:, 8:16, :], [waf.r], [wa2r])
                    if j < 8:
                        for fi in range(4):
                            fc = j * 4 + fi
                            for kc in range(16):
                                K.mm(adaps.t[:, fc, :], wa.t[:, kc, fi * 128:(fi + 1) * 128], cTb2.t[:, kc, :],
                                     kc == 0, kc == 15, [wa.r, wa2r, cTb2.r], [adaps.r])
                    else:
                        for g in range(2):
                            gp = gps[g]
                            for kc in range(16):
                                K.mm(gp.t[:, :], cB[g].t[:, kc, :], wa.t[:, kc, :], kc == 0, False, [wa.r, wa2r, cB[g].r], [gp.r])
                            K.mm(gp.t[:, :], ones_b[0:1, :], brow_b.t[0:1, l * 6144 + j * 512: l * 6144 + (j + 1) * 512],
                                 False, True, [cb.r, brow_b.r], [gp.r])
                            K.cp("act", gate_b[l][g].t[:, (j - 8) * 512:(j - 7) * 512], gp.t[:, :], [gp.r], [gate_b[l][g].r])
                    if j == 7:
                        for g in range(2):
                            K.tt("dve", shiftT[l].t[:, :, g], adaps.t[:, 0:16, g], badaT.t[:, l * 48:l * 48 + 16], ALU.add,
                                 [adaps.r, badaT.r], [shiftT[l].r])
                            K.tt("dve", tmpA.t[:, :, g], adaps.t[:, 16:32, g], badaT.t[:, l * 48 + 16:l * 48 + 32], ALU.add,
                                 [adaps.r, badaT.r], [tmpA.r])
                            K.stt(gsT[l].t[:, :, g], tmpA.t[:, :, g], 1.0, ngT.t[:, l * 16:(l + 1) * 16], ALU.add, ALU.mult,
                                  [tmpA.r, ngT.r], [gsT[l].r])
            P.barrier()
        wg_v = wg.rearrange("(kc p) n -> p kc n", p=128)
        for (src, dst, rl) in [(wo0, wo0_s, R_wo0)]:
            for rb in range(16):
                K.dma("pool", dst[rb * 128:(rb + 1) * 128, :], src[rb * 128:(rb + 1) * 128, :], [], [rl[rb]], "pc", nobar=True,
                      max_dma_last_dim=8192)
        for wc in range(12):
            K.dma("pool", wg_s[wc], wg_v[:, :, wc * 512:(wc + 1) * 512], [], [R_wg[wc]], "pcg", nobar=True)
        K.dma("pool", wta_s[:, :, :], wg_v[:, :, 6144:6160], [], [R_wg[12]], "pcg", nobar=True)
        for (src, dst, rl) in [(wo1, wo1_s, R_wo1)]:
            for rb in range(16):
                K.dma("pool", dst[rb * 128:(rb + 1) * 128, :], src[rb * 128:(rb + 1) * 128, :], [], [rl[rb]], "pc", nobar=True,
                      max_dma_last_dim=8192)

        def norm_p1(xt_ap, xt_res, ntok, hb, ssb):
            K.ttr(junk.t[0:ntok, :], xt_ap, xt_ap, ssb.t[0:ntok, 0:1], [xt_res], [ssb.r])
            K.act(ssb.t[0:ntok, 1:2], ssb.t[0:ntok, 0:1], AF.Ln, [ssb.r], [ssb.r], bias=EPS, scale=1.0 / D)
            K.act(ssb.t[0:ntok, 2:3], ssb.t[0:ntok, 1:2], AF.Exp, [ssb.r], [ssb.r], scale=-0.5)
            K.ts("dve", hb.t[0:ntok, :], xt_ap, ssb.t[0:ntok, 2:3], None, ALU.mult, None, [xt_res, ssb.r], [hb.r])

        def norm_tile(xt_ap, xt_res, ntok, l, g, hb, ssb, tpr, tpi, hT_ap_fn, hT_res, evac_eng):
            norm_p1(xt_ap, xt_res, ntok, hb, ssb)
            norm_p2(ntok, l, g, hb, tpr, tpi, hT_ap_fn, hT_res, evac_eng)

        def norm_p2(ntok, l, g, hb, tpr, tpi, hT_ap_fn, hT_res, evac_eng):
            for grp in range(4):
                tp = tpr[tpi[0] % len(tpr)]; tpi[0] += 1
                for i in range(4):
                    kc = grp * 4 + i
                    K.tr(tp.t[:, i, 0:ntok], hb.t[0:ntok, kc * 128:(kc + 1) * 128], identb[0:ntok, 0:ntok], [hb.r, cb.r], [tp.r])
                for i in range(4):
                    kc = grp * 4 + i
                    if evac_eng == "dve":
                        K.ts("dve", hT_ap_fn(kc), tp.t[:, i, 0:ntok], gsT[l].t[:, kc, g:g + 1], shiftT[l].t[:, kc, g:g + 1],
                             ALU.mult, ALU.add, [tp.r, gsT[l].r, shiftT[l].r], [hT_res])
                    else:
                        K.act(hT_ap_fn(kc), tp.t[:, i, 0:ntok], AF.Identity, [tp.r, gsT[l].r, shiftT[l].r], [hT_res],
                              bias=shiftT[l].t[:, kc, g:g + 1], scale=gsT[l].t[:, kc, g:g + 1])

        def silu_from_psum(ps_ap, ps_res, n_p, n_f, out_ap, out_res, tmp):
            K.act(tmp.t[0:n_p, 0:n_f], ps_ap, AF.Exp, [ps_res], [tmp.r], scale=-1.0)
            K.ts("dve", tmp.t[0:n_p, 0:n_f], tmp.t[0:n_p, 0:n_f], 1.0, None, ALU.add, None, [tmp.r], [tmp.r])
            K.recip(tmp.t[0:n_p, 0:n_f], tmp.t[0:n_p, 0:n_f], [tmp.r], [tmp.r])
            K.tt("dve", out_ap, ps_ap, tmp.t[0:n_p, 0:n_f], ALU.mult, [ps_res, tmp.r], [out_res])

        blocks = [(bi * 512, 512, 0) for bi in range(NBLK)] + [(T, TS, 1)]

        def xrows(tok0, n):
            return xp[tok0:tok0 + n, :] if tok0 < T else xs[tok0 - T:tok0 - T + n, :]

        with ExitStack() as es:
          if STAGE >= 1:
                xt_r = K.ring(es, 4, [128, D], F32, "xt"); hb_r = K.ring(es, 4, [128, D], BF16, "hb")
                ss_r = K.ring(es, 4, [128, 4], F32, "ss")
                hT_r = K.ring(es, 2, [128, 16, 512], BF16, "hT")
                hT_res = [[Res("hTr") for t in range(4)] for s in range(2)]
                wt_r = K.ring(es, 3, [128, 16, 512], BF16, "wt")
                qst_r = K.ring(es, 2, [128, 512], BF16, "qst"); stmp_r = K.ring(es, 1, [128, 512], F32, "stmp")
                kf_r = K.ring(es, 2, [128, 512], F32, "kf"); kb_r = K.ring(es, 2, [128, 512], BF16, "kb")
                kTst_r = K.ring(es, 1, [128, 4, 512], BF16, "kTst")
                tp_r = K.ring(es, 2, [128, 8, 128], BF16, "tp", psum=True)
                ps_r = K.ring(es, 4, [128, 512], F32, "psA", psum=True)
                tp2_r = K.ring(es, 2, [128, 8, 128], BF16, "tp2", psum=True)
                tpi = [0]; ci = {"xt": 0, "wt": 0, "ps": 0, "st": 0, "kf": 0, "kT": 0, "tp2": 0}
                def a_xload(bi):
                    tok0, ntokb, g = blocks[bi]
                    for t in range((ntokb + 127) // 128):
                        ntok = min(128, ntokb - t * 128)
                        K.dma("sp", xt_r[t].t[0:ntok, :], xrows(tok0 + t * 128, ntok), [], [xt_r[t].r], "xt%d" % t)

                def a_p1(bi):
                    tok0, ntokb, g = blocks[bi]
                    for t in range((ntokb + 127) // 128):
                        ntok = min(128, ntokb - t * 128)
                        norm_p1(xt_r[t].t[0:ntok, :], xt_r[t].r, ntok, hb_r[t], ss_r[t])

                def a_p2(bi):
                    tok0, ntokb, g = blocks[bi]
                    hTb = hT_r[bi % 2]; hTres = hT_res[bi % 2]
                    for t in range((ntokb + 127) // 128):
                        ntok = min(128, ntokb - t * 128)
                        norm_p2(ntok, 0, g, hb_r[t], tp_r, tpi,
                                (lambda kc, hTb=hTb, ntok=ntok, t=t: hTb.t[:, kc, t * 128:t * 128 + ntok]), hTres[t], "dve" if t % 2 == 0 else "act")

                steps = [(bi, wc) for bi in range(len(blocks)) for wc in range(16)]
                kpend = [None]

                def a_wload(si):
                    if si < len(steps):
                        wc = steps[si][1]
                        K.dma("sp", wt_r[si % 3].t[:], wq_s[wc], [R_wq[wc]], [wt_r[si % 3].r], "wt%d" % (si % 3))

                a_xload(0); a_p1(0); a_p2(0)
                a_wload(0); a_wload(1)
                for bi, (tok0, ntokb, g) in enumerate(blocks):
                    ntl = (ntokb + 127) // 128
                    hTb = hT_r[bi % 2]; hTres = hT_res[bi % 2]
                    hres = [hTres[t] for t in range(ntl)]
                    for wc in range(16):
                        si = bi * 16 + wc
                        wt = wt_r[si % 3]
                        a_wload(si + 2)
                        if bi + 1 < len(blocks):
                            if wc == 2:
                                a_xload(bi + 1)
                            if wc == 5:
                                a_p1(bi + 1)
                            if wc == 9:
                                a_p2(bi + 1)
                        kind = wc // 4; hbase = (wc % 4) * 4
                        if kind in (0, 3):
                            for hh in range(4):
                                ps = ps_r[ci["ps"] % 4]; ci["ps"] += 1
                                for kc in range(16):
                                    K.mm(ps.t[:, 0:ntokb], wt.t[:, kc, hh * 128:(hh + 1) * 128], hTb.t[:, kc, 0:ntokb],
                                         kc == 0, kc == 15, [wt.r] + hres, [ps.r])
                                qst = qst_r[ci["st"] % 2]; stmp = stmp_r[0]; ci["st"] += 1
                                if kind == 0:
                                    K.act(qst.t[:, 0:ntokb], ps.t[:, 0:ntokb], AF.Copy, [ps.r], [qst.r], scale=DH ** -0.5)
                                    K.dma("sp", qT_s[hbase + hh, :, tok0:tok0 + ntokb], qst.t[:, 0:ntokb], [qst.r], [],
                                          "qst%d" % (ci["st"] % 2))
                                else:
                                    silu_from_psum(ps.t[:, 0:ntokb], ps.r, 128, ntokb, qst.t[:, 0:ntokb], qst.r, stmp)
                                    K.dma("sp", sgT_s[hbase + hh, :, tok0:tok0 + ntokb], qst.t[:, 0:ntokb], [qst.r], [],
                                          "qst%d" % (ci["st"] % 2))
                        else:
                            kTst = kTst_r[0]; ci["kT"] += 1
                            for t in range(ntl):
                                ntok = min(128, ntokb - t * 128)
                                ps = ps_r[ci["ps"] % 4]; ci["ps"] += 1
                                for kc in range(16):
                                    K.mm(ps.t[0:ntok, :], hTb.t[:, kc, t * 128:t * 128 + ntok], wt.t[:, kc, :], kc == 0, kc == 15, [wt.r, hTres[t]], [ps.r])
                                kf = kf_r[ci["kf"] % 2]; kb = kb_r[ci["kf"] % 2]; ci["kf"] += 1
                                K.cp("act", kf.t[0:ntok, :], ps.t[0:ntok, :], [ps.r], [kf.r])
                                if tok0 < T:
                                    dst = (kp if kind == 1 else vp)[tok0 + t * 128:tok0 + t * 128 + ntok, hbase * 128:hbase * 128 + 512]
                                else:
                                    dst = (ks if kind == 1 else vs)[0:ntok, hbase * 128:hbase * 128 + 512]
                                K.dma("sp", dst, kf.t[0:ntok, :], [kf.r], [Res()], "kf%d" % (ci["kf"] % 2))
                                K.cp("dve", kb.t[0:ntok, :], ps.t[0:ntok, :], [ps.r], [kb.r])
                                if kind == 1:
                                    def _ktr(kb=kb, ntok=ntok, t=t, kTst=kTst):
                                        tp2 = tp2_r[ci["tp2"] % 2]; ci["tp2"] += 1
                                        for hh in range(4):
                                            K.tr(tp2.t[:, hh, 0:ntok], kb.t[0:ntok, hh * 128:(hh + 1) * 128], identb[0:ntok, 0:ntok],
                                                 [kb.r, cb.r], [tp2.r])
                                        K.cp("act" if t % 2 else "dve", kTst.t[:, :, t * 128:t * 128 + ntok], tp2.t[:, 0:4, 0:ntok], [tp2.r], [kTst.r])
                                    if kpend[0] is not None:
                                        kpend[0]()
                                    kpend[0] = _ktr
                                else:
                                    K.dma("sp", v_s[tok0 + t * 128:tok0 + t * 128 + ntok, hbase * 128:hbase * 128 + 512], kb.t[0:ntok, :],
                                          [kb.r], [], "kb%d" % (ci["kf"] % 2))
                            if kind == 1:
                                kpend[0](); kpend[0] = None
                                K.dma("sp", kT_s[hbase:hbase + 4, :, tok0:tok0 + ntokb].rearrange("h d t -> d h t"), kTst.t[:, :, 0:ntokb],
                                      [kTst.r], [], "kTst0")
                P.barrier()

        if STAGE >= 2:
            with ExitStack() as es:
                qh_r = K.ring(es, 2, [128, NT], BF16, "qh"); kh_r = K.ring(es, 2, [128, NT], BF16, "kh")
                vh_r = K.ring(es, 2, [128, 33, 128], BF16, "vh"); sgh_r = K.ring(es, 2, [128, NT], BF16, "sgh")
                e_r = K.ring(es, 4, [128, 512], F32, "e"); spb_r = K.ring(es, 3, [128, 512], BF16, "spb")
                R_r = K.ring(es, 4, [128, 512], BF16, "Rr"); C_r = K.ring(es, 3, [128, 512], BF16, "C")
                a_r = K.ring(es, 3, [128, 512], BF16, "a"); og_r = K.ring(es, 2, [128, 512], BF16, "og")
                kcf = K.sb(es, [128, 32, 128], F32, "kcf"); vcf = K.sb(es, [128, 32, 128], F32, "vcf")
                kcb = K.sb(es, [128, 32, 128], BF16, "kcb"); vcb = K.sb(es, [128, 32, 128], BF16, "vcb")
                kcT = K.sb(es, [128, T], BF16, "kcT")
                z_r = K.ring(es, 3, [128, 512], F32, "z", psum=True); cum_r = K.ring(es, 2, [128, 512], F32, "cum", psum=True)
                o_r = K.ring(es, 2, [128, 512], F32, "o", psum=True); tpB_r = K.ring(es, 1, [128, 8, 128], BF16, "tpB", psum=True)
                ctr = {"blk": 0, "qb": 0, "tp": 0}

                def head_loads(h):
                    qh = qh_r[h % 2]; kh = kh_r[h % 2]; vh = vh_r[h % 2]; sgh = sgh_r[h % 2]
                    K.dma("sp", qh.t[:], qT_s[h], [], [qh.r], "qh%d" % (h % 2))
                    K.dma("sp", kh.t[:], kT_s[h], [], [kh.r], "kh%d" % (h % 2))
                    K.dma("sp", vh.t[:, 0:32, :], v_s[0:T, h * 128:(h + 1) * 128].rearrange("(t p) d -> p t d", p=128),
                          [], [vh.r], "vh%d" % (h % 2))
                    K.dma("sp", vh.t[0:TS, 32, :], v_s[T:NT, h * 128:(h + 1) * 128], [], [vh.r], "vh%d" % (h % 2))
                    K.dma("sp", sgh.t[:], sgT_s[h], [], [sgh.r], "sgh%d" % (h % 2))

                def cache_prep(h):
                    K.dma("sp", kcf.t[:], ck[:, h, :].rearrange("(t p) d -> p t d", p=128), [], [kcf.r], "kcf")
                    K.dma("sp", vcf.t[:], cv[:, h, :].rearrange("(t p) d -> p t d", p=128), [], [vcf.r], "vcf")
                    K.cp("pool", kcb.t[:], kcf.t[:], [kcf.r], [kcb.r])
                    K.cp("pool", vcb.t[:], vcf.t[:], [vcf.r], [vcb.r])
                    for gq in range(8):
                        tp = tpB_r[0]
                        for i4 in range(4):
                            K.tr(tp.t[:, i4, :], kcb.t[:, gq * 4 + i4, :], identb, [kcb.r, cb.r], [tp.r])
                        K.cp("dve", kcT.t[:, gq * 512:(gq + 1) * 512], tp.t[:, 0:4, :], [tp.r], [kcT.r])

                items = []
                for h in range(H):
                    qh = qh_r[h % 2]; kh = kh_r[h % 2]; vh = vh_r[h % 2]; sgh = sgh_r[h % 2]
                    qbs = []
                    for i in range(NBLK):
                        q0 = i * 512
                        kl = []
                        for kb in range(4 * i + 3, -1, -1):
                            m = kb - 4 * i
                            off = 128 * m if m > 0 else 0
                            kl.append((kh.t[:, kb * 128:(kb + 1) * 128], vh.t[:, kb, :], [kh.r, vh.r], 128, off, m >= 0))
                        qbs.append((q0, 512, kl, ogT_s[h, :, q0:q0 + 512]))
                    kl = [(kh.t[:, T:NT], vh.t[0:TS, 32, :], [kh.r, vh.r], TS, 0, True)]
                    for kb in range(31, -1, -1):
                        kl.append((kcT.t[:, kb * 128:(kb + 1) * 128], vcb.t[:, kb, :], [kcT.r, vcb.r], 128, 0, False))
                    qbs.append((T, TS, kl, ogT_s[h, :, T:NT]))
                    for qi, (q0, nq, kl, out_dram) in enumerate(qbs):
                        qb = {"q0": q0, "nq": nq, "out": out_dram, "qh": qh, "sgh": sgh, "n": len(kl), "slot": None}
                        for idx, (kT_ap, v_ap, rds, nk, off, diag) in enumerate(kl):
                            items.append({"qb": qb, "idx": idx, "kT": kT_ap, "v": v_ap, "rds": rds, "nk": nk, "off": off, "diag": diag,
                                          "hstart": h if (qi == 0 and idx == 0) else None})

                def s0(it):
                    qb = it["qb"]; nq = qb["nq"]; q0 = qb["q0"]; off = it["off"]; nk = it["nk"]; n = nq - off
                    if it["idx"] == 0:
                        s = ctr["qb"] % 2; ctr["qb"] += 1
                        qb["R"] = [R_r[2 * s], R_r[2 * s + 1]]; qb["o"] = o_r[s]; qb["og"] = og_r[s]; qb["s"] = s
                        K.memset("pool", qb["R"][0].t[:, 0:nq], 0.0, [qb["R"][0].r])
                        K.memset("pool", qb["R"][1].t[:, 0:nq], 0.0, [qb["R"][1].r])
                    j = ctr["blk"]; ctr["blk"] += 1
                    it["z"] = z_r[j % 3]; it["e"] = e_r[j % 4]; it["sp"] = spb_r[j % 3]; it["cum"] = cum_r[j % 2]
                    it["a"] = a_r[j % 3]
                    zb = it["z"]
                    K.mm(zb.t[0:nk, 0:n], it["kT"], qb["qh"].t[:, q0 + off:q0 + nq], True, True, it["rds"] + [qb["qh"].r], [zb.r])

                def s1(it):
                    qb = it["qb"]; nq = qb["nq"]; off = it["off"]; nk = it["nk"]; n = nq - off
                    zb = it["z"]; eb = it["e"]
                    K.act(eb.t[0:nk, 0:n], zb.t[0:nk, 0:n], AF.Exp, [zb.r], [eb.r])
                    if it["diag"]:
                        w = min(128, n)
                        K.tt("dve", eb.t[0:nk, 0:w], eb.t[0:nk, 0:w], maskST[0:nk, 0:w], ALU.mult, [eb.r, cf.r], [eb.r])

                def s2(it):
                    qb = it["qb"]; nq = qb["nq"]; off = it["off"]; nk = it["nk"]; n = nq - off
                    K.act(it["sp"].t[0:nk, 0:n], it["e"].t[0:nk, 0:n], AF.Ln, [it["e"].r], [it["sp"].r], bias=1.0)

                def s3(it):
                    qb = it["qb"]; nq = qb["nq"]; q0 = qb["q0"]; off = it["off"]; nk = it["nk"]; n = nq - off
                    sb_ = it["sp"]; cb_ = it["cum"]; Rb = qb["R"][it["idx"] % 2]; Rn = qb["R"][(it["idx"] + 1) % 2]
                    K.mm(cb_.t[0:nk, 0:n], lmat_n[0:nk, 0:nk], sb_.t[0:nk, 0:n], True, False, [cb.r, sb_.r], [cb_.r])
                    K.mm(cb_.t[0:nk, 0:n], ones_n[:, 0:nk], Rb.t[:, off:nq], False, False, [cb.r, Rb.r], [cb_.r])
                    K.mm(cb_.t[0:nk, 0:n], it["kT"], qb["qh"].t[:, q0 + off:q0 + nq], False, True, it["rds"] + [qb["qh"].r], [cb_.r])
                    if it["idx"] < qb["n"] - 1:
                        if nk < 128:
                            K.tt("dve", Rn.t[0:nk, off:nq], Rb.t[0:nk, off:nq], sb_.t[0:nk, 0:n], ALU.add, [Rb.r, sb_.r], [Rn.r])
                        else:
                            K.tt("dve", Rn.t[:, off:nq], Rb.t[:, off:nq], sb_.t[:, 0:n], ALU.add, [Rb.r, sb_.r], [Rn.r])

                def s4(it):
                    qb = it["qb"]; nq = qb["nq"]; off = it["off"]; nk = it["nk"]; n = nq - off
                    ab = it["a"]
                    if it["idx"] == 0 and off > 0:
                        K.memset("pool", ab.t[0:nk, 0:off], 0.0, [ab.r])
                        K.act(ab.t[0:nk, off:nq], it["cum"].t[0:nk, 0:n], AF.Exp, [it["cum"].r], [ab.r])
                    else:
                        K.act(ab.t[0:nk, 0:n], it["cum"].t[0:nk, 0:n], AF.Exp, [it["cum"].r], [ab.r])

                def s5(it):
                    qb = it["qb"]; nq = qb["nq"]; off = it["off"]; nk = it["nk"]; n = nq - off
                    ab = it["a"]
                    if it["diag"]:
                        w = min(128, n)
                        c0 = off if (it["idx"] == 0 and off > 0) else 0
                        K.tt("dve", ab.t[0:nk, c0:c0 + w], ab.t[0:nk, c0:c0 + w], maskST[0:nk, 0:w], ALU.mult, [ab.r, cf.r], [ab.r])

                def s6(it):
                    qb = it["qb"]; nq = qb["nq"]; off = it["off"]; nk = it["nk"]; n = nq - off
                    ab = it["a"]; ob = qb["o"]; idx = it["idx"]; nblk = qb["n"]
                    if idx == 0 and off > 0:
                        K.mm(ob.t[:, 0:nq], it["v"], ab.t[0:nk, 0:nq], True, nblk == 1, it["rds"] + [ab.r], [ob.r])
                    else:
                        K.mm(ob.t[:, off:nq], it["v"], ab.t[0:nk, 0:n], idx == 0, idx == nblk - 1, it["rds"] + [ab.r], [ob.r])

                def s7(it):
                    qb = it["qb"]; nq = qb["nq"]; q0 = qb["q0"]
                    if it["idx"] == qb["n"] - 1:
                        ogb = qb["og"]; ob = qb["o"]
                        K.tt("dve", ogb.t[:, 0:nq], ob.t[:, 0:nq], qb["sgh"].t[:, q0:q0 + nq], ALU.mult, [ob.r, qb["sgh"].r], [ogb.r])
                        K.dma("sp", qb["out"], ogb.t[:, 0:nq], [ogb.r], [], "og%d" % qb["s"])

                stages = [s0, s1, s2, s3, s4, s5, s6, s7]
                NS = len(stages)
                head_loads(0)
                nit = len(items)
                for i in range(nit + NS):
                    k = i - NS
                    if 0 <= k < nit and items[k]["hstart"] is not None:
                        hh_ = items[k]["hstart"]
                        if hh_ + 1 < H:
                            head_loads(hh_ + 1)
                        cache_prep(hh_)
                    for si, fn in enumerate(stages):
                        if 0 <= i - si < nit:
                            fn(items[i - si])
                P.barrier()

        if STAGE >= 3:
            with ExitStack() as es:
                wo = K.sb(es, [128, 16, D], BF16, "wo")
                ogb_r = K.ring(es, 2, [128, 16, 512], BF16, "ogblk")
                xt_r = K.ring(es, 2, [128, D], F32, "xtC"); hb_r = K.ring(es, 2, [128, D], BF16, "hbC")
                ss_r = K.ring(es, 2, [128, 4], F32, "ssC"); tmp_r = K.ring(es, 2, [128, 512], F32, "tmpC")
                h1st_r = K.ring(es, 2, [128, 16, 128], BF16, "h1st")
                y_r = K.ring(es, 4, [128, 512], F32, "yC", psum=True)
                tp_r = K.ring(es, 2, [128, 8, 128], BF16, "tpC", psum=True)
                tpi = [0]; ci = {"x": 0, "y": 0, "t": 0}
                K.dma("sp", wo.t[:], wo0_s.rearrange("(kc p) n -> p kc n", p=128), R_wo0, [wo.r], "woC")
                ctiles = [(tok0 + t * 128, min(128, ntokb - t * 128)) for (tok0, ntokb, g) in blocks for t in range((ntokb + 127) // 128)]

                def c_xload(k):
                    if k < len(ctiles):
                        K.dma("sp", xt_r[k % 2].t[0:ctiles[k][1], :], xrows(ctiles[k][0], ctiles[k][1]), [], [xt_r[k % 2].r], "xtC%d" % (k % 2))

                def c_ogload(bi):
                    if bi < len(blocks):
                        tok0_, ntokb_, g_ = blocks[bi]
                        K.dma("sp", ogb_r[bi % 2].t[:, :, 0:ntokb_], ogT_s[:, :, tok0_:tok0_ + ntokb_].rearrange("h d t -> d h t"), [],
                              [ogb_r[bi % 2].r], "ogblk%d" % (bi % 2))

                c_ogload(0); c_xload(0)
                pend = [None]
                for bi, (tok0, ntokb, g) in enumerate(blocks):
                    ogb = ogb_r[bi % 2]
                    c_ogload(bi + 1)
                    ntl = (ntokb + 127) // 128
                    for t in range(ntl):
                        ntok = min(128, ntokb - t * 128)
                        s = ci["x"] % 2; ci["x"] += 1
                        xt = xt_r[s]; hb = hb_r[s]; ssb = ss_r[s]; h1st = h1st_r[s]
                        c_xload(ci["x"])
                        for n in range(4):
                            y = y_r[ci["y"] % 4]; ci["y"] += 1
                            tmp = tmp_r[ci["t"] % 2]; ci["t"] += 1
                            for kc in range(16):
                                K.mm(y.t[0:ntok, :], ogb.t[:, kc, t * 128:t * 128 + ntok], wo.t[:, kc, n * 512:(n + 1) * 512], kc == 0, kc == 15,
                                     [ogb.r, wo.r], [y.r])
                            K.tt("dve", tmp.t[0:ntok, :], y.t[0:ntok, :], gate_b[0][g].t[0:ntok, n * 512:(n + 1) * 512], ALU.mult,
                                 [y.r, gate_b[0][g].r], [tmp.r])
                            K.tt("pool", xt.t[0:ntok, n * 512:(n + 1) * 512], tmp.t[0:ntok, :], xt.t[0:ntok, n * 512:(n + 1) * 512], ALU.add,
                                 [tmp.r, xt.r], [xt.r])
                        K.dma("sp", x1_s[tok0 + t * 128:tok0 + t * 128 + ntok, :], xt.t[0:ntok, :], [xt.r], [], "x1o%d" % s)
                        norm_p1(xt.t[0:ntok, :], xt.r, ntok, hb, ssb)
                        if pend[0] is not None:
                            pend[0]()
                        def _p2(ntok=ntok, g=g, hb=hb, h1st=h1st, t=t, tok0=tok0, s=s):
                            norm_p2(ntok, 1, g, hb, tp_r, tpi, (lambda kc: h1st.t[:, kc, 0:ntok]), h1st.r, "dve" if t % 2 == 0 else "act")
                            K.dma("sp", h1T_s[:, :, tok0 + t * 128:tok0 + t * 128 + ntok].rearrange("c p t -> p c t"), h1st.t[:, :, 0:ntok],
                                  [h1st.r], [], "h1o%d" % s)
                        pend[0] = _p2
                pend[0]()
                P.barrier()

        esL0.close()
        if STAGE >= 4:
            with ExitStack() as es:
                h1b_r = K.ring(es, 2, [128, 16, 512], BF16, "h1b")
                wa2_b = K.sb(es, [16, 1024], BF16, "wa2b"); ba_b = K.sb(es, [1, 1024], BF16, "bab")
                K.dma("pool", wa2_b.t[:], wa2[:, :], [], [wa2_b.r], "wa2")
                K.dma("pool", ba_b.t[:], ba[:, :], [], [ba_b.r], "bab")
                wt_r = K.ring(es, 3, [128, 16, 512], BF16, "wtD")
                wta = K.sb(es, [128, 16, 16], BF16, "wta")
                alrT = K.sb(es, [16, 512], BF16, "alrT")
                eg_r = K.ring(es, 2, [128, 1024], F32, "eg")
                EbT = K.sb(es, [128, 8, 512], F32, "EbT"); EnbT = K.sb(es, [128, 8, 512], F32, "EnbT")
                st_r = K.ring(es, 2, [128, 512], BF16, "stD"); stmp_r = K.ring(es, 2, [128, 512], F32, "stmpD")
                kdst_r = K.ring(es, 2, [128, 4, 1024], BF16, "kdst")
                ps_r = K.ring(es, 4, [128, 512], F32, "psD", psum=True)
                bT_r = K.ring(es, 2, [128, 4, 128], F32, "bT", psum=True)
                tp_r = K.ring(es, 2, [128, 8, 128], BF16, "tpD", psum=True)
                ci = {"wt": 0, "ps": 0, "st": 0, "tp": 0, "eg": 0, "bT": 0}
                K.dma("sp", wta.t[:], wta_s[:, :, :], [R_wg[12]], [wta.r], "wta")
                tile_idx = 0
                def d_hload(bi):
                    if bi < len(blocks):
                        tok0_, ntokb_, g_ = blocks[bi]
                        K.dma("sp", h1b_r[bi % 2].t[:, :, 0:ntokb_], h1T_s[:, :, tok0_:tok0_ + ntokb_].rearrange("c p t -> p c t"), [],
                              [h1b_r[bi % 2].r], "h1b%d" % (bi % 2))

                def d_wload(si):
                    if si < 12 * len(blocks):
                        wc_ = si % 12
                        K.dma("sp", wt_r[si % 3].t[:], wg_s[wc_], [R_wg[wc_]], [wt_r[si % 3].r], "wtD%d" % (si % 3))

                d_hload(0); d_wload(0); d_wload(1)
                for bi, (tok0, ntokb, g) in enumerate(blocks):
                    h1b = h1b_r[bi % 2]
                    d_hload(bi + 1)
                    ntl = (ntokb + 127) // 128
                    ps = ps_r[ci["ps"] % 4]; ci["ps"] += 1
                    for kc in range(16):
                        K.mm(ps.t[0:16, 0:ntokb], wta.t[:, kc, :], h1b.t[:, kc, 0:ntokb], kc == 0, kc == 15, [wta.r, h1b.r], [ps.r])
                    K.cp("act", alrT.t[:, 0:ntokb], ps.t[0:16, 0:ntokb], [ps.r], [alrT.r])
                    for t in range(ntl):
                        ntok = min(128, ntokb - t * 128)
                        eg = eg_r[ci["eg"] % 2]; ci["eg"] += 1
                        for n in range(2):
                            ps = ps_r[ci["ps"] % 4]; ci["ps"] += 1
                            K.mm(ps.t[0:ntok, :], alrT.t[0:16, t * 128:t * 128 + ntok], wa2_b.t[0:16, n * 512:(n + 1) * 512], True, False,
                                 [alrT.r, wa2_b.r], [ps.r])
                            K.mm(ps.t[0:ntok, :], ones_b[0:1, 0:ntok], ba_b.t[0:1, n * 512:(n + 1) * 512], False, True, [cb.r, ba_b.r], [ps.r])
                            K.act(eg.t[0:ntok, n * 512:(n + 1) * 512], ps.t[0:ntok, :], AF.Exp, [ps.r], [eg.r], scale=-1.0)
                        K.act(eg.t[0:ntok, :], eg.t[0:ntok, :], AF.Ln, [eg.r], [eg.r], bias=1.0)
                        for half in range(2):
                            bT = bT_r[ci["bT"] % 2]; ci["bT"] += 1
                            for i4 in range(4):
                                dc = half * 4 + i4
                                K.mm(bT.t[:, i4, 0:ntok], eg.t[0:ntok, dc * 128:(dc + 1) * 128], uincl[0:ntok, 0:ntok], True, True,
                                     [eg.r, cf.r], [bT.r])
                            K.act(EbT.t[:, half * 4:half * 4 + 4, t * 128:t * 128 + ntok], bT.t[:, :, 0:ntok], AF.Exp, [bT.r], [EbT.r])
                            K.act(EnbT.t[:, half * 4:half * 4 + 4, t * 128:t * 128 + ntok], bT.t[:, :, 0:ntok], AF.Exp, [bT.r], [EnbT.r], scale=-1.0)
                        K.cp("dve", dec_all.t[:, tile_idx, :], EbT.t[:, :, t * 128 + ntok - 1], [EbT.r], [dec_all.r])
                        tile_idx += 1
                    kdst = kdst_r[bi % 2]
                    for wc in range(12):
                        si = bi * 12 + wc
                        wt = wt_r[si % 3]
                        d_wload(si + 2)
                        if wc < 4:
                            for i4 in range(4):
                                dc = (wc % 2) * 4 + i4
                                ps = ps_r[ci["ps"] % 4]; ci["ps"] += 1
                                for kc in range(16):
                                    K.mm(ps.t[:, 0:ntokb], wt.t[:, kc, i4 * 128:(i4 + 1) * 128], h1b.t[:, kc, 0:ntokb],
                                         kc == 0, kc == 15, [wt.r, h1b.r], [ps.r])
                                st = st_r[ci["st"] % 2]; ci["st"] += 1
                                if wc < 2:
                                    K.stt(st.t[:, 0:ntokb], ps.t[:, 0:ntokb], 256 ** -0.5, EbT.t[:, dc, 0:ntokb], ALU.mult, ALU.mult,
                                          [ps.r, EbT.r], [st.r])
                                    K.dma("sp", qinT_s[dc, :, tok0:tok0 + ntokb], st.t[:, 0:ntokb], [st.r], [], "stD%d" % (ci["st"] % 2))
                                else:
                                    K.tt("dve", st.t[:, 0:ntokb], ps.t[:, 0:ntokb], EnbT.t[:, dc, 0:ntokb], ALU.mult, [ps.r, EnbT.r], [st.r])
                                    K.dma("sp", kdT_s[dc, :, tok0:tok0 + ntokb], st.t[:, 0:ntokb], [st.r], [], "stD%d" % (ci["st"] % 2))
                                    for t in range(ntl):
                                        ntok = min(128, ntokb - t * 128)
                                        tp = tp_r[ci["tp"] % 2]; ci["tp"] += 1
                                        K.tr(tp.t[0:ntok, 0, :], st.t[:, t * 128:t * 128 + ntok], identb, [st.r, cb.r], [tp.r])
                                        K.cp("act", kdst.t[0:ntok, t, dc * 128:(dc + 1) * 128], tp.t[0:ntok, 0, :], [tp.r], [kdst.r])
                            if wc == 3:
                                for t in range(ntl):
                                    ntok = min(128, ntokb - t * 128)
                                    K.dma("sp", kd_s[tok0 + t * 128:tok0 + t * 128 + ntok, :], kdst.t[0:ntok, t, :], [kdst.r], [],
                                          "kdst%d" % (bi % 2))
                        else:
                            isv = wc < 8
                            col0 = ((wc - 4) % 4) * 512
                            for t in range(ntl):
                                ntok = min(128, ntokb - t * 128)
                                ps = ps_r[ci["ps"] % 4]; ci["ps"] += 1
                                for kc in range(16):
                                    K.mm(ps.t[0:ntok, :], h1b.t[:, kc, t * 128:t * 128 + ntok], wt.t[:, kc, :], kc == 0, kc == 15, [wt.r, h1b.r], [ps.r])
                                st = st_r[ci["st"] % 2]; stmp = stmp_r[ci["st"] % 2]; ci["st"] += 1
                                if isv:
                                    K.cp("act", st.t[0:ntok, :], ps.t[0:ntok, :], [ps.r], [st.r])
                                    K.dma("sp", vg_s[tok0 + t * 128:tok0 + t * 128 + ntok, col0:col0 + 512], st.t[0:ntok, :], [st.r], [],
                                          "stD%d" % (ci["st"] % 2))
                                else:
                                    silu_from_psum(ps.t[0:ntok, :], ps.r, ntok, 512, st.t[0:ntok, :], st.r, stmp)
                                    K.dma("sp", sr_s[tok0 + t * 128:tok0 + t * 128 + ntok, col0:col0 + 512], st.t[0:ntok, :], [st.r], [],
                                          "stD%d" % (ci["st"] % 2))
                P.barrier()

            with ExitStack() as es:
                wo = K.sb(es, [128, 16, D], BF16, "wo1")
                S = K.sb(es, [128, 8, 512], F32, "S"); Sb = K.sb(es, [128, 8, 512], BF16, "Sb")
                qin_r = K.ring(es, 2, [128, 8, 128], BF16, "qin"); kdT_r = K.ring(es, 2, [128, 8, 128], BF16, "kdT")
                kd_r = K.ring(es, 2, [128, 1024], BF16, "kd"); vg_r = K.ring(es, 2, [128, D], BF16, "vg")
                sr_r = K.ring(es, 1, [128, D], BF16, "sr"); x1_r = K.ring(es, 3, [128, D], F32, "x1")
                gsr_r = K.ring(es, 1, [128, D], F32, "gsr"); aT_r = K.ring(es, 4, [128, 128], BF16, "aT")
                osb_r = K.ring(es, 2, [128, 512], F32, "osb"); ss_r = K.ring(es, 8, [128, 4], F32, "ssD")
                og1_r = K.ring(es, 2, [128, D], BF16, "og1"); og1T_r = K.ring(es, 1, [128, 16, 128], BF16, "og1T")
                tmp_r = K.ring(es, 2, [128, 512], F32, "tmpE")
                aps_r = K.ring(es, 1, [128, 512], F32, "aps", psum=True)
                ops_r = K.ring(es, 2, [128, 512], F32, "ops", psum=True)
                sps_r = K.ring(es, 2, [128, 512], F32, "sps", psum=True)
                tp_r = K.ring(es, 1, [128, 8, 128], BF16, "tpE", psum=True)
                y_r = K.ring(es, 2, [128, 512], F32, "yE", psum=True)
                ci = {"ops": 0, "sps": 0, "ss": 0, "y": 0, "t": 0, "osb": 0, "aT": 0}
                S_res = [Res("S%d" % i) for i in range(8)]; Sb_res = [Res("Sb%d" % i) for i in range(8)]
                K.dma("sp", wo.t[:], wo1_s.rearrange("(kc p) n -> p kc n", p=128), R_wo1, [wo.r], "woE")
                K.memset("pool", S.t[:], 0.0, S_res)
                K.memset("pool", Sb.t[:], 0.0, Sb_res)
                tiles = [(t * 128, 128, 0) for t in range(NBLK * 4)] + [(T, TS, 1)]

                def e_loads(k):
                    if k >= len(tiles):
                        return
                    tok0_, ntok_, g_ = tiles[k]
                    s_ = k % 2
                    K.dma("sp", qin_r[s_].t[:, :, 0:ntok_], qinT_s[:, :, tok0_:tok0_ + ntok_].rearrange("c p t -> p c t"), [], [qin_r[s_].r], "qin%d" % s_)
                    K.dma("sp", kdT_r[s_].t[:, :, 0:ntok_], kdT_s[:, :, tok0_:tok0_ + ntok_].rearrange("c p t -> p c t"), [], [kdT_r[s_].r], "kdT%d" % s_)
                    K.dma("sp", kd_r[s_].t[0:ntok_, :], kd_s[tok0_:tok0_ + ntok_, :], [], [kd_r[s_].r], "kd%d" % s_)
                    K.dma("sp", vg_r[s_].t[0:ntok_, :], vg_s[tok0_:tok0_ + ntok_, :], [], [vg_r[s_].r], "vg%d" % s_)
                    K.dma("sp", x1_r[k % 3].t[0:ntok_, :], x1_s[tok0_:tok0_ + ntok_, :], [], [x1_r[k % 3].r], "x1i%d" % (k % 3))

                def e_srload(k):
                    if k < len(tiles):
                        tok0_, ntok_, g_ = tiles[k]
                        K.dma("sp", sr_r[0].t[0:ntok_, :], sr_s[tok0_:tok0_ + ntok_, :], [], [sr_r[0].r], "sr0")

                def e_xyz(ti):
                    tok0, ntok, g = tiles[ti]
                    s = ti % 2
                    qin = qin_r[s]; kdT = kdT_r[s]; kd = kd_r[s]; vg = vg_r[s]; sr = sr_r[0]; gsr = gsr_r[0]; og1 = og1_r[s]
                    K.tt("pool", gsr.t[0:ntok, :], sr.t[0:ntok, :], glag_b.t[0:ntok, :], ALU.mult, [sr.r, glag_b.r], [gsr.r])
                    e_srload(ti + 1)
                    aTs = []
                    for h in range(4):
                        aps = aps_r[0]; aT = aT_r[h]
                        for dc in range(2):
                            K.mm(aps.t[0:ntok, 0:ntok], kdT.t[:, h * 2 + dc, 0:ntok], qin.t[:, h * 2 + dc, 0:ntok], dc == 0, dc == 1,
                                 [kdT.r, qin.r], [aps.r])
                        K.tt("dve", aT.t[0:ntok, 0:ntok], aps.t[0:ntok, 0:ntok], maskLE[0:ntok, 0:ntok], ALU.mult, [aps.r, cf.r], [aT.r])
                        aTs.append(aT)
                    for h in range(4):
                        aT = aTs[h]
                        ops = ops_r[ci["ops"] % 2]; ci["ops"] += 1
                        K.mm(ops.t[0:ntok, :], aT.t[0:ntok, 0:ntok], vg.t[0:ntok, h * 512:(h + 1) * 512], True, False, [aT.r, vg.r], [ops.r])
                        for dc in range(2):
                            K.mm(ops.t[0:ntok, :], qin.t[:, h * 2 + dc, 0:ntok], Sb.t[:, h * 2 + dc, :], False, dc == 1,
                                 [qin.r, Sb_res[h * 2 + dc]], [ops.r])
                        osb = osb_r[ci["osb"] % 2]; ci["osb"] += 1
                        ssb = ss_r[ci["ss"] % 8]; ci["ss"] += 1
                        K.cp("act", osb.t[0:ntok, :], ops.t[0:ntok, :], [ops.r], [osb.r])
                        K.ttr(junk.t[0:ntok, 0:512], osb.t[0:ntok, :], osb.t[0:ntok, :], ssb.t[0:ntok, 0:1], [osb.r], [ssb.r])
                        K.act(ssb.t[0:ntok, 1:2], ssb.t[0:ntok, 0:1], AF.Ln, [ssb.r], [ssb.r], bias=EPS, scale=1.0 / 512)
                        K.act(ssb.t[0:ntok, 2:3], ssb.t[0:ntok, 1:2], AF.Exp, [ssb.r], [ssb.r], scale=-0.5)
                        K.stt(og1.t[0:ntok, h * 512:(h + 1) * 512], osb.t[0:ntok, :], ssb.t[0:ntok, 2:3], gsr.t[0:ntok, h * 512:(h + 1) * 512],
                              ALU.mult, ALU.mult, [osb.r, ssb.r, gsr.r], [og1.r])
                    for h in range(4):
                        for dc in range(2):
                            hd = h * 2 + dc
                            sps = sps_r[ci["sps"] % 2]; ci["sps"] += 1
                            K.mm(sps.t[:, :], kd.t[0:ntok, hd * 128:(hd + 1) * 128], vg.t[0:ntok, h * 512:(h + 1) * 512], True, True,
                                 [kd.r, vg.r], [sps.r])
                            K.tt("dve", S.t[:, hd, :], sps.t[:, :], S.t[:, hd, :], ALU.add, [sps.r, S_res[hd]], [S_res[hd]])
                            K.act(S.t[:, hd, :], S.t[:, hd, :], AF.Copy, [S_res[hd], dec_all.r], [S_res[hd]], scale=dec_all.t[:, ti, hd:hd + 1])
                            K.cp("pool", Sb.t[:, hd, :], S.t[:, hd, :], [S_res[hd]], [Sb_res[hd]])

                def e_w(ti):
                    tok0, ntok, g = tiles[ti]
                    og1 = og1_r[ti % 2]; og1T = og1T_r[0]; x1 = x1_r[ti % 3]
                    for grp in range(4):
                        tp = tp_r[0]
                        for i4 in range(4):
                            kc = grp * 4 + i4
                            K.tr(tp.t[:, i4, 0:ntok], og1.t[0:ntok, kc * 128:(kc + 1) * 128], identb[0:ntok, 0:ntok], [og1.r, cb.r], [tp.r])
                        K.cp("act" if grp % 2 else "dve", og1T.t[:, grp * 4:grp * 4 + 4, 0:ntok], tp.t[:, 0:4, 0:ntok], [tp.r], [og1T.r])
                    for n in range(4):
                        y = y_r[ci["y"] % 2]; ci["y"] += 1
                        tmp = tmp_r[ci["t"] % 2]; ci["t"] += 1
                        for kc in range(16):
                            K.mm(y.t[0:ntok, :], og1T.t[:, kc, 0:ntok], wo.t[:, kc, n * 512:(n + 1) * 512], kc == 0, kc == 15, [og1T.r, wo.r], [y.r])
                        K.tt("dve", tmp.t[0:ntok, :], y.t[0:ntok, :], gate_b[1][g].t[0:ntok, n * 512:(n + 1) * 512], ALU.mult,
                             [y.r, gate_b[1][g].r], [tmp.r])
                        K.tt("pool", x1.t[0:ntok, n * 512:(n + 1) * 512], tmp.t[0:ntok, :], x1.t[0:ntok, n * 512:(n + 1) * 512], ALU.add,
                             [tmp.r, x1.r], [x1.r])
                    ssb = ss_r[ci["ss"] % 8]; ci["ss"] += 1
                    K.ttr(junk.t[0:ntok, :], x1.t[0:ntok, :], x1.t[0:ntok, :], ssb.t[0:ntok, 0:1], [x1.r], [ssb.r])
                    K.act(ssb.t[0:ntok, 1:2], ssb.t[0:ntok, 0:1], AF.Ln, [ssb.r], [ssb.r], bias=EPS, scale=1.0 / D)
                    K.act(ssb.t[0:ntok, 2:3], ssb.t[0:ntok, 1:2], AF.Exp, [ssb.r], [ssb.r], scale=-0.5)
                    K.stt(x1.t[0:ntok, :], x1.t[0:ntok, :], ssb.t[0:ntok, 2:3], fing_b.t[0:ntok, :], ALU.mult, ALU.mult,
                          [x1.r, ssb.r, fing_b.r], [x1.r])
                    dst = yp[tok0:tok0 + ntok, :] if g == 0 else ys[0:ntok, :]
                    K.dma("sp", dst, x1.t[0:ntok, :], [x1.r], [Res()], "yo%d" % (ti % 3))

                e_loads(0); e_srload(0)
                for ti, (tok0, ntok, g) in enumerate(tiles):
                    if g == 1:
                        K.dma("sp", sp_o.rearrange("h (c p) e -> p (h c) e", p=128), S.t[:], S_res, [Res()], "Sout")
                        K.dma("sp", S.t[:], sg_in.rearrange("h (c p) e -> p (h c) e", p=128), [], S_res, "Sin")
                        K.cp("pool", Sb.t[:], S.t[:], S_res, Sb_res)
                    e_loads(ti + 1)
                    e_xyz(ti)
                    if ti >= 1:
                        e_w(ti - 1)
                e_w(len(tiles) - 1)
                K.dma("sp", ss_o.rearrange("h (c p) e -> p (h c) e", p=128), S.t[:], S_res, [Res()], "Sout2")
                P.barrier()

        P.barrier()
        for e in ENG:
            if e == "pe":
                continue
            P.op(e, lambda eng: eng.nop(), (), ())
        with ExitStack() as es2:
            P.emit(nc, es2)
    return nc


_NC = None


def _consts():
    i = np.arange(128)
    cfv = np.zeros((128, NCF), np.float32)
    cfv[:, 0:128] = np.eye(128, dtype=np.float32)
    cfv[:, 128:256] = (i[:, None] < i[None, :]).astype(np.float32)
    cfv[:, 256:384] = (i[:, None] <= i[None, :]).astype(np.float32) * (-1.0 / 16.0)
    cfv[:, 384:512] = (i[:, None] <= i[None, :]).astype(np.float32)
    cfv[:, 512:640] = 1.0
    cbv = np.zeros((128, NCB), np.float32)
    cbv[:, 0:128] = np.eye(128, dtype=np.float32)
    cbv[:, 128:256] = (i[:, None] >= i[None, :]).astype(np.float32)
    cbv[:, 256:384] = 1.0
    cbv[:, 384:512] = -cbv[:, 128:256]
    cbv[:, 512:640] = -1.0
    return cfv, cbv.astype(ml_dtypes.bfloat16)


def kernel(x_prompt, x_sample, cache_sb_k, cache_sb_v, state_gla, c_prompt, c_sample,
           w_ada, b_ada, norm_g, sb_w_in, sb_w_out, gla_w_in, gla_w_a2, gla_b_a,
           gla_norm_g, gla_w_out, final_norm_g):
    global _NC
    if _NC is None:
        _NC = build()
    f = lambda a: np.ascontiguousarray(np.asarray(a), dtype=np.float32)
    cfv, cbv = _consts()
    shared = {
        "wada": f(w_ada), "bada": f(b_ada).reshape(96, 128), "brow": f(b_ada), "ng": f(norm_g).reshape(32, 128),
        "wq": f(sb_w_in)[0], "wo0": f(sb_w_out)[0], "wg": f(gla_w_in)[0], "wa2": f(gla_w_a2)[0], "ba": f(gla_b_a),
        "glag": f(gla_norm_g), "wo1": f(gla_w_out)[0], "fing": f(final_norm_g).reshape(1, D), "cf": cfv, "cb": cbv,
    }
    x_prompt = f(x_prompt); x_sample = f(x_sample); cache_sb_k = f(cache_sb_k); cache_sb_v = f(cache_sb_v)
    state_gla = f(state_gla); c_prompt = f(c_prompt); c_sample = f(c_sample)
    in_maps = []
    for b in range(8):
        m = dict(shared)
        m["xp"] = x_prompt[b]; m["xs"] = x_sample[b]
        m["ck"] = cache_sb_k[0, b]; m["cv"] = cache_sb_v[0, b]; m["sg"] = state_gla[0, b]
        m["c32"] = np.ascontiguousarray(np.stack([c_prompt[b], c_sample[b]]).reshape(32, 128))
        in_maps.append(m)
    res = run_bass_kernel_spmd(_NC, in_maps, core_ids=list(range(8)))
    r = res.results
    y_p = np.stack([r[b]["yp"] for b in range(8)])
    y_s = np.stack([r[b]["ys"] for b in range(8)])
    k_p = np.stack([r[b]["kp"].reshape(T, H, DH) for b in range(8)])[None]
    v_p = np.stack([r[b]["vp"].reshape(T, H, DH) for b in range(8)])[None]
    k_s = np.stack([r[b]["ks"].reshape(TS, H, DH) for b in range(8)])[None]
    v_s = np.stack([r[b]["vs"].reshape(TS, H, DH) for b in range(8)])[None]
    s_p = np.stack([r[b]["spo"] for b in range(8)])[None]
    s_s = np.stack([r[b]["sso"] for b in range(8)])[None]
    return (y_p, y_s, k_p, v_p, k_s, v_s, s_p, s_s)
```
